# Optimizing a Trainium2 kernel written in Bass

```python
import math
import jax, jax.numpy as jnp
from jax import lax
import numpy as np

D_MODEL = 1024
BATCH = 2
SEQ = 16384
DEPTH = 1
DEC_BATCH = 16
DEC_SEQ = 16
PAST_LEN = 1024

CHUNK = 64
Q_BLOCK = 128
RMS_EPS = 1e-6
NEG_INF = -1e30
DELTA_HEADS = D_MODEL // 128
DELTA_DK = 128
DELTA_DV = 128
CONV_W = 4
DELTA_QKV_W = DELTA_HEADS * (2 * DELTA_DK + DELTA_DV)
DIFF_HEADS = D_MODEL // 128
DIFF_QK_DIM = 64
DIFF_V_DIM = 2 * DIFF_QK_DIM
IN_SPLIT_SIZES = (DELTA_QKV_W, DELTA_HEADS * DELTA_DV, DELTA_HEADS, DELTA_HEADS,
                  DIFF_HEADS * 2 * DIFF_QK_DIM, DIFF_HEADS * 2 * DIFF_QK_DIM, DIFF_HEADS * DIFF_V_DIM,
                  D_MODEL, D_MODEL)
IN_W = DELTA_QKV_W + DELTA_HEADS * DELTA_DV + 2 * DELTA_HEADS + DIFF_HEADS * (4 * DIFF_QK_DIM + DIFF_V_DIM) + 2 * D_MODEL
PEER_HEADS = 8
N_KEYS = 128
N_EXPERTS = N_KEYS * N_KEYS
PEER_DQ = 256
PEER_DHALF = PEER_DQ // 2
PEER_TOPK = 16
PEER_BLOCK = 256

kernel_name = 'hybrid_stream_gdn_diffattn_peer'


def rmsnorm(x, g):
    xf = x.astype(jnp.float32)
    y = xf * lax.rsqrt(jnp.mean(xf * xf, axis=-1, keepdims=True) + RMS_EPS) * g.astype(jnp.float32)
    return y.astype(x.dtype)


def l2norm(x):
    xf = x.astype(jnp.float32)
    return xf * lax.rsqrt(jnp.sum(xf * xf, axis=-1, keepdims=True) + RMS_EPS)


def gated_delta_rule(q, k, v, g, beta, s0):
    b, l, h, _ = q.shape
    dv = v.shape[-1]
    c = min(CHUNK, l)
    n = l // c
    f32 = jnp.float32

    def blocks(t):
        t = t.astype(f32).reshape((b, n, c, h) + t.shape[3:])
        return jnp.moveaxis(t, 3, 1)

    q, k, v, g, beta = (blocks(t) for t in (q, k, v, g, beta))
    gcum = jnp.cumsum(g, axis=-1)
    incl = jnp.tril(jnp.ones((c, c), bool))
    strict = jnp.tril(jnp.ones((c, c), bool), -1)
    diff = gcum[..., :, None] - gcum[..., None, :]
    decay = jnp.where(incl, jnp.exp(jnp.where(incl, diff, 0.0)), 0.0)
    kb = k * beta[..., None]
    vb = v * beta[..., None]
    lmat = jnp.where(strict, jnp.einsum('bhnid,bhnjd->bhnij', kb, k) * decay, 0.0)
    eye = jnp.eye(c, dtype=f32)
    tmat = lax.linalg.triangular_solve(lmat + eye, jnp.broadcast_to(eye, lmat.shape),
                                       left_side=True, lower=True, unit_diagonal=True)
    w = jnp.einsum('bhnij,bhnjd->bhnid', tmat, kb * jnp.exp(gcum)[..., None])
    u = jnp.einsum('bhnij,bhnjd->bhnid', tmat, vb)
    intra = jnp.einsum('bhnid,bhnjd->bhnij', q, k) * decay
    qg = q * jnp.exp(gcum)[..., None]
    glast = gcum[..., -1]
    kd = k * jnp.exp(glast[..., None] - gcum)[..., None]

    def step(s, xs):
        w_n, u_n, qg_n, intra_n, kd_n, gl_n = xs
        v_new = u_n - jnp.einsum('bhcd,bhde->bhce', w_n, s)
        o_n = jnp.einsum('bhcd,bhde->bhce', qg_n, s) + jnp.einsum('bhij,bhje->bhie', intra_n, v_new)
        s = s * jnp.exp(gl_n)[..., None, None] + jnp.einsum('bhcd,bhce->bhde', kd_n, v_new)
        return s, o_n

    xs = tuple(jnp.moveaxis(t, 2, 0) for t in (w, u, qg, intra, kd, glast))
    s_fin, o = lax.scan(step, s0.astype(f32), xs)
    o = jnp.transpose(o, (1, 0, 3, 2, 4)).reshape(b, l, h, dv)
    return o, s_fin


def diff_softmax_mix(q, k, v, lam, mask):
    s = jnp.einsum('bqhmd,bkhmd->bhmqk', q, k).astype(jnp.float32) * (DIFF_QK_DIM ** -0.5)
    if mask is not None:
        s = jnp.where(mask, s, NEG_INF)
    p = jax.nn.softmax(s, axis=-1)
    wgt = p[:, :, 0] - lam * p[:, :, 1]
    return jnp.einsum('bhqk,bkhd->bqhd', wgt.astype(v.dtype), v)


def diff_attn_prompt(q, k, v, lam):
    b, l = q.shape[:2]
    nb = l // Q_BLOCK
    qb = jnp.moveaxis(q.reshape((b, nb, Q_BLOCK) + q.shape[2:]), 1, 0)
    key_chunk = jnp.arange(l) // CHUNK

    def one(args):
        i, qi = args
        q_chunk = (i * Q_BLOCK + jnp.arange(Q_BLOCK)) // CHUNK
        mask = key_chunk[None, :] <= q_chunk[:, None]
        return diff_softmax_mix(qi, k, v, lam, mask)

    out = lax.map(one, (jnp.arange(nb), qb))
    return jnp.moveaxis(out, 0, 1).reshape(b, l, DIFF_HEADS, DIFF_V_DIM)


def peer(xn, w_q, sub_keys, u_tab, v_tab):
    b, l, d = xn.shape
    t = b * l
    nblk = -(-t // PEER_BLOCK)
    xt = jnp.pad(xn.reshape(t, d), ((0, nblk * PEER_BLOCK - t), (0, 0))).reshape(nblk, PEER_BLOCK, d)

    def one(xb):
        qh = (xb @ w_q).reshape(PEER_BLOCK, PEER_HEADS, 2, PEER_DHALF)
        s = jnp.einsum('thpd,hpnd->thpn', qh, sub_keys).astype(jnp.float32)
        sv, si = lax.top_k(s, PEER_TOPK)
        cand = (sv[:, :, 0, :, None] + sv[:, :, 1, None, :]).reshape(PEER_BLOCK, PEER_HEADS, PEER_TOPK * PEER_TOPK)
        cid = (si[:, :, 0, :, None] * N_KEYS + si[:, :, 1, None, :]).reshape(PEER_BLOCK, PEER_HEADS, PEER_TOPK * PEER_TOPK)
        top_v, top_pos = lax.top_k(cand, PEER_TOPK)
        eid = jnp.take_along_axis(cid, top_pos, axis=-1)
        gate = jax.nn.softmax(top_v, axis=-1)
        ue = u_tab[eid]
        ve = v_tab[eid]
        act = jax.nn.gelu(jnp.einsum('td,thkd->thk', xb, ue).astype(jnp.float32), approximate=False)
        return jnp.einsum('thk,thkd->td', (gate * act).astype(ve.dtype), ve)

    out = lax.map(one, xt).reshape(nblk * PEER_BLOCK, d)[:t]
    return out.reshape(b, l, d)


def layer(x, conv_buf, s0, past_k, past_v, lam_init, p):
    (norm_mix_g, w_in, conv_w, a_log, dt_bias, delta_norm_g, q_norm_g, k_norm_g,
     lq1, lk1, lq2, lk2, diff_norm_g, w_out, norm_ffn_g, peer_w_q, peer_sub_keys, peer_u, peer_v) = p
    b, l, _ = x.shape
    xn = rmsnorm(x, norm_mix_g)
    proj = xn @ w_in
    splits, acc = [], 0
    for sz in IN_SPLIT_SIZES[:-1]:
        acc += sz
        splits.append(acc)
    qkv, z, beta_raw, a_raw, fq, fk, fv, gate_a, gate_b = jnp.split(proj, splits, axis=-1)

    ext = jnp.concatenate([conv_buf.astype(qkv.dtype), qkv], axis=1)
    new_conv = ext[:, -(CONV_W - 1):]
    conv = conv_w[0] * ext[:, 0:l]
    for j in range(1, CONV_W):
        conv = conv + conv_w[j] * ext[:, j:j + l]
    qkv_c = jax.nn.silu(conv)
    dq, dk, dv = jnp.split(qkv_c, [DELTA_HEADS * DELTA_DK, 2 * DELTA_HEADS * DELTA_DK], axis=-1)
    dq = l2norm(dq.reshape(b, l, DELTA_HEADS, DELTA_DK)) * (DELTA_DK ** -0.5)
    dk = l2norm(dk.reshape(b, l, DELTA_HEADS, DELTA_DK))
    dv = dv.reshape(b, l, DELTA_HEADS, DELTA_DV)
    beta = jax.nn.sigmoid(beta_raw.astype(jnp.float32))
    g = -jnp.exp(a_log.astype(jnp.float32)) * jax.nn.softplus(a_raw.astype(jnp.float32) + dt_bias.astype(jnp.float32))
    o_a, s_new = gated_delta_rule(dq, dk, dv, g, beta, s0)
    o_a = rmsnorm(o_a, delta_norm_g) * jax.nn.silu(z.reshape(b, l, DELTA_HEADS, DELTA_DV).astype(jnp.float32))
    o_a = o_a.reshape(b, l, DELTA_HEADS * DELTA_DV).astype(x.dtype)

    fq = rmsnorm(fq.reshape(b, l, DIFF_HEADS, 2, DIFF_QK_DIM), q_norm_g)
    fk = rmsnorm(fk.reshape(b, l, DIFF_HEADS, 2, DIFF_QK_DIM), k_norm_g)
    fv = fv.reshape(b, l, DIFF_HEADS, DIFF_V_DIM)
    lam = (jnp.exp(jnp.sum(lq1.astype(jnp.float32) * lk1.astype(jnp.float32)))
           - jnp.exp(jnp.sum(lq2.astype(jnp.float32) * lk2.astype(jnp.float32))) + lam_init)
    if past_k is None:
        o_b = diff_attn_prompt(fq, fk, fv, lam)
    else:
        keys = jnp.concatenate([past_k.astype(fk.dtype), fk], axis=1)
        vals = jnp.concatenate([past_v.astype(fv.dtype), fv], axis=1)
        o_b = diff_softmax_mix(fq, keys, vals, lam, None)
    o_b = (rmsnorm(o_b, diff_norm_g) * (1.0 - lam_init)).reshape(b, l, DIFF_HEADS * DIFF_V_DIM)

    merged = jax.nn.sigmoid(gate_a) * o_a + jax.nn.sigmoid(gate_b) * o_b
    h = x + (merged @ w_out).astype(x.dtype)
    y = h + peer(rmsnorm(h, norm_ffn_g), peer_w_q, peer_sub_keys, peer_u, peer_v).astype(x.dtype)
    return y, fk, fv, s_new, new_conv


def setup_inputs(seed: int = 0) -> dict:
    key = jax.random.key(seed)
    ks = jax.random.split(key, 32)
    f32 = jnp.float32

    def nrm(k, shape, scale):
        return scale * jax.random.normal(k, shape, f32)

    def gain(k, n):
        return 1.0 + 0.02 * jax.random.normal(k, (DEPTH, n), f32)

    dt = jnp.exp(jax.random.uniform(ks[8], (DEPTH, DELTA_HEADS), f32, math.log(1e-3), math.log(1e-1)))
    return {
        'x_prompt': nrm(ks[0], (BATCH, SEQ, D_MODEL), 1.0),
        'x_sample': nrm(ks[1], (DEC_BATCH, DEC_SEQ, D_MODEL), 1.0),
        'cache_diff_k': nrm(ks[2], (DEPTH, DEC_BATCH, PAST_LEN, DIFF_HEADS, 2, DIFF_QK_DIM), 1.0),
        'cache_diff_v': nrm(ks[3], (DEPTH, DEC_BATCH, PAST_LEN, DIFF_HEADS, DIFF_V_DIM), 1.0),
        'state_delta_s': nrm(ks[4], (DEPTH, DEC_BATCH, DELTA_HEADS, DELTA_DK, DELTA_DV), 0.1),
        'state_delta_conv': nrm(ks[5], (DEPTH, DEC_BATCH, CONV_W - 1, DELTA_QKV_W), 1.0),
        'norm_mix_g': gain(ks[6], D_MODEL),
        'w_in': nrm(ks[7], (DEPTH, D_MODEL, IN_W), D_MODEL ** -0.5),
        'conv_w': nrm(ks[9], (DEPTH, CONV_W, DELTA_QKV_W), CONV_W ** -0.5),
        'delta_a_log': jnp.log(jax.random.uniform(ks[10], (DEPTH, DELTA_HEADS), f32, 1.0, 16.0)),
        'delta_dt_bias': dt + jnp.log(-jnp.expm1(-dt)),
        'delta_norm_g': gain(ks[11], DELTA_DV),
        'diff_q_norm_g': gain(ks[12], DIFF_QK_DIM),
        'diff_k_norm_g': gain(ks[13], DIFF_QK_DIM),
        'diff_lambda_q1': nrm(ks[14], (DEPTH, DIFF_QK_DIM), 0.1),
        'diff_lambda_k1': nrm(ks[15], (DEPTH, DIFF_QK_DIM), 0.1),
        'diff_lambda_q2': nrm(ks[16], (DEPTH, DIFF_QK_DIM), 0.1),
        'diff_lambda_k2': nrm(ks[17], (DEPTH, DIFF_QK_DIM), 0.1),
        'diff_norm_g': gain(ks[18], DIFF_V_DIM),
        'w_out': nrm(ks[19], (DEPTH, D_MODEL, D_MODEL), D_MODEL ** -0.5),
        'norm_ffn_g': gain(ks[20], D_MODEL),
        'peer_w_q': nrm(ks[21], (DEPTH, D_MODEL, PEER_HEADS * PEER_DQ), D_MODEL ** -0.5),
        'peer_sub_keys': nrm(ks[22], (DEPTH, PEER_HEADS, 2, N_KEYS, PEER_DHALF), PEER_DHALF ** -0.5),
        'peer_u': nrm(ks[23], (DEPTH, N_EXPERTS, D_MODEL), D_MODEL ** -0.5),
        'peer_v': nrm(ks[24], (DEPTH, N_EXPERTS, D_MODEL), D_MODEL ** -0.5),
    }


def reference(x_prompt, x_sample, cache_diff_k, cache_diff_v, state_delta_s, state_delta_conv,
              norm_mix_g, w_in, conv_w, delta_a_log, delta_dt_bias, delta_norm_g,
              diff_q_norm_g, diff_k_norm_g, diff_lambda_q1, diff_lambda_k1, diff_lambda_q2, diff_lambda_k2,
              diff_norm_g, w_out, norm_ffn_g, peer_w_q, peer_sub_keys, peer_u, peer_v):
    yp, ys = x_prompt, x_sample
    kp, vp, sp, cp = [], [], [], []
    kq, vq, sq, cq = [], [], [], []
    for l in range(DEPTH):
        p = (norm_mix_g[l], w_in[l], conv_w[l], delta_a_log[l], delta_dt_bias[l], delta_norm_g[l],
             diff_q_norm_g[l], diff_k_norm_g[l], diff_lambda_q1[l], diff_lambda_k1[l],
             diff_lambda_q2[l], diff_lambda_k2[l], diff_norm_g[l], w_out[l], norm_ffn_g[l],
             peer_w_q[l], peer_sub_keys[l], peer_u[l], peer_v[l])
        lam_init = 0.8 - 0.6 * math.exp(-0.3 * l)
        conv0 = jnp.zeros((yp.shape[0], CONV_W - 1, DELTA_QKV_W), yp.dtype)
        s0 = jnp.zeros((yp.shape[0], DELTA_HEADS, DELTA_DK, DELTA_DV), jnp.float32)
        yp, k_new, v_new, s_new, c_new = layer(yp, conv0, s0, None, None, lam_init, p)
        kp.append(k_new); vp.append(v_new); sp.append(s_new); cp.append(c_new)
        ys, k_new, v_new, s_new, c_new = layer(ys, state_delta_conv[l], state_delta_s[l],
                                               cache_diff_k[l], cache_diff_v[l], lam_init, p)
        kq.append(k_new); vq.append(v_new); sq.append(s_new); cq.append(c_new)
    return (yp, ys,
            jnp.stack(kp), jnp.stack(vp), jnp.stack(sp), jnp.stack(cp),
            jnp.stack(kq), jnp.stack(vq), jnp.stack(sq), jnp.stack(cq))
```

```python
import numpy as np
import ml_dtypes
import concourse.bass as bass
import concourse.mybir as mybir
from concourse.alu_op_type import AluOpType as ALU
from concourse.bass_utils import run_bass_kernel_spmd

F32 = mybir.dt.float32
BF16 = mybir.dt.bfloat16
I32 = mybir.dt.int32
U32 = mybir.dt.uint32
AF = mybir.ActivationFunctionType
AX = mybir.AxisListType


class Cfg:
    D = 1024
    BATCH = 2
    SEQ = 16384
    DEC_BATCH = 16
    DEC_SEQ = 16
    PAST = 1024
    H = 8
    NKEYS = 128
    TOPK = 16
    EPS = 1e-6
    LAM_INIT = 0.8 - 0.6 * 1.0


EPOCH = 12000
COMPUTE = ("pe", "act", "dve", "pool")


class Buf:
    def __init__(self, ap_owner, name):
        self.t = ap_owner
        self.name = name
        self.w = {}
        self.r = {}

    def __getitem__(self, idx):
        return self.t[idx]


class Op:
    __slots__ = ("eng", "fn", "deps", "idx", "dsem", "dcum", "key")


class Trk:
    def __init__(self, nc):
        self.nc = nc
        self.lists = {e: [] for e in COMPUTE + ("sp",)}
        self.dma_sems = {}
        self.nev = {}
        self.last = {}
        self.bufs = []

    def buf(self, t, name):
        b = Buf(t, name)
        self.bufs.append(b)
        return b

    def barrier(self):
        deps = list(self.last.values())
        for e in COMPUTE + ("sp",):
            self.op(e, lambda eng: eng.nop(), extra=deps)

    def op(self, eng, fn, reads=(), writes=(), dma=None, extra=()):
        o = Op()
        o.eng = eng
        o.fn = fn
        o.dsem = dma
        o.key = ("dma", dma) if dma else eng
        deps = []
        for b in reads:
            for k, w in b.w.items():
                if self._need(o, k, raw=True):
                    deps.append(w)
        for b in writes:
            for k, w in b.w.items():
                if self._need(o, k, raw=False):
                    deps.append(w)
            for k, r in b.r.items():
                if self._need(o, k, raw=False):
                    deps.append(r)
        deps.extend(extra)
        o.deps = deps
        if not extra:
            self.last[o.key] = o
        lst = self.lists[eng]
        lst.append(o)
        if not dma:
            o.idx = self.nev.get(eng, 0)
            self.nev[eng] = o.idx + 1
        if dma:
            ent = self.dma_sems.setdefault(dma, [None, 0])
            ent[1] += 16
            o.dcum = ent[1]
        for b in reads:
            b.r[o.key] = o
        for b in writes:
            b.w[o.key] = o
        return o

    def _need(self, o, k, raw):
        if o.dsem or (isinstance(k, tuple)):
            return True
        if k != o.eng:
            return True
        if o.eng == "pe":
            return False
        return True

    def emit(self):
        nc = self.nc
        import contextlib
        with contextlib.ExitStack() as st:
            sems = {}
            for e in COMPUTE + ("sp",):
                n = self.nev.get(e, 0) // EPOCH + 1
                sems[e] = [st.enter_context(nc.semaphore(f"s_{e}_{i}")) for i in range(n)]
            for name, ent in self.dma_sems.items():
                ent[0] = st.enter_context(nc.semaphore(f"d_{name}"))
            block = st.enter_context(nc.Block())

            def target(dep):
                if dep.dsem:
                    return self.dma_sems[dep.dsem][0], dep.dcum
                return sems[dep.eng][dep.idx // EPOCH], dep.idx % EPOCH + 1

            def run(ename, eng):
                waited = {}
                for o in self.lists[ename]:
                    for d in o.deps:
                        s, v = target(d)
                        kk = id(s)
                        if waited.get(kk, 0) >= v:
                            continue
                        waited[kk] = v
                        eng.wait_ge(s, v)
                    ins = o.fn(eng)
                    if o.dsem:
                        ins.then_inc(self.dma_sems[o.dsem][0], 16)
                    else:
                        ins.then_inc(sems[ename][o.idx // EPOCH], 1)
                if ename == "sp":
                    for name, ent in self.dma_sems.items():
                        eng.wait_ge(ent[0], ent[1])
                    for e2 in COMPUTE:
                        n = self.nev.get(e2, 0)
                        if n:
                            eng.wait_ge(sems[e2][(n - 1) // EPOCH], (n - 1) % EPOCH + 1)

            @block.tensor
            def _(e):
                run("pe", e)

            @block.scalar
            def _(e):
                run("act", e)

            @block.vector
            def _(e):
                run("dve", e)

            @block.gpsimd
            def _(e):
                run("pool", e)

            @block.sync
            def _(e):
                run("sp", e)


QKV_W = 3072
COLS = dict(qkv=(0, 3072), z=(3072, 4096), beta=(4096, 4104), a=(4104, 4112),
            fq=(4112, 5136), fk=(5136, 6160), fv=(6160, 7184), ga=(7184, 8208), gb=(8208, 9232))
IN_W = 9232
ARENA_F32 = 18880
KB = 16
NEG = -1.0e4


def build(cfg, NT):
    nc = bass.Bass("TRN2", target_bir_lowering=False)
    D = cfg.D
    KC = D // 128
    T = Trk(nc)
    import contextlib
    st = contextlib.ExitStack()

    def din(name, shape, dt=F32):
        return nc.dram_tensor(name, list(shape), dt, kind="ExternalInput").ap()

    def dout(name, shape, dt=F32):
        return nc.dram_tensor(name, list(shape), dt, kind="ExternalOutput").ap()

    def dscr(name, shape, dt=BF16):
        return nc.dram_tensor(name, list(shape), dt, kind="Internal").ap()

    def sb(name, shape, dt=F32):
        t = st.enter_context(nc.sbuf_tensor(name, list(shape), dt))
        return T.buf(t, name)

    def ps(name, shape, dt=F32):
        t = st.enter_context(nc.psum_tensor(name, list(shape), dt))
        return T.buf(t, name)

    NOWN = NT // 4
    NS = cfg.DEC_BATCH // 8
    LS = cfg.DEC_SEQ
    PAST = cfg.PAST
    NPT = PAST // 128
    xp = din("xp", [NT * 128, D])
    kmask_in = din("kmask", [128, NT])
    xs_in = din("xs", [NS * LS, D])
    w_in = din("w_in", [D, IN_W])
    w_out = din("w_out", [D, D])
    w_pq = din("w_pq", [D, 2048])
    subk = din("subk", [16 * 128, 128])
    peer_u = din("peer_u", [cfg.NKEYS * cfg.NKEYS, D])
    peer_v = din("peer_v", [cfg.NKEYS * cfg.NKEYS, D])
    g_mix = din("g_mix", [128, KC])
    rows_in = din("rows", [1, 2048])
    ident_in = din("ident", [128, 128])
    conv_w = din("conv_w", [4, QKV_W])
    ss_in = din("ss_in", [NS, 8, 128, 128])
    cs_in = din("cs_in", [NS, 3, QKV_W])
    ck_in = din("ck_in", [NS, PAST, 1024])
    cv_in = din("cv_in", [NS, PAST, 1024])
    smask_in = din("smask", [128, 1])
    y_out = dout("y_own", [NOWN * 128, D])
    k_out = dout("k_own", [NOWN * 128, 1024])
    v_out = dout("v_own", [NOWN * 128, 1024])
    conv_out = dout("conv_fin", [3, QKV_W])
    s_fin = dout("s_fin", [8, 128, 128])
    ys_out = dout("ys", [NS * LS, D])
    ks_out = dout("ks", [NS * LS, 1024])
    vs_out = dout("vs", [NS * LS, 1024])
    cs_out = dout("cs", [NS, 3, QKV_W])
    ss_out = dout("ss", [NS, 8, 128, 128])
    gc = {}
    for C_ in (64, LS):
        ntk = 128 if C_ == 64 else LS
        nch = ntk // C_
        gc[C_] = dict(
            shA=din(f"shA{C_}", [ntk, nch * 4 * C_], BF16), shB=din(f"shB{C_}", [3, 3 * C_], BF16),
            sel=din(f"sel{C_}", [ntk, nch * C_]), tri=din(f"tri{C_}", [C_, C_]),
            msl=din(f"msl{C_}", [C_, C_]),
            msu=din(f"msu{C_}", [C_, C_]), mui=din(f"mui{C_}", [C_, C_]),
            id8=din(f"id8{C_}", [C_, C_]), plc=din(f"plc{C_}", [C_, nch * ntk], BF16))
    kT_scr = dscr("kT_scr", [8, 128, NT * 128])
    v_scr = dscr("v_scr", [8, 128, NT, 128])
    kTs_scr = dscr("kTs_scr", [8, 128, PAST + 128])
    vs_scr = dscr("vs_scr", [8, 128, NPT + 1, 128])
    w_in_b = dscr("w_in_b", [D, IN_W])
    w_out_b = dscr("w_out_b", [D, D])
    w_pq_b = dscr("w_pq_b", [D, 2048])
    pu_b = dscr("pu_b", [cfg.NKEYS * cfg.NKEYS, D])
    pv_b = dscr("pv_b", [cfg.NKEYS * cfg.NKEYS, D])
    wconv_b = T.buf(None, "wconv")
    tconv_b = T.buf(None, "tconv")
    kT_scr_b = T.buf(None, "kT_scr")
    v_scr_b = T.buf(None, "v_scr")
    kTs_scr_b = T.buf(None, "kTs_scr")
    vs_scr_b = T.buf(None, "vs_scr")

    ident_f = sb("ident_f", [128, 128])
    ident_b = sb("ident_b", [128, 128], BF16)
    ones_f = sb("ones_f", [128, 128])
    gmix_t = sb("gmix_t", [128, KC])
    rows = sb("rows_t", [128, 2048])
    qg_t = rows
    negA = sb("negA", [128, 8])
    lam_t = sb("lam_t", [128, 8])
    kmask = sb("kmask_t", [128, NT])
    smask = sb("smask_t", [128, 1])
    xt = sb("xt", [128, D])
    xs_bf = sb("xs_bf", [128, D], BF16)
    junk = xs_bf
    xnT = sb("xnT", [128, KC, 128], BF16)
    stat = sb("stat", [128, 8])
    wbuf = [sb(f"wbuf{i}", [128, KC, 512], BF16) for i in range(2)]
    proj = sb("proj", [128, IN_W])
    kn = sb("kn", [128, 1024])
    sq = xt
    kst = sb("kst", [128, 16])
    wrows = sb("wrows", [128, 4, QKV_W], BF16)
    skT = sb("skT", [128, 16, 128], BF16)
    gcs = {}
    for C_ in (64, LS):
        ntk = 128 if C_ == 64 else LS
        nch = ntk // C_
        gcs[C_] = dict(shA=sb(f"t_shA{C_}", [ntk, nch * 4 * C_], BF16), shB=sb(f"t_shB{C_}", [3, 3 * C_], BF16),
                       sel=sb(f"t_sel{C_}", [ntk, nch * C_]), tri=sb(f"t_tri{C_}", [C_, C_]),
                       msl=sb(f"t_msl{C_}", [C_, C_]),
                       msu=sb(f"t_msu{C_}", [C_, C_]), mui=sb(f"t_mui{C_}", [C_, C_]),
                       id8=sb(f"t_id8{C_}", [C_, C_]), plc=sb(f"t_plc{C_}", [C_, nch * ntk], BF16))
    prodb = [sb("prodb0", [128, 4, 512], BF16)] * 2
    tail = sb("tail", [3, QKV_W], BF16)
    tailpb = [sb("tailpb0", [3, 3, 512], BF16)] * 2
    S_f = sb("S_f", [128, 8, 128])
    S_b = sb("S_b", [128, 8, 128], BF16)
    o_a = sb("o_a", [128, 8, 128])
    kv_b = sb("kv_b", [128, 1024], BF16)
    kT_st = sb("kT_st", [128, 8, 128], BF16)
    arena = st.enter_context(nc.sbuf_tensor("arena", [128, ARENA_F32], F32))
    pbank = [ps(f"pb{i}", [128, 512]) for i in range(8)]

    def bfv(pb):
        return pb.t[:].bitcast(BF16)

    class Phase:
        def __init__(self):
            self.off = 0

        def a(self, name, parts, shape, dt=F32):
            n = 1
            for x in shape:
                n *= x
            words = (n * (2 if dt == BF16 else 4) + 3) // 4
            words = (words + 7) // 8 * 8
            assert self.off + words <= ARENA_F32, (name, self.off, words)
            v = arena[0:parts, self.off:self.off + words]
            self.offs = getattr(self, "offs", {})
            self.offs[name] = self.off
            self.off += words
            if dt != F32:
                v = v.bitcast(dt)
            v = v[:, 0:n]
            if len(shape) == 2:
                v = v.rearrange("p (a b) -> p a b", b=shape[1])
            elif len(shape) == 3:
                v = v.rearrange("p (a b c) -> p a b c", b=shape[1], c=shape[2])
            return T.buf(v, name)

    def dma(eng, out, in_, reads, writes, sem):
        return T.op(eng, lambda e: e.dma_start(out=out, in_=in_), reads=reads, writes=writes, dma=sem)

    def act(out, in_, func, reads, writes, **kw):
        return T.op("act", lambda e: e.activation(out=out, in_=in_, func=func, **kw), reads, writes)

    def tt(out, in0, in1, op, reads, writes, eng="dve"):
        return T.op(eng, lambda e: e.tensor_tensor(out=out, in0=in0, in1=in1, op=op), reads, writes)

    def ts(out, in0, s1, s2, op0, op1, reads, writes):
        if op1 is None:
            return T.op("dve", lambda e: e.tensor_scalar(out=out, in0=in0, scalar1=s1, scalar2=None, op0=op0), reads, writes)
        return T.op("dve", lambda e: e.tensor_scalar(out=out, in0=in0, scalar1=s1, scalar2=s2, op0=op0, op1=op1),
                    reads, writes)

    def mm(out, lhsT, rhs, start, stop, reads, writes):
        return T.op("pe", lambda e: e.matmul(out, lhsT=lhsT, rhs=rhs, start=start, stop=stop), reads, writes)

    def tr(out, in_, n, reads, writes):
        return T.op("pe", lambda e: e.transpose(out=out, in_=in_, identity=ident_b[0:n, 0:n]), list(reads) + [ident_b], writes)

    dma("sp", ident_f[:], ident_in[:, :], [], [ident_f], "c0")
    dma("sp", gmix_t[:], g_mix[:, :], [], [gmix_t], "c1")
    dma("sp", rows[:], rows_in[0:1, :].to_broadcast([128, 2048]), [], [rows], "c2")
    dma("sp", kmask[:], kmask_in[:, :], [], [kmask], "c3")
    dma("sp", smask[:], smask_in[:, :], [], [smask], "c3b")
    T.op("dve", lambda e: e.tensor_copy(out=ident_b[:], in_=ident_f[:]), [ident_f], [ident_b])
    for j in range(4):
        for hf in range(2):
            dma("pool", wrows[:, j, hf * 1536:(hf + 1) * 1536],
                conv_w[j:j + 1, hf * 1536:(hf + 1) * 1536].to_broadcast([128, 1536]), [], [wrows], "c4")
    ci = 7
    for C_ in gcs:
        for nm in gcs[C_]:
            dst = gcs[C_][nm]
            src = gc[C_][nm]
            dma("sp", dst[:], src[:, :], [], [dst], f"c{ci}")
            ci += 1
    act(negA[:], rows[:, 128:136], AF.Exp, [rows], [negA])
    ts(negA[:], negA[:], -1.0, None, ALU.mult, None, [negA], [negA])
    T.op("pool", lambda e: e.memset(ones_f[:], 1.0), [], [ones_f])
    tt(xs_bf[:, 0:64], rows[:, 144:208], rows[:, 208:272], ALU.mult, [rows], [xs_bf])
    T.op("dve", lambda e: e.tensor_reduce(out=lam_t[:, 0:1], in_=xs_bf[:, 0:64], axis=AX.X, op=ALU.add), [xs_bf], [lam_t])
    tt(xs_bf[:, 64:128], rows[:, 272:336], rows[:, 336:400], ALU.mult, [rows], [xs_bf])
    T.op("dve", lambda e: e.tensor_reduce(out=lam_t[:, 1:2], in_=xs_bf[:, 64:128], axis=AX.X, op=ALU.add), [xs_bf], [lam_t])
    act(lam_t[:, 0:2], lam_t[:, 0:2], AF.Exp, [lam_t], [lam_t])
    tt(lam_t[:, 2:3], lam_t[:, 0:1], lam_t[:, 1:2], ALU.subtract, [lam_t], [lam_t])
    ts(lam_t[:, 2:3], lam_t[:, 2:3], cfg.LAM_INIT, None, ALU.add, None, [lam_t], [lam_t])
    ts(lam_t[:, 3:4], lam_t[:, 2:3], -1.0, None, ALU.mult, None, [lam_t], [lam_t])
    for hp in range(16):
        dma("sp", xt[:, 0:128], subk[hp * 128:(hp + 1) * 128, :], [], [xt], "x")
        act(xs_bf[:, 0:128], xt[:, 0:128], AF.Copy, [xt], [xs_bf])
        tr(bfv(pbank[0])[:, 0:128], xs_bf[:, 0:128], 128, [xs_bf], [pbank[0]])
        act(skT[:, hp, :], bfv(pbank[0])[:, 0:128], AF.Copy, [pbank[0]], [skT])

    for (src, dst, ncol) in ((w_in, w_in_b, IN_W), (w_out, w_out_b, D), (w_pq, w_pq_b, 2048)):
        for c0 in range(0, ncol, 2048):
            c1 = min(c0 + 2048, ncol)
            dma("pool", dst[:, c0:c1], src[:, c0:c1], [], [wconv_b], "cvw")
    WSRC = {id(w_in): w_in_b, id(w_out): w_out_b, id(w_pq): w_pq_b}

    def convert_tables():
        for (src, dst) in ((peer_u, pu_b), (peer_v, pv_b)):
            for r0 in range(0, cfg.NKEYS * cfg.NKEYS, 2048):
                dma("pool", dst[r0:r0 + 2048, :], src[r0:r0 + 2048, :], [], [tconv_b], "cvt")

    wcount = {"n": 0}

    def load_w(src, c0, c1):
        i = wcount["n"] % 2
        wcount["n"] += 1
        wb = wbuf[i]
        n = c1 - c0
        srcb = WSRC[id(src)]
        T.op("pool", lambda e: e.dma_start(out=wb[:, :, 0:n],
                                          in_=srcb[:, c0:c1].rearrange("(kc p) n -> p kc n", p=128)),
             reads=[wconv_b], writes=[wb], dma=f"w{i}")
        return wb

    def rstd_of(col_ss, col_out, ntok, n):
        ts(stat[0:ntok, 6:7], stat[0:ntok, col_ss:col_ss + 1], 1.0 / n, cfg.EPS, ALU.mult, ALU.add, [stat], [stat])
        act(stat[0:ntok, 7:8], stat[0:ntok, 6:7], AF.Sqrt, [stat], [stat])
        T.op("dve", lambda e: e.reciprocal(out=stat[0:ntok, col_out:col_out + 1], in_=stat[0:ntok, 7:8]), [stat], [stat])

    def norm_T(src, srcb, dstT, ntok, gcol):
        act(junk[0:ntok, :], src, AF.Square, [srcb], [junk, stat], accum_out=stat[0:ntok, 0:1])
        rstd_of(0, 3, ntok, D)
        act(xs_bf[0:ntok, :], src, AF.Copy, [srcb, stat], [xs_bf], scale=stat[0:ntok, 3:4])
        pb = pbank[0]
        pbv = bfv(pb)
        for kc in range(KC):
            tr(pbv[:, kc * 128:kc * 128 + ntok], xs_bf[0:ntok, kc * 128:(kc + 1) * 128], ntok, [xs_bf], [pb])
        for kc in range(KC):
            if gcol is not None:
                ts(dstT[:, kc, 0:ntok], pbv[:, kc * 128:kc * 128 + ntok], gcol[:, kc:kc + 1], None, ALU.mult, None,
                   [pb, gmix_t], [dstT])
            else:
                act(dstT[:, kc, 0:ntok], pbv[:, kc * 128:kc * 128 + ntok], AF.Copy, [pb], [dstT])

    def stage_a(xsrc, ntok):
        dma("sp", xt[0:ntok, :], xsrc, [], [xt], "x")
        norm_T(xt[0:ntok, :], xt, xnT, ntok, gmix_t)

    pcount = {"n": 0}

    def project(src, lhsT_buf, c0, c1, ntok, sink):
        for b0 in range(c0, c1, 512):
            b1 = min(b0 + 512, c1)
            n = b1 - b0
            wb = load_w(src, b0, b1)
            pb = pbank[1 + pcount["n"] % 2]
            pcount["n"] += 1
            for kc in range(KC):
                mm(pb[0:ntok, 0:n], lhsT_buf[:, kc, 0:ntok], wb[:, kc, 0:n], kc == 0, kc == KC - 1, [lhsT_buf, wb], [pb])
            sink(pb, b0, b1)

    def inproj(c0, c1, ntok):
        project(w_in, xnT, c0, c1, ntok,
                lambda pb, b0, b1: act(proj[0:ntok, b0:b1], pb[0:ntok, 0:b1 - b0], AF.Copy, [pb], [proj]))

    def qknorm(dst, c0, g0, ntok):
        v3 = lambda ap: ap.rearrange("p (g d) -> p g d", d=64)
        act(sq[0:ntok, :], proj[0:ntok, c0:c0 + 1024], AF.Square, [proj], [sq])
        T.op("dve", lambda e: e.tensor_reduce(out=kst[0:ntok, :], in_=v3(sq[0:ntok, :]), axis=AX.X, op=ALU.add), [sq], [kst])
        ts(kst[0:ntok, :], kst[0:ntok, :], 1.0 / 64, cfg.EPS, ALU.mult, ALU.add, [kst], [kst])
        act(kst[0:ntok, :], kst[0:ntok, :], AF.Sqrt, [kst], [kst])
        T.op("dve", lambda e: e.reciprocal(out=kst[0:ntok, :], in_=kst[0:ntok, :]), [kst], [kst])
        tt(v3(dst[0:ntok, :]), v3(proj[0:ntok, c0:c0 + 1024]), kst[0:ntok, :].unsqueeze(2).to_broadcast([ntok, 16, 64]),
           ALU.mult, [proj, kst], [dst])
        tt(v3(dst[0:ntok, :]), v3(dst[0:ntok, :]), rows[0:ntok, g0:g0 + 64].unsqueeze(1).to_broadcast([ntok, 16, 64]),
           ALU.mult, [dst, rows], [dst])

    def stage_c(k_ap, k_b, v_ap, v_b, ntok, kT_dst, kT_dst_b, v_dst, v_dst_b):
        if getattr(cfg, "DBG_NOC", 0):
            return
        act(kv_b[0:ntok, :], k_ap, AF.Copy, [k_b], [kv_b])
        pb = pbank[0]
        for h in range(8):
            tr(bfv(pb)[:, h * 128:h * 128 + ntok], kv_b[0:ntok, h * 128:(h + 1) * 128], ntok, [kv_b], [pb])
        act(kT_st[:, :, 0:ntok], bfv(pb)[:, :].rearrange("p (h t) -> p h t", t=128)[:, :, 0:ntok], AF.Copy, [pb], [kT_st])
        dma("sp", kT_dst.rearrange("h d t -> d h t"), kT_st[:, :, 0:ntok], [kT_st], [kT_dst_b], "ks")
        act(kv_b[0:ntok, :], v_ap, AF.Copy, [v_b, kT_st], [kv_b])
        dma("sp", v_dst.rearrange("h p d -> p h d"), kv_b[0:ntok, :].rearrange("p (h d) -> p h d", d=128), [kv_b], [v_dst_b], "vs")

    gb = {"n": 0}

    def gbank():
        b = pbank[3 + gb["n"] % 5]
        gb["n"] += 1
        return b

    G_ph = Phase()
    qkvc = G_ph.a("qkvc", 64, [QKV_W])
    ba = G_ph.a("ba", 64, [16])
    gst = G_ph.a("gst", 64, [96])
    eGlB = G_ph.a("eGlB", 128, [8])
    knf = G_ph.a("knf", 64, [8, 128])
    kn_b = G_ph.a("kn_b", 64, [8, 128], BF16)
    kb_b = G_ph.a("kb_b", 64, [8, 128], BF16)
    kd_b = G_ph.a("kd_b", 64, [8, 128], BF16)
    qn_b = G_ph.a("qn_b", 64, [8, 128], BF16)
    qg_b = G_ph.a("qg_b", 64, [8, 128], BF16)
    vb_f = G_ph.a("vb_f", 64, [8, 128])
    knT = G_ph.a("knT", 128, [8, 64], BF16)
    kbT = G_ph.a("kbT", 128, [8, 64], BF16)
    qnT = G_ph.a("qnT", 128, [8, 64], BF16)
    qgT = G_ph.a("qgT", 128, [8, 64], BF16)
    gtri = G_ph.a("gtri", 64, [8, 64])
    dif = G_ph.a("dif", 64, [8, 64])
    earg = G_ph.a("earg", 64, [8, 64])
    Dsl = G_ph.a("Dsl", 64, [8, 64])
    DTsu = G_ph.a("DTsu", 64, [8, 64])
    DTui = G_ph.a("DTui", 64, [8, 64])
    intraT = G_ph.a("intraT", 64, [8, 64], BF16)
    Nb = [G_ph.a(f"Nb{i}", 64, [8, 64], BF16) for i in range(2)]
    Mb = [G_ph.a(f"Mb{i}", 64, [8, 64], BF16) for i in range(2)]
    Pf = G_ph.a("Pf", 64, [8, 64])
    Qf = G_ph.a("Qf", 64, [8, 64])
    Pb = [G_ph.a(f"Pb{i}", 64, [8, 64], BF16) for i in range(2)]
    Qb = [G_ph.a(f"Qb{i}", 64, [8, 64], BF16) for i in range(2)]
    tmpks = G_ph.a("tmpks", 64, [8, 128])
    rhs2 = G_ph.a("rhs2", 64, [8, 128], BF16)
    vnew_b = G_ph.a("vnew_b", 64, [8, 128], BF16)
    o_ch = [G_ph.a(f"o_ch{i}", 64, [8, 128], BF16) for i in range(2)]

    def bc8(buf, lo, C, n):
        return buf[0:C, lo:lo + 8].unsqueeze(2).to_broadcast([C, 8, n])

    def hv(ap, d):
        return ap.rearrange("p (h d) -> p h d", d=d)

    def l2norm_cols(c0, col, C, scale):
        act(tmpks[0:C, :, :].rearrange("p h d -> p (h d)"), qkvc[0:C, c0:c0 + 1024], AF.Square, [qkvc], [tmpks])
        T.op("dve", lambda e: e.tensor_reduce(out=gst[0:C, col:col + 8], in_=tmpks[0:C, :, :], axis=AX.X, op=ALU.add),
             [tmpks], [gst])
        ts(gst[0:C, col:col + 8], gst[0:C, col:col + 8], cfg.EPS, None, ALU.add, None, [gst], [gst])
        act(gst[0:C, col:col + 8], gst[0:C, col:col + 8], AF.Sqrt, [gst], [gst])
        T.op("dve", lambda e: e.reciprocal(out=gst[0:C, col:col + 8], in_=gst[0:C, col:col + 8]), [gst], [gst])
        if scale != 1.0:
            ts(gst[0:C, col:col + 8], gst[0:C, col:col + 8], scale, None, ALU.mult, None, [gst], [gst])

    def gdn_chunk(C, ntok, c, own):
        G = gcs[C]
        nd = {64: 5, 16: 3}[C]
        m3 = lambda ap: ap.rearrange("p (h c) -> p h c", c=C)
        g3 = lambda nm: G[nm][0:C, 0:C].unsqueeze(1).to_broadcast([C, 8, C])
        for cb in (range(0, 6) if own else range(2, 6)):
            pi = cb % 2
            cs_ = slice(cb * 512, (cb + 1) * 512)
            tt(prodb[pi][0:ntok], proj[0:ntok, cs_].unsqueeze(1).to_broadcast([ntok, 4, 512]), wrows[0:ntok, :, cs_],
               ALU.mult, [proj, wrows], [prodb[pi]])
            lst = [(G["shA"][0:ntok, (c * 4 + j) * C:(c * 4 + j + 1) * C], prodb[pi][0:ntok, j, :], [G["shA"], prodb[pi]])
                   for j in range(4)]
            if c == 0:
                tt(tailpb[pi][:], tail[:, cs_].unsqueeze(1).to_broadcast([3, 3, 512]), wrows[0:3, 0:3, cs_], ALU.mult,
                   [tail, wrows], [tailpb[pi]])
                lst += [(G["shB"][0:3, j * C:(j + 1) * C], tailpb[pi][0:3, j, :], [G["shB"], tailpb[pi]]) for j in range(3)]
            pb = gbank()
            for i, (l, r, rd) in enumerate(lst):
                mm(pb[0:C, :], l, r, i == 0, i == len(lst) - 1, rd, [pb])
            act(qkvc[0:C, cs_], pb[0:C, :], AF.Silu, [pb], [qkvc])
        pb = gbank()
        mm(pb[0:C, 0:16], G["sel"][0:ntok, c * C:(c + 1) * C], proj[0:ntok, 4096:4112], True, True, [G["sel"], proj], [pb])
        T.op("dve", lambda e, pb=pb: e.tensor_copy(out=ba[0:C, :], in_=pb[0:C, 0:16]), [pb], [ba])
        act(gst[0:C, 0:8], ba[0:C, 0:8], AF.Sigmoid, [ba], [gst])
        tt(gst[0:C, 8:16], ba[0:C, 8:16], rows[0:C, 136:144], ALU.add, [ba, rows], [gst])
        act(gst[0:C, 8:16], gst[0:C, 8:16], AF.Exp, [gst], [gst])
        act(gst[0:C, 8:16], gst[0:C, 8:16], AF.Ln, [gst], [gst], bias=1.0)
        tt(gst[0:C, 8:16], gst[0:C, 8:16], negA[0:C, :], ALU.mult, [gst, negA], [gst])
        pb = gbank()
        mm(pb[0:C, 0:8], G["tri"][0:C, 0:C], gst[0:C, 8:16], True, True, [G["tri"], gst], [pb])
        mm(pb[0:C, 8:16], ones_f[0:C, 0:C], gst[0:C, 8:16], True, True, [ones_f, gst], [pb])
        mm(pb[0:128, 16:24], ones_f[0:C, 0:128], gst[0:C, 8:16], True, True, [ones_f, gst], [pb])
        T.op("dve", lambda e, pb=pb: e.tensor_copy(out=gst[0:C, 16:32], in_=pb[0:C, 0:16]), [pb], [gst])
        act(eGlB[:, :], pb[0:128, 16:24], AF.Exp, [pb], [eGlB])
        act(gst[0:C, 32:40], gst[0:C, 16:24], AF.Exp, [gst], [gst])
        tt(gst[0:C, 40:48], gst[0:C, 32:40], gst[0:C, 0:8], ALU.mult, [gst], [gst])
        tt(gst[0:C, 48:56], gst[0:C, 24:32], gst[0:C, 16:24], ALU.subtract, [gst], [gst])
        act(gst[0:C, 48:56], gst[0:C, 48:56], AF.Exp, [gst], [gst])
        l2norm_cols(1024, 56, C, 1.0)
        kview = hv(qkvc[0:C, 1024:2048], 128)
        vview = hv(qkvc[0:C, 2048:3072], 128)
        tt(knf[0:C], kview, bc8(gst, 56, C, 128), ALU.mult, [qkvc, gst], [knf])
        act(kn_b[0:C], knf[0:C], AF.Copy, [knf], [kn_b])
        tt(kb_b[0:C], knf[0:C], bc8(gst, 0, C, 128), ALU.mult, [knf, gst], [kb_b])
        tt(kd_b[0:C], knf[0:C], bc8(gst, 48, C, 128), ALU.mult, [knf, gst], [kd_b])
        tt(vb_f[0:C], vview, bc8(gst, 0, C, 128), ALU.mult, [qkvc, gst], [vb_f])
        pairs = [(kn_b, knT), (kb_b, kbT)]
        if own:
            l2norm_cols(0, 64, C, 128.0 ** -0.5)
            qview = hv(qkvc[0:C, 0:1024], 128)
            tt(knf[0:C], qview, bc8(gst, 64, C, 128), ALU.mult, [qkvc, gst, kn_b, kb_b, kd_b], [knf])
            act(qn_b[0:C], knf[0:C], AF.Copy, [knf], [qn_b])
            tt(qg_b[0:C], knf[0:C], bc8(gst, 32, C, 128), ALU.mult, [knf, gst], [qg_b])
            pairs += [(qn_b, qnT), (qg_b, qgT)]
        for src, dstT in pairs:
            pb = gbank()
            pbv = bfv(pb)
            for h in range(8):
                tr(pbv[:, h * C:(h + 1) * C], src[0:C, h, :], C, [src], [pb])
            act(dstT[:, :, 0:C], m3(pbv[:, 0:8 * C]), AF.Copy, [pb], [dstT])
        tt(gtri[0:C, :, 0:C], g3("tri"), bc8(gst, 8, C, C), ALU.mult, [G["tri"], gst], [gtri])
        pbG = gbank()
        for h in range(8):
            mm(pbG[0:C, h * C:(h + 1) * C], ones_f[0:C, 0:C], gtri[0:C, h, 0:C], True, True, [ones_f, gtri], [pbG])
        tt(dif[0:C, :, 0:C], m3(pbG[0:C, 0:8 * C]), bc8(gst, 16, C, C), ALU.subtract, [pbG, gst], [dif])
        ts(earg[0:C, :, 0:C], dif[0:C, :, 0:C], 0.0, -1.0, ALU.max, ALU.mult, [dif], [earg])
        act(earg[0:C, :, 0:C], earg[0:C, :, 0:C], AF.Exp, [earg], [earg])
        tt(Dsl[0:C, :, 0:C], earg[0:C, :, 0:C], g3("msl"), ALU.mult, [earg, G["msl"]], [Dsl])
        ts(earg[0:C, :, 0:C], dif[0:C, :, 0:C], 0.0, None, ALU.min, None, [dif, Dsl], [earg])
        act(earg[0:C, :, 0:C], earg[0:C, :, 0:C], AF.Exp, [earg], [earg])
        tt(DTsu[0:C, :, 0:C], earg[0:C, :, 0:C], g3("msu"), ALU.mult, [earg, G["msu"]], [DTsu])
        if own:
            tt(DTui[0:C, :, 0:C], earg[0:C, :, 0:C], g3("mui"), ALU.mult, [earg, G["mui"]], [DTui])
            pb = gbank()
            for h in range(8):
                mm(pb[0:C, h * C:(h + 1) * C], knT[:, h, 0:C], qnT[:, h, 0:C], True, True, [knT, qnT], [pb])
            tt(intraT[0:C, :, 0:C], m3(pb[0:C, 0:8 * C]), DTui[0:C, :, 0:C], ALU.mult, [pb, DTui], [intraT])
        for (la, ra, Dm, outb, accf, accb) in ((kbT, knT, Dsl, Nb[0], Qf, Qb[0]), (knT, kbT, DTsu, Mb[0], Pf, Pb[0])):
            pb = gbank()
            for h in range(8):
                mm(pb[0:C, h * C:(h + 1) * C], la[:, h, 0:C], ra[:, h, 0:C], True, True, [la, ra], [pb])
            T.op("dve", lambda e, pb=pb, Dm=Dm, outb=outb: e.scalar_tensor_tensor(
                out=outb[0:C, :, 0:C], in0=m3(pb[0:C, 0:8 * C]), scalar=-1.0, in1=Dm[0:C, :, 0:C], op0=ALU.mult,
                op1=ALU.mult), [pb, Dm], [outb])
            tt(accf[0:C, :, 0:C], outb[0:C, :, 0:C], g3("id8"), ALU.add, [outb, G["id8"]], [accf])
            act(accb[0:C, :, 0:C], accf[0:C, :, 0:C], AF.Copy, [accf], [accb])
        cur = 0
        for k in range(1, nd + 1):
            nxt = 1 - cur
            last = (k == nd)
            pbN, pbM = gbank(), gbank()
            for h in range(8):
                if not last:
                    mm(pbN[0:C, h * C:(h + 1) * C], Mb[cur][0:C, h, 0:C], Nb[cur][0:C, h, 0:C], True, True,
                       [Mb[cur], Nb[cur]], [pbN])
                mm(pbM[0:C, h * C:(h + 1) * C], Nb[cur][0:C, h, 0:C], Mb[cur][0:C, h, 0:C], True, True,
                   [Mb[cur], Nb[cur]], [pbM])
            if not last:
                act(Nb[nxt][0:C, :, 0:C], m3(pbN[0:C, 0:8 * C]), AF.Copy, [pbN], [Nb[nxt]])
            T.op("dve", lambda e, nxt=nxt, pbM=pbM: e.tensor_copy(out=Mb[nxt][0:C, :, 0:C], in_=m3(pbM[0:C, 0:8 * C])),
                 [pbM], [Mb[nxt]])
            pbP, pbQ = gbank(), gbank()
            for h in range(8):
                mm(pbP[0:C, h * C:(h + 1) * C], Qb[cur][0:C, h, 0:C], Mb[nxt][0:C, h, 0:C], True, True,
                   [Qb[cur], Mb[nxt]], [pbP])
                if not last:
                    mm(pbQ[0:C, h * C:(h + 1) * C], Pb[cur][0:C, h, 0:C], Nb[nxt][0:C, h, 0:C], True, True,
                       [Pb[cur], Nb[nxt]], [pbQ])
            tt(Pf[0:C, :, 0:C], m3(pbP[0:C, 0:8 * C]), Pf[0:C, :, 0:C], ALU.add, [pbP, Pf], [Pf])
            act(Pb[nxt][0:C, :, 0:C], Pf[0:C, :, 0:C], AF.Copy, [Pf], [Pb[nxt]])
            if not last:
                tt(Qf[0:C, :, 0:C], m3(pbQ[0:C, 0:8 * C]), Qf[0:C, :, 0:C], ALU.add, [pbQ, Qf], [Qf])
                act(Qb[nxt][0:C, :, 0:C], Qf[0:C, :, 0:C], AF.Copy, [Qf], [Qb[nxt]])
            cur = nxt
        PT = Pb[cur]
        for half in range(2):
            pb = gbank()
            for hh in range(4):
                h = half * 4 + hh
                mm(pb[0:C, hh * 128:(hh + 1) * 128], knT[:, h, 0:C], S_b[:, h, :], True, True, [knT, S_b], [pb])
            hs = slice(half * 4, half * 4 + 4)
            tt(tmpks[0:C, hs, :], hv(pb[0:C, :], 128),
               gst[0:C, 40 + half * 4:44 + half * 4].unsqueeze(2).to_broadcast([C, 4, 128]), ALU.mult, [pb, gst], [tmpks])
        tt(rhs2[0:C], vb_f[0:C], tmpks[0:C], ALU.subtract, [vb_f, tmpks], [rhs2])
        for half in range(2):
            pb = gbank()
            for hh in range(4):
                h = half * 4 + hh
                mm(pb[0:C, hh * 128:(hh + 1) * 128], PT[0:C, h, 0:C], rhs2[0:C, h, :], True, True, [PT, rhs2], [pb])
            hs = slice(half * 4, half * 4 + 4)
            act(vnew_b[0:C, hs, :], hv(pb[0:C, :], 128), AF.Copy, [pb], [vnew_b])
        if own:
            for half in range(2):
                pb = gbank()
                for hh in range(4):
                    h = half * 4 + hh
                    mm(pb[0:C, hh * 128:(hh + 1) * 128], qgT[:, h, 0:C], S_b[:, h, :], True, False, [qgT, S_b], [pb])
                    mm(pb[0:C, hh * 128:(hh + 1) * 128], intraT[0:C, h, 0:C], vnew_b[0:C, h, :], False, True,
                       [intraT, vnew_b], [pb])
                hs = slice(half * 4, half * 4 + 4)
                act(o_ch[c][0:C, hs, :], hv(pb[0:C, :], 128), AF.Copy, [pb], [o_ch[c]])
        tt(S_f[:], S_f[:], eGlB[:, :].unsqueeze(2).to_broadcast([128, 8, 128]), ALU.mult, [S_f, eGlB], [S_f])
        for half in range(2):
            pb = gbank()
            for hh in range(4):
                h = half * 4 + hh
                mm(pb[:, hh * 128:(hh + 1) * 128], kd_b[0:C, h, :], vnew_b[0:C, h, :], True, True, [kd_b, vnew_b], [pb])
            hs = slice(half * 4, half * 4 + 4)
            tt(S_f[:, hs, :], hv(pb[:, :], 128), S_f[:, hs, :], ALU.add, [pb, S_f], [S_f])
        act(S_b[:], S_f[:], AF.Copy, [S_f], [S_b])

    def gdn_tile(C, ntok, own):
        nch = ntok // C
        for c in range(nch):
            gdn_chunk(C, ntok, c, own)
        if own:
            G = gcs[C]
            for half in range(2):
                pb = gbank()
                for c in range(nch):
                    mm(pb[0:ntok, :], G["plc"][0:C, c * ntok:(c + 1) * ntok],
                       o_ch[c][0:C, half * 4:half * 4 + 4, :].rearrange("p h d -> p (h d)"), c == 0, c == nch - 1,
                       [G["plc"], o_ch[c]], [pb])
                act(o_a[0:ntok, half * 4:half * 4 + 4, :], hv(pb[0:ntok, :], 128), AF.Copy, [pb], [o_a])

    A_ph = Phase()
    o_bn = A_ph.a("o_bn", 128, [8, 128])
    qn_f = A_ph.a("qn_f", 128, [1024])
    qT = A_ph.a("qT", 128, [8, 128], BF16)
    KTb = [A_ph.a(f"KTb{i}", 128, [KB * 128], BF16) for i in range(2)]
    Vb = [A_ph.a(f"Vb{i}", 128, [KB, 132], BF16) for i in range(2)]
    PTb = [[A_ph.a(f"PT{m}{i}", 128, [512], BF16) for i in range(2)] for m in range(2)]
    o_b = A_ph.a("o_b", 128, [8, 128])
    ast = A_ph.a("ast", 128, [16])

    def attention(ntq, nkt, kT_of, kT_b, v_of, v_b, bias_of, diag_kt):
        qknorm(qn_f, COLS["fq"][0], 0, ntq)
        act(kv_b[0:ntq, :], qn_f[0:ntq, :], AF.Copy, [qn_f], [kv_b])
        pb = pbank[6]
        for h in range(8):
            tr(bfv(pb)[:, h * 128:h * 128 + ntq], kv_b[0:ntq, h * 128:(h + 1) * 128], ntq, [kv_b], [pb])
        act(qT[:, :, 0:ntq], bfv(pb)[:, :].rearrange("p (h t) -> p h t", t=128)[:, :, 0:ntq], AF.Copy, [pb], [qT])
        for i in range(2):
            T.op("pool", lambda e, i=i: e.memset(Vb[i][:, :, 128:129], 1.0), [], [Vb[i]])
        lc = 0
        gcount = 0
        pend = []

        def flush():
            while pend:
                pend.pop(0)()

        for h in range(8):
            Ob = [pbank[4 + 2 * (h % 2)], pbank[5 + 2 * (h % 2)]]
            for b0 in range(0, nkt, KB):
                nb = min(KB, nkt - b0)
                bi = lc % 2
                lc += 1
                dma("sp", KTb[bi][:, 0:nb * 128], kT_of(h, b0, b0 + nb), [kT_b], [KTb[bi]], f"kt{bi}")
                dma("sp", Vb[bi][:, 0:nb, 0:128], v_of(h, b0, b0 + nb), [v_b], [Vb[bi]], f"vv{bi}")
                for g0 in range(0, nb, 4):
                    ng = min(4, nb - g0)
                    par = gcount % 2
                    gcount += 1
                    pvs = []
                    for m in range(2):
                        Sb = pbank[m * 2 + par]
                        PT = PTb[m][par]
                        ms = slice(m * 64, (m + 1) * 64)
                        for i in range(ng):
                            kt = g0 + i
                            mm(Sb[:, i * ntq:(i + 1) * ntq], KTb[bi][ms, kt * 128:(kt + 1) * 128], qT[ms, h, 0:ntq], True, True,
                               [KTb[bi], qT], [Sb])
                        biases = [bias_of(b0 + g0 + i) for i in range(ng)]
                        if any(bb is not None for bb in biases):
                            for i in range(ng):
                                kw = {} if biases[i] is None else {"bias": biases[i]}
                                act(PT[:, i * ntq:(i + 1) * ntq], Sb[:, i * ntq:(i + 1) * ntq], AF.Exp, [Sb, kmask, smask], [PT],
                                    scale=0.125, **kw)
                        else:
                            act(PT[:, 0:ng * ntq], Sb[:, 0:ng * ntq], AF.Exp, [Sb], [PT], scale=0.125)
                        for i in range(ng):
                            kt_abs = b0 + g0 + i
                            if kt_abs == diag_kt:
                                T.op("dve", lambda e, PT=PT, i=i: e.memset(PT[64:128, i * ntq:i * ntq + 64], 0.0), [], [PT])

                        def pv(m=m, PT=PT, bi=bi, g0=g0, ng=ng, b0=b0, Ob=Ob):
                            for i in range(ng):
                                kt = g0 + i
                                kt_abs = b0 + kt
                                mm(Ob[m][0:ntq, 0:129], PT[:, i * ntq:(i + 1) * ntq], Vb[bi][:, kt, 0:129], kt_abs == 0,
                                   kt_abs == nkt - 1, [PT, Vb[bi]], [Ob[m]])
                        pvs.append(pv)
                    flush()
                    pend.extend(pvs)

            def fin(h=h, Ob=Ob):
                T.op("dve", lambda e: e.reciprocal(out=ast[0:ntq, 0:1], in_=Ob[0][0:ntq, 128:129]), [Ob[0]], [ast])
                T.op("dve", lambda e: e.reciprocal(out=ast[0:ntq, 1:2], in_=Ob[1][0:ntq, 128:129]), [Ob[1]], [ast])
                tt(ast[0:ntq, 1:2], ast[0:ntq, 1:2], lam_t[0:ntq, 3:4], ALU.mult, [ast, lam_t], [ast])
                ts(o_b[0:ntq, h, :], Ob[0][0:ntq, 0:128], ast[0:ntq, 0:1], None, ALU.mult, None, [Ob[0], ast], [o_b])
                T.op("dve", lambda e: e.scalar_tensor_tensor(out=o_b[0:ntq, h, :], in0=Ob[1][0:ntq, 0:128],
                                                             scalar=ast[0:ntq, 1:2], in1=o_b[0:ntq, h, :], op0=ALU.mult,
                                                             op1=ALU.add), [Ob[1], ast, o_b], [o_b])
            pend.append(fin)
        flush()
        head_rms(o_b, o_bn, ntq, 528, 1.0 - cfg.LAM_INIT, ast, 8)

    def head_rms(src, dst, ntok, grow0, mul, stbuf, scol):
        act(kn[0:ntok, :], src[0:ntok].rearrange("p h d -> p (h d)"), AF.Square, [src], [kn])
        T.op("dve", lambda e: e.tensor_reduce(out=stbuf[0:ntok, scol:scol + 8], in_=hv(kn[0:ntok, :], 128), axis=AX.X,
                                              op=ALU.add), [kn], [stbuf])
        ts(stbuf[0:ntok, scol:scol + 8], stbuf[0:ntok, scol:scol + 8], 1.0 / 128, cfg.EPS, ALU.mult, ALU.add, [stbuf], [stbuf])
        act(stbuf[0:ntok, scol:scol + 8], stbuf[0:ntok, scol:scol + 8], AF.Sqrt, [stbuf], [stbuf])
        T.op("dve", lambda e: e.reciprocal(out=stbuf[0:ntok, scol:scol + 8], in_=stbuf[0:ntok, scol:scol + 8]), [stbuf], [stbuf])
        if mul != 1.0:
            ts(stbuf[0:ntok, scol:scol + 8], stbuf[0:ntok, scol:scol + 8], mul, None, ALU.mult, None, [stbuf], [stbuf])
        tt(dst[0:ntok], src[0:ntok], stbuf[0:ntok, scol:scol + 8].unsqueeze(2).to_broadcast([ntok, 8, 128]), ALU.mult,
           [src, stbuf], [dst])
        tt(dst[0:ntok], dst[0:ntok], rows[0:ntok, grow0:grow0 + 128].unsqueeze(1).to_broadcast([ntok, 8, 128]), ALU.mult,
           [dst, rows], [dst])

    P_ph = Phase()
    assert A_ph.off >= 0
    P_ph.off = 1024
    o_an = P_ph.a("o_an", 128, [8, 128])
    sg = P_ph.a("sg", 128, [1024])
    mg_b = P_ph.a("mg_b", 128, [1024], BF16)
    mgT = P_ph.a("mgT", 128, [8, 128], BF16)
    h_f = P_ph.a("h_f", 128, [1024])
    hn_b = P_ph.a("hn_b", 128, [1024], BF16)
    hnT = P_ph.a("hnT", 128, [8, 128], BF16)
    pq_b = P_ph.a("pq_b", 128, [2048], BF16)
    pqT = P_ph.a("pqT", 128, [16, 128], BF16)
    sc = P_ph.a("sc", 128, [16, 128])
    sc2 = P_ph.a("sc2", 128, [256])
    sv = P_ph.a("sv", 128, [16, 16])
    si = P_ph.a("si", 128, [16, 16], U32)
    sif = P_ph.a("sif", 128, [16, 16])
    cand = P_ph.a("cand", 128, [8, 256])
    cid = P_ph.a("cid", 128, [8, 256])
    tv = P_ph.a("tv", 128, [8, 16])
    eid = P_ph.a("eid", 128, [128])
    eid_i = P_ph.a("eid_i", 128, [128], I32)
    gate = P_ph.a("gate", 128, [8, 16])
    pst = P_ph.a("pst", 128, [32])
    actv = P_ph.a("actv", 128, [128])
    wgt = P_ph.a("wgt", 128, [128])
    gbuf = [P_ph.a(f"gbuf{i}", 128, [1024], BF16) for i in range(4)]
    dgb = [P_ph.a(f"dgb{i}", 128, [128], BF16) for i in range(2)]
    pjunk = P_ph.a("pjunk", 128, [1024], BF16)
    for nm_ in ("sc", "cand", "cid"):
        for i_ in range(4):
            o_ = P_ph.offs[nm_] + i_ * 512
            gbuf.append(T.buf(arena[0:128, o_:o_ + 512].bitcast(BF16), f"gal_{nm_}{i_}"))
    NG = len(gbuf)


    def merge_peer(ntok, xsrc, ydst):
        head_rms(o_a, o_an, ntok, 400, 1.0, pst, 0)
        act(sg[0:ntok, :], proj[0:ntok, COLS["z"][0]:COLS["z"][1]], AF.Silu, [proj], [sg])
        tt(o_an[0:ntok].rearrange("p h d -> p (h d)"), o_an[0:ntok].rearrange("p h d -> p (h d)"), sg[0:ntok, :], ALU.mult,
           [o_an, sg], [o_an])
        act(sg[0:ntok, :], proj[0:ntok, COLS["ga"][0]:COLS["ga"][1]], AF.Sigmoid, [proj, o_an], [sg])
        tt(o_an[0:ntok].rearrange("p h d -> p (h d)"), o_an[0:ntok].rearrange("p h d -> p (h d)"), sg[0:ntok, :], ALU.mult,
           [o_an, sg], [o_an])
        act(sg[0:ntok, :], proj[0:ntok, COLS["gb"][0]:COLS["gb"][1]], AF.Sigmoid, [proj, o_an], [sg])
        tt(sg[0:ntok, :], sg[0:ntok, :], o_bn[0:ntok].rearrange("p h d -> p (h d)"), ALU.mult, [sg, o_bn], [sg])
        tt(mg_b[0:ntok, :], sg[0:ntok, :], o_an[0:ntok].rearrange("p h d -> p (h d)"), ALU.add, [sg, o_an], [mg_b])
        pb = pbank[0]
        for kc in range(KC):
            tr(bfv(pb)[:, kc * 128:kc * 128 + ntok], mg_b[0:ntok, kc * 128:(kc + 1) * 128], ntok, [mg_b], [pb])
        act(mgT[:, :, 0:ntok], bfv(pb)[:, :].rearrange("p (h t) -> p h t", t=128)[:, :, 0:ntok], AF.Copy, [pb], [mgT])
        dma("sp", h_f[0:ntok, :], xsrc, [], [h_f], "hx")
        project(w_out, mgT, 0, D, ntok,
                lambda pb, b0, b1: tt(h_f[0:ntok, b0:b1], pb[0:ntok, 0:b1 - b0], h_f[0:ntok, b0:b1], ALU.add, [pb, h_f], [h_f]))
        act(pjunk[0:ntok, :], h_f[0:ntok, :], AF.Square, [h_f], [pjunk, stat], accum_out=stat[0:ntok, 0:1])
        rstd_of(0, 3, ntok, D)
        act(sg[0:ntok, :], h_f[0:ntok, :], AF.Copy, [h_f, stat], [sg], scale=stat[0:ntok, 3:4])
        tt(hn_b[0:ntok, :], sg[0:ntok, :], rows[0:ntok, 1024:2048], ALU.mult, [sg, rows], [hn_b])
        pb = pbank[0]
        for kc in range(KC):
            tr(bfv(pb)[:, kc * 128:kc * 128 + ntok], hn_b[0:ntok, kc * 128:(kc + 1) * 128], ntok, [hn_b], [pb])
        act(hnT[:, :, 0:ntok], bfv(pb)[:, :].rearrange("p (h t) -> p h t", t=128)[:, :, 0:ntok], AF.Copy, [pb], [hnT])
        project(w_pq, hnT, 0, 2048, ntok,
                lambda pb, b0, b1: act(pq_b[0:ntok, b0:b1], pb[0:ntok, 0:b1 - b0], AF.Copy, [pb], [pq_b]))
        for half in range(2):
            pb = pbank[3 + half]
            for j in range(8):
                hp = half * 8 + j
                tr(bfv(pb)[:, j * 128:j * 128 + ntok], pq_b[0:ntok, hp * 128:(hp + 1) * 128], ntok, [pq_b], [pb])
            act(pqT[:, half * 8:half * 8 + 8, 0:ntok], bfv(pb)[:, :].rearrange("p (h t) -> p h t", t=128)[:, :, 0:ntok],
                AF.Copy, [pb], [pqT])
        for q4 in range(4):
            pb = pbank[3 + q4]
            for j in range(4):
                hp = q4 * 4 + j
                mm(pb[0:ntok, j * 128:(j + 1) * 128], pqT[:, hp, 0:ntok], skT[:, hp, :], True, True, [pqT, skT], [pb])
            act(sc[0:ntok, q4 * 4:q4 * 4 + 4, :], hv(pb[0:ntok, :], 128), AF.Copy, [pb], [sc])

        def top16(vals_of, src_ap, srcb, n, dstv, dsti, g):
            T.op("dve", lambda e: e.max(out=dstv[0:ntok, g, 0:8], in_=src_ap), [srcb], [dstv])
            if dsti is not None:
                T.op("dve", lambda e: e.max_index(out=dsti[0:ntok, g, 0:8], in_max=dstv[0:ntok, g, 0:8], in_values=src_ap),
                     [srcb, dstv], [dsti])
            T.op("dve", lambda e: e.match_replace(out=sc2[0:ntok, 0:n], in_to_replace=dstv[0:ntok, g, 0:8], in_values=src_ap,
                                                  imm_value=-1.0e30), [srcb, dstv], [sc2])
            T.op("dve", lambda e: e.max(out=dstv[0:ntok, g, 8:16], in_=sc2[0:ntok, 0:n]), [sc2], [dstv])
            if dsti is not None:
                T.op("dve", lambda e: e.max_index(out=dsti[0:ntok, g, 8:16], in_max=dstv[0:ntok, g, 8:16],
                                                  in_values=sc2[0:ntok, 0:n]), [sc2, dstv], [dsti])

        for g in range(16):
            top16(None, sc[0:ntok, g, :], sc, 128, sv, si, g)
        T.op("dve", lambda e: e.tensor_copy(out=sif[0:ntok], in_=si[0:ntok]), [si], [sif])
        sv4 = sv[0:ntok].rearrange("p (h t) k -> p h t k", t=2)
        sif4 = sif[0:ntok].rearrange("p (h t) k -> p h t k", t=2)
        c4 = lambda b: b[0:ntok].rearrange("p h (a b) -> p h a b", b=16)
        tt(c4(cand), sv4[:, :, 0, :].unsqueeze(3).to_broadcast([ntok, 8, 16, 16]),
           sv4[:, :, 1, :].unsqueeze(2).to_broadcast([ntok, 8, 16, 16]), ALU.add, [sv], [cand])
        ts(sif4[:, :, 0, :], sif4[:, :, 0, :], float(cfg.NKEYS), None, ALU.mult, None, [sif], [sif])
        tt(c4(cid), sif4[:, :, 0, :].unsqueeze(3).to_broadcast([ntok, 8, 16, 16]),
           sif4[:, :, 1, :].unsqueeze(2).to_broadcast([ntok, 8, 16, 16]), ALU.add, [sif], [cid])
        for hh in range(8):
            top16(None, cand[0:ntok, hh, :], cand, 256, tv, None, hh)
        for hh in range(8):
            for k in range(16):
                T.op("dve", lambda e, hh=hh, k=k: e.scalar_tensor_tensor(
                    out=sc2[0:ntok, 0:256], in0=cand[0:ntok, hh, :], scalar=tv[0:ntok, hh, k:k + 1], in1=cid[0:ntok, hh, :],
                    op0=ALU.is_equal, op1=ALU.mult, accum_out=eid[0:ntok, hh * 16 + k:hh * 16 + k + 1]),
                    [cand, tv, cid], [sc2, eid])
        ts(eid[0:ntok, :], eid[0:ntok, :], float(cfg.NKEYS * cfg.NKEYS - 1), None, ALU.min, None, [eid], [eid])
        T.op("dve", lambda e: e.tensor_copy(out=eid_i[0:ntok, :], in_=eid[0:ntok, :]), [eid], [eid_i])
        ts(pst[0:ntok, 0:8], tv[0:ntok, :, 0], -1.0, None, ALU.mult, None, [tv], [pst])
        for hh in range(8):
            act(gate[0:ntok, hh, :], tv[0:ntok, hh, :], AF.Exp, [tv, pst], [gate, pst], bias=pst[0:ntok, hh:hh + 1],
                accum_out=pst[0:ntok, 8 + hh:9 + hh])
        T.op("dve", lambda e: e.reciprocal(out=pst[0:ntok, 16:24], in_=pst[0:ntok, 8:16]), [pst], [pst])
        tt(gate[0:ntok], gate[0:ntok], pst[0:ntok, 16:24].unsqueeze(2).to_broadcast([ntok, 8, 16]), ALU.mult, [gate, pst], [gate])
        T.barrier()
        for s_ in range(128):
            gbf = gbuf[s_ % NG]
            T.op("pool", lambda e, s_=s_, gbf=gbf: e.indirect_dma_start(
                out=gbf[0:ntok, :], out_offset=None, in_=pu_b[:, :],
                in_offset=bass.IndirectOffsetOnAxis(ap=eid_i[0:ntok, s_:s_ + 1], axis=0)), [eid_i, tconv_b], [gbf], dma=f"g{s_ % NG}")
            T.op("dve", lambda e, s_=s_, gbf=gbf: e.scalar_tensor_tensor(
                out=pjunk[0:ntok, :], in0=gbf[0:ntok, :], scalar=1.0, in1=hn_b[0:ntok, :], op0=ALU.mult,
                op1=ALU.mult, accum_out=actv[0:ntok, s_:s_ + 1]), [gbf, hn_b], [pjunk, actv])
        act(actv[0:ntok, :], actv[0:ntok, :], AF.Gelu, [actv], [actv])
        tt(wgt[0:ntok, :], actv[0:ntok, :], gate[0:ntok].rearrange("p h k -> p (h k)"), ALU.mult, [actv, gate], [wgt])
        Y = [pbank[1], pbank[2]]
        for s_ in range(128):
            gbf = gbuf[s_ % NG]
            dg = dgb[s_ % 2]
            T.op("pool", lambda e, s_=s_, gbf=gbf: e.indirect_dma_start(
                out=gbf[0:ntok, :], out_offset=None, in_=pv_b[:, :],
                in_offset=bass.IndirectOffsetOnAxis(ap=eid_i[0:ntok, s_:s_ + 1], axis=0)), [eid_i, tconv_b], [gbf], dma=f"g{s_ % NG}")
            act(dg[0:ntok, 0:ntok], ident_b[0:ntok, 0:ntok], AF.Copy, [ident_b, wgt], [dg], scale=wgt[0:ntok, s_:s_ + 1])
            for hb in range(2):
                mm(Y[hb][0:ntok, :], dg[0:ntok, 0:ntok], gbf[0:ntok, hb * 512:(hb + 1) * 512], s_ == 0, s_ == 127, [dg, gbf], [Y[hb]])
        for hb in range(2):
            tt(h_f[0:ntok, hb * 512:(hb + 1) * 512], Y[hb][0:ntok, :], h_f[0:ntok, hb * 512:(hb + 1) * 512], ALU.add,
               [Y[hb], h_f], [h_f])
        dma("sp", ydst, h_f[0:ntok, :], [h_f], [], "yo")

    T.op("pool", lambda e: e.memset(tail[:], 0.0), [], [tail])
    T.op("pool", lambda e: e.memset(S_f[:], 0.0), [], [S_f])
    T.op("pool", lambda e: e.memset(S_b[:], 0.0), [], [S_b])

    def own_tail(ntq, nkt, kT_of, kT_b, v_of, v_b, bias_of, diag_kt, xsrc, ydst):
        lvl = getattr(cfg, "DBG", 9)
        if lvl == 0:
            return
        T.barrier()
        if lvl == 1:
            attention(ntq, nkt, kT_of, kT_b, v_of, v_b, bias_of, diag_kt)
            T.barrier()
            return
        attention(ntq, nkt, kT_of, kT_b, v_of, v_b, bias_of, diag_kt)
        T.barrier()
        merge_peer(ntq, xsrc, ydst)
        T.barrier()

    for p in range(NT):
        own = (p % 4 == 3)
        if p == 1:
            convert_tables()
        xsrc = xp[p * 128:(p + 1) * 128, :]
        stage_a(xsrc, 128)
        inproj(COLS["qkv"][0], COLS["qkv"][1], 128)
        inproj(4096, 4112, 128)
        inproj(COLS["fk"][0], COLS["fv"][1], 128)
        if own:
            inproj(COLS["z"][0], COLS["z"][1], 128)
            inproj(COLS["fq"][0], COLS["fq"][1], 128)
            inproj(COLS["ga"][0], COLS["gb"][1], 128)
        qknorm(kn, COLS["fk"][0], 64, 128)
        ts_ = slice(p * 128, (p + 1) * 128)
        stage_c(kn[:, :], kn, proj[:, COLS["fv"][0]:COLS["fv"][1]], proj, 128, kT_scr[:, :, ts_], kT_scr_b, v_scr[:, :, p, :], v_scr_b)
        if own:
            j = p // 4
            dma("sp", k_out[j * 128:(j + 1) * 128, :], kn[:, :], [kn], [], "ko")
            dma("sp", v_out[j * 128:(j + 1) * 128, :], proj[:, COLS["fv"][0]:COLS["fv"][1]], [proj], [], "vo")
        gdn_tile(64, 128, own)
        for hf in range(2):
            dma("pool", tail[:, hf * 1536:(hf + 1) * 1536], proj[125:128, hf * 1536:(hf + 1) * 1536], [proj] + tailpb, [tail], "tl")
        if p == NT - 1:
            dma("sp", conv_out[:, :], proj[125:128, 0:QKV_W], [proj], [], "co")
            dma("sp", s_fin.rearrange("h d e -> d h e"), S_f[:], [S_f], [], "so")
        if own:
            j = p // 4
            own_tail(128, p + 1,
                     lambda h, t0, t1: kT_scr[h, :, t0 * 128:t1 * 128], kT_scr_b,
                     lambda h, t0, t1: v_scr[h, :, t0:t1, :],
                     v_scr_b,
                     lambda kt: (kmask[:, kt:kt + 1] if kt < 3 else None), p,
                     xsrc, y_out[j * 128:(j + 1) * 128, :])

    T.op("pool", lambda e: e.memset(kT_st[:], 0.0), [], [kT_st])
    T.op("pool", lambda e: e.memset(kv_b[:], 0.0), [], [kv_b])
    dma("sp", kTs_scr[:, :, PAST:PAST + 128].rearrange("h d t -> d h t"), kT_st[:, :, :], [kT_st], [kTs_scr_b], "ks")
    dma("sp", vs_scr[:, :, NPT, :].rearrange("h p d -> p h d"), kv_b[:, :].rearrange("p (h d) -> p h d", d=128), [kv_b], [vs_scr_b], "vs")
    for sidx in range(NS):
        T.barrier()
        for kt in range(NPT):
            dma("sp", kn[:, :], ck_in[sidx, kt * 128:(kt + 1) * 128, :], [], [kn], "xk")
            dma("sp", xt[:, :], cv_in[sidx, kt * 128:(kt + 1) * 128, :], [], [xt], "x")
            ts_ = slice(kt * 128, (kt + 1) * 128)
            stage_c(kn[:, :], kn, xt[:, :], xt, 128, kTs_scr[:, :, ts_], kTs_scr_b, vs_scr[:, :, kt, :], vs_scr_b)
        xsrc = xs_in[sidx * LS:(sidx + 1) * LS, :]
        stage_a(xsrc, LS)
        inproj(0, IN_W, LS)
        qknorm(kn, COLS["fk"][0], 64, LS)
        ts_ = slice(PAST, PAST + LS)
        stage_c(kn[0:LS, :], kn, proj[0:LS, COLS["fv"][0]:COLS["fv"][1]], proj, LS, kTs_scr[:, :, ts_], kTs_scr_b,
                vs_scr[:, 0:LS, NPT, :], vs_scr_b)
        dma("sp", ks_out[sidx * LS:(sidx + 1) * LS, :], kn[0:LS, :], [kn], [], "ko")
        dma("sp", vs_out[sidx * LS:(sidx + 1) * LS, :], proj[0:LS, COLS["fv"][0]:COLS["fv"][1]], [proj], [], "vo")
        dma("sp", cs_out[sidx, :, :], proj[LS - 3:LS, 0:QKV_W], [proj], [], "co")
        for hf in range(2):
            dma("pool", tail[:, hf * 1536:(hf + 1) * 1536], cs_in[sidx, :, hf * 1536:(hf + 1) * 1536], tailpb, [tail], "tl")
        dma("sp", S_f[:], ss_in[sidx].rearrange("h d e -> d h e"), [], [S_f], "si")
        act(S_b[:], S_f[:], AF.Copy, [S_f], [S_b])
        gdn_tile(LS, LS, True)
        dma("sp", ss_out[sidx].rearrange("h d e -> d h e"), S_f[:], [S_f], [], "so")
        own_tail(LS, NPT + 1,
                 lambda h, t0, t1: kTs_scr[h, :, t0 * 128:t1 * 128], kTs_scr_b,
                 lambda h, t0, t1: vs_scr[h, :, t0:t1, :],
                 vs_scr_b,
                 lambda kt: (smask[:, 0:1] if kt == NPT else None), None,
                 xsrc, ys_out[sidx * LS:(sidx + 1) * LS, :])

    T.emit()
    st.close()
    return nc


def _gconst(C, ntok):
    bf = ml_dtypes.bfloat16
    nch = ntok // C
    shA = np.zeros((ntok, nch * 4 * C), np.float32)
    sel = np.zeros((ntok, nch * C), np.float32)
    plc = np.zeros((C, nch * ntok), np.float32)
    for c in range(nch):
        for j in range(4):
            for i in range(C):
                t = C * c + i + j - 3
                if 0 <= t < ntok:
                    shA[t, (c * 4 + j) * C + i] = 1.0
        for i in range(C):
            sel[C * c + i, c * C + i] = 1.0
            plc[i, c * ntok + C * c + i] = 1.0
    shB = np.zeros((3, 3 * C), np.float32)
    for j in range(3):
        for i in range(C):
            t = i + j
            if t < 3:
                shB[t, j * C + i] = 1.0
    ar = np.arange(C)
    tri = (ar[:, None] <= ar[None, :]).astype(np.float32)
    rep = lambda m: np.ascontiguousarray(m.astype(np.float32))
    return {f"shA{C}": shA.astype(bf), f"shB{C}": shB.astype(bf), f"sel{C}": sel, f"tri{C}": tri,
            f"msl{C}": rep((ar[:, None] > ar[None, :]).astype(np.float32)),
            f"msu{C}": rep((ar[None, :] > ar[:, None]).astype(np.float32)),
            f"mui{C}": rep((ar[None, :] >= ar[:, None]).astype(np.float32)),
            f"id8{C}": rep(np.eye(C, dtype=np.float32)), f"plc{C}": plc.astype(bf)}


def _prep(cfg, inputs):
    NT = cfg.SEQ // 128
    f = lambda k: np.asarray(inputs[k], np.float32)
    rows = np.zeros((1, 2048), np.float32)
    rows[0, 0:64] = f("diff_q_norm_g")[0]
    rows[0, 64:128] = f("diff_k_norm_g")[0]
    rows[0, 128:136] = f("delta_a_log")[0]
    rows[0, 136:144] = f("delta_dt_bias")[0]
    rows[0, 144:208] = f("diff_lambda_q1")[0]
    rows[0, 208:272] = f("diff_lambda_k1")[0]
    rows[0, 272:336] = f("diff_lambda_q2")[0]
    rows[0, 336:400] = f("diff_lambda_k2")[0]
    rows[0, 400:528] = f("delta_norm_g")[0]
    rows[0, 528:656] = f("diff_norm_g")[0]
    rows[0, 1024:2048] = f("norm_ffn_g")[0]
    common = {
        "w_in": np.ascontiguousarray(f("w_in")[0]),
        "w_out": np.ascontiguousarray(f("w_out")[0]),
        "w_pq": np.ascontiguousarray(f("peer_w_q")[0]),
        "subk": np.ascontiguousarray(f("peer_sub_keys")[0].reshape(16 * 128, 128)),
        "peer_u": np.ascontiguousarray(f("peer_u")[0]),
        "peer_v": np.ascontiguousarray(f("peer_v")[0]),
        "g_mix": np.ascontiguousarray(f("norm_mix_g")[0].reshape(cfg.D // 128, 128).T),
        "rows": rows,
        "ident": np.eye(128, dtype=np.float32),
        "conv_w": np.ascontiguousarray(f("conv_w")[0]),
        **_gconst(64, 128), **_gconst(cfg.DEC_SEQ, cfg.DEC_SEQ),
    }
    smask = np.full((128, 1), NEG, np.float32)
    smask[:cfg.DEC_SEQ] = 0.0
    common["smask"] = smask
    xpr = f("x_prompt")
    in_maps = []
    for c in range(8):
        b, r = c // 4, c % 4
        lead = 3 - r
        xpad = np.zeros((NT * 128, cfg.D), np.float32)
        nreal = NT - lead
        xpad[lead * 128:] = xpr[b, :nreal * 128]
        kmask = np.zeros((128, NT), np.float32)
        kmask[:, :lead] = NEG
        m = dict(common)
        m.update({
            "xp": xpad, "kmask": kmask,
            "xs": np.ascontiguousarray(f("x_sample")[2 * c:2 * c + 2].reshape(-1, cfg.D)),
            "ss_in": np.ascontiguousarray(f("state_delta_s")[0][2 * c:2 * c + 2]),
            "cs_in": np.ascontiguousarray(f("state_delta_conv")[0][2 * c:2 * c + 2]),
            "ck_in": np.ascontiguousarray(f("cache_diff_k")[0][2 * c:2 * c + 2].reshape(2, cfg.PAST, 1024)),
            "cv_in": np.ascontiguousarray(f("cache_diff_v")[0][2 * c:2 * c + 2].reshape(2, cfg.PAST, 1024)),
        })
        in_maps.append(m)
    return NT, in_maps


def kernel(**inputs):
    cfg = Cfg
    NT, in_maps = _prep(cfg, inputs)
    nc = build(cfg, NT)
    res = run_bass_kernel_spmd(nc, in_maps, core_ids=list(range(8))).results
    B, S, Dm, H = cfg.BATCH, cfg.SEQ, cfg.D, cfg.H
    DB, DS = cfg.DEC_BATCH, cfg.DEC_SEQ
    f32 = np.float32
    y_p = np.zeros((B, NT, 128, Dm), f32)
    y_s = np.zeros((DB, DS, Dm), f32)
    k_p = np.zeros((1, B, NT, 128, 1024), f32)
    v_p = np.zeros((1, B, NT, 128, 1024), f32)
    s_p = np.zeros((1, B, H, 128, 128), f32)
    c_p = np.zeros((1, B, 3, QKV_W), f32)
    k_s = np.zeros((1, DB, DS, 1024), f32)
    v_s = np.zeros((1, DB, DS, 1024), f32)
    s_s = np.zeros((1, DB, H, 128, 128), f32)
    c_s = np.zeros((1, DB, 3, QKV_W), f32)
    for c in range(8):
        b, r = c // 4, c % 4
        o = res[c]
        y_p[b, r::4] = o["y_own"].reshape(NT // 4, 128, Dm)
        k_p[0, b, r::4] = o["k_own"].reshape(NT // 4, 128, 1024)
        v_p[0, b, r::4] = o["v_own"].reshape(NT // 4, 128, 1024)
        if r == 3:
            c_p[0, b] = o["conv_fin"]
            s_p[0, b] = o["s_fin"]
        y_s[2 * c:2 * c + 2] = o["ys"].reshape(2, DS, Dm)
        k_s[0, 2 * c:2 * c + 2] = o["ks"].reshape(2, DS, 1024)
        v_s[0, 2 * c:2 * c + 2] = o["vs"].reshape(2, DS, 1024)
        s_s[0, 2 * c:2 * c + 2] = o["ss"]
        c_s[0, 2 * c:2 * c + 2] = o["cs"]
    return (y_p.reshape(B, S, Dm), y_s, k_p.reshape(1, B, S, H, 2, 64), v_p.reshape(1, B, S, H, 128), s_p, c_p,
            k_s.reshape(1, DB, DS, H, 2, 64), v_s.reshape(1, DB, DS, H, 128), s_s, c_s)
```

```python
import numpy as np
import ml_dtypes
import concourse.bass as bass
import concourse.mybir as mybir
from concourse.alu_op_type import AluOpType as ALU
from concourse.bass_utils import run_bass_kernel_spmd

F32 = mybir.dt.float32
BF16 = mybir.dt.bfloat16
I32 = mybir.dt.int32
U32 = mybir.dt.uint32
AF = mybir.ActivationFunctionType
AX = mybir.AxisListType


class Cfg:
    D = 1024
    BATCH = 2
    SEQ = 16384
    DEC_BATCH = 16
    DEC_SEQ = 16
    PAST = 1024
    H = 8
    NKEYS = 128
    TOPK = 16
    EPS = 1e-6
    LAM_INIT = 0.8 - 0.6 * 1.0


EPOCH = 12000
COMPUTE = ("pe", "act", "dve", "pool")


class Buf:
    def __init__(self, ap_owner, name):
        self.t = ap_owner
        self.name = name
        self.w = {}
        self.r = {}

    def __getitem__(self, idx):
        return self.t[idx]


class Op:
    __slots__ = ("eng", "fn", "deps", "idx", "dsem", "dcum", "key")


class Trk:
    def __init__(self, nc):
        self.nc = nc
        self.lists = {e: [] for e in COMPUTE + ("sp",)}
        self.dma_sems = {}
        self.nev = {}
        self.last = {}
        self.bufs = []

    def buf(self, t, name):
        b = Buf(t, name)
        self.bufs.append(b)
        return b

    def barrier(self):
        deps = list(self.last.values())
        for e in COMPUTE + ("sp",):
            self.op(e, lambda eng: eng.nop(), extra=deps)

    def op(self, eng, fn, reads=(), writes=(), dma=None, extra=()):
        o = Op()
        o.eng = eng
        o.fn = fn
        o.dsem = dma
        o.key = ("dma", dma) if dma else eng
        deps = []
        for b in reads:
            for k, w in b.w.items():
                if self._need(o, k, raw=True):
                    deps.append(w)
        for b in writes:
            for k, w in b.w.items():
                if self._need(o, k, raw=False):
                    deps.append(w)
            for k, r in b.r.items():
                if self._need(o, k, raw=False):
                    deps.append(r)
        deps.extend(extra)
        o.deps = deps
        if not extra:
            self.last[o.key] = o
        lst = self.lists[eng]
        lst.append(o)
        if not dma:
            o.idx = self.nev.get(eng, 0)
            self.nev[eng] = o.idx + 1
        if dma:
            ent = self.dma_sems.setdefault(dma, [None, 0])
            ent[1] += 16
            o.dcum = ent[1]
        for b in reads:
            b.r[o.key] = o
        for b in writes:
            b.w[o.key] = o
        return o

    def _need(self, o, k, raw):
        if o.dsem or (isinstance(k, tuple)):
            return True
        if k != o.eng:
            return True
        if o.eng == "pe":
            return False
        return True

    def emit(self):
        nc = self.nc
        import contextlib
        with contextlib.ExitStack() as st:
            sems = {}
            for e in COMPUTE + ("sp",):
                n = self.nev.get(e, 0) // EPOCH + 1
                sems[e] = [st.enter_context(nc.semaphore(f"s_{e}_{i}")) for i in range(n)]
            for name, ent in self.dma_sems.items():
                ent[0] = st.enter_context(nc.semaphore(f"d_{name}"))
            block = st.enter_context(nc.Block())

            def target(dep):
                if dep.dsem:
                    return self.dma_sems[dep.dsem][0], dep.dcum
                return sems[dep.eng][dep.idx // EPOCH], dep.idx % EPOCH + 1

            def run(ename, eng):
                waited = {}
                for o in self.lists[ename]:
                    for d in o.deps:
                        s, v = target(d)
                        kk = id(s)
                        if waited.get(kk, 0) >= v:
                            continue
                        waited[kk] = v
                        eng.wait_ge(s, v)
                    ins = o.fn(eng)
                    if o.dsem:
                        ins.then_inc(self.dma_sems[o.dsem][0], 16)
                    else:
                        ins.then_inc(sems[ename][o.idx // EPOCH], 1)
                if ename == "sp":
                    for name, ent in self.dma_sems.items():
                        eng.wait_ge(ent[0], ent[1])
                    for e2 in COMPUTE:
                        n = self.nev.get(e2, 0)
                        if n:
                            eng.wait_ge(sems[e2][(n - 1) // EPOCH], (n - 1) % EPOCH + 1)

            @block.tensor
            def _(e):
                run("pe", e)

            @block.scalar
            def _(e):
                run("act", e)

            @block.vector
            def _(e):
                run("dve", e)

            @block.gpsimd
            def _(e):
                run("pool", e)

            @block.sync
            def _(e):
                run("sp", e)


QKV_W = 3072
COLS = dict(qkv=(0, 3072), z=(3072, 4096), beta=(4096, 4104), a=(4104, 4112),
            fq=(4112, 5136), fk=(5136, 6160), fv=(6160, 7184), ga=(7184, 8208), gb=(8208, 9232))
IN_W = 9232
ARENA_F32 = 18880
KB = 16
NEG = -1.0e4


def build(cfg, NT):
    nc = bass.Bass("TRN2", target_bir_lowering=False)
    D = cfg.D
    KC = D // 128
    T = Trk(nc)
    import contextlib
    st = contextlib.ExitStack()

    def din(name, shape, dt=F32):
        return nc.dram_tensor(name, list(shape), dt, kind="ExternalInput").ap()

    def dout(name, shape, dt=F32):
        return nc.dram_tensor(name, list(shape), dt, kind="ExternalOutput").ap()

    def dscr(name, shape, dt=BF16):
        return nc.dram_tensor(name, list(shape), dt, kind="Internal").ap()

    def sb(name, shape, dt=F32):
        t = st.enter_context(nc.sbuf_tensor(name, list(shape), dt))
        return T.buf(t, name)

    def ps(name, shape, dt=F32):
        t = st.enter_context(nc.psum_tensor(name, list(shape), dt))
        return T.buf(t, name)

    NOWN = NT // 4
    NS = cfg.DEC_BATCH // 8
    LS = cfg.DEC_SEQ
    PAST = cfg.PAST
    NPT = PAST // 128
    xp = din("xp", [NT * 128, D])
    kmask_in = din("kmask", [128, NT])
    xs_in = din("xs", [NS * LS, D])
    w_in = din("w_in", [D, IN_W])
    w_out = din("w_out", [D, D])
    w_pq = din("w_pq", [D, 2048])
    subk = din("subk", [16 * 128, 128])
    peer_u = din("peer_u", [cfg.NKEYS * cfg.NKEYS, D])
    peer_v = din("peer_v", [cfg.NKEYS * cfg.NKEYS, D])
    g_mix = din("g_mix", [128, KC])
    rows_in = din("rows", [1, 2048])
    ident_in = din("ident", [128, 128])
    conv_w = din("conv_w", [4, QKV_W])
    ss_in = din("ss_in", [NS, 8, 128, 128])
    cs_in = din("cs_in", [NS, 3, QKV_W])
    ck_in = din("ck_in", [NS, PAST, 1024])
    cv_in = din("cv_in", [NS, PAST, 1024])
    smask_in = din("smask", [128, 1])
    y_out = dout("y_own", [NOWN * 128, D])
    k_out = dout("k_own", [NOWN * 128, 1024])
    v_out = dout("v_own", [NOWN * 128, 1024])
    conv_out = dout("conv_fin", [3, QKV_W])
    s_fin = dout("s_fin", [8, 128, 128])
    ys_out = dout("ys", [NS * LS, D])
    ks_out = dout("ks", [NS * LS, 1024])
    vs_out = dout("vs", [NS * LS, 1024])
    cs_out = dout("cs", [NS, 3, QKV_W])
    ss_out = dout("ss", [NS, 8, 128, 128])
    gc = {}
    for C_ in (64, LS):
        ntk = 128 if C_ == 64 else LS
        nch = ntk // C_
        gc[C_] = dict(
            shA=din(f"shA{C_}", [ntk, nch * 4 * C_], BF16), shB=din(f"shB{C_}", [3, 3 * C_], BF16),
            sel=din(f"sel{C_}", [ntk, nch * C_]), tri=din(f"tri{C_}", [C_, C_]),
            msl=din(f"msl{C_}", [C_, C_]),
            msu=din(f"msu{C_}", [C_, C_]), mui=din(f"mui{C_}", [C_, C_]),
            id8=din(f"id8{C_}", [C_, C_]), plc=din(f"plc{C_}", [C_, nch * ntk], BF16))
    kT_scr = dscr("kT_scr", [8, 128, NT * 128])
    v_scr = dscr("v_scr", [8, 128, NT, 128])
    kTs_scr = dscr("kTs_scr", [8, 128, PAST + 128])
    vs_scr = dscr("vs_scr", [8, 128, NPT + 1, 128])
    w_in_b = dscr("w_in_b", [D, IN_W])
    w_out_b = dscr("w_out_b", [D, D])
    w_pq_b = dscr("w_pq_b", [D, 2048])
    pu_b = dscr("pu_b", [cfg.NKEYS * cfg.NKEYS, D])
    pv_b = dscr("pv_b", [cfg.NKEYS * cfg.NKEYS, D])
    wconv_b = T.buf(None, "wconv")
    tconv_b = T.buf(None, "tconv")
    kT_scr_b = T.buf(None, "kT_scr")
    v_scr_b = T.buf(None, "v_scr")
    kTs_scr_b = T.buf(None, "kTs_scr")
    vs_scr_b = T.buf(None, "vs_scr")

    ident_f = sb("ident_f", [128, 128])
    ident_b = sb("ident_b", [128, 128], BF16)
    ones_f = sb("ones_f", [128, 128])
    gmix_t = sb("gmix_t", [128, KC])
    rows = sb("rows_t", [128, 2048])
    qg_t = rows
    negA = sb("negA", [128, 8])
    lam_t = sb("lam_t", [128, 8])
    kmask = sb("kmask_t", [128, NT])
    smask = sb("smask_t", [128, 1])
    xt = sb("xt", [128, D])
    xs_bf = sb("xs_bf", [128, D], BF16)
    junk = xs_bf
    xnT = sb("xnT", [128, KC, 128], BF16)
    stat = sb("stat", [128, 8])
    wbuf = [sb(f"wbuf{i}", [128, KC, 512], BF16) for i in range(2)]
    PO = QKV_W
    proj = sb("proj", [128, IN_W - PO])
    qkv_b = [sb(f"qkv_b{i}", [128, QKV_W], BF16) for i in range(2)]
    ba_raw = [sb(f"ba_raw{i}", [128, 16]) for i in range(2)]
    cur = {"i": 0}
    kn = sb("kn", [128, 1024])
    sq = xt
    kst = sb("kst", [128, 16])
    wrows = sb("wrows", [128, 4, QKV_W], BF16)
    skT = sb("skT", [128, 16, 128], BF16)
    gcs = {}
    for C_ in (64, LS):
        ntk = 128 if C_ == 64 else LS
        nch = ntk // C_
        gcs[C_] = dict(shA=sb(f"t_shA{C_}", [ntk, nch * 4 * C_], BF16), shB=sb(f"t_shB{C_}", [3, 3 * C_], BF16),
                       sel=sb(f"t_sel{C_}", [ntk, nch * C_]), tri=sb(f"t_tri{C_}", [C_, C_]),
                       msl=sb(f"t_msl{C_}", [C_, C_]),
                       msu=sb(f"t_msu{C_}", [C_, C_]), mui=sb(f"t_mui{C_}", [C_, C_]),
                       id8=sb(f"t_id8{C_}", [C_, C_]), plc=sb(f"t_plc{C_}", [C_, nch * ntk], BF16))
    prodb = [sb("prodb0", [128, 4, 512], BF16)] * 2
    tail = sb("tail", [3, QKV_W], BF16)
    tailpb = [sb("tailpb0", [3, 3, 512], BF16)] * 2
    S_f = sb("S_f", [128, 8, 128])
    S_b = sb("S_b", [128, 8, 128], BF16)
    o_a = sb("o_a", [128, 8, 128])
    kv_b = sb("kv_b", [128, 1024], BF16)
    kT_st = sb("kT_st", [128, 8, 128], BF16)
    arena = st.enter_context(nc.sbuf_tensor("arena", [128, ARENA_F32], F32))
    pbank = [ps(f"pb{i}", [128, 512]) for i in range(8)]

    def bfv(pb):
        return pb.t[:].bitcast(BF16)

    class Phase:
        def __init__(self):
            self.off = 0

        def a(self, name, parts, shape, dt=F32):
            n = 1
            for x in shape:
                n *= x
            words = (n * (2 if dt == BF16 else 4) + 3) // 4
            words = (words + 7) // 8 * 8
            assert self.off + words <= ARENA_F32, (name, self.off, words)
            v = arena[0:parts, self.off:self.off + words]
            self.offs = getattr(self, "offs", {})
            self.offs[name] = self.off
            self.off += words
            if dt != F32:
                v = v.bitcast(dt)
            v = v[:, 0:n]
            if len(shape) == 2:
                v = v.rearrange("p (a b) -> p a b", b=shape[1])
            elif len(shape) == 3:
                v = v.rearrange("p (a b c) -> p a b c", b=shape[1], c=shape[2])
            return T.buf(v, name)

    def dma(eng, out, in_, reads, writes, sem):
        return T.op(eng, lambda e: e.dma_start(out=out, in_=in_), reads=reads, writes=writes, dma=sem)

    def act(out, in_, func, reads, writes, **kw):
        return T.op("act", lambda e: e.activation(out=out, in_=in_, func=func, **kw), reads, writes)

    def tt(out, in0, in1, op, reads, writes, eng="dve"):
        return T.op(eng, lambda e: e.tensor_tensor(out=out, in0=in0, in1=in1, op=op), reads, writes)

    def ts(out, in0, s1, s2, op0, op1, reads, writes):
        if op1 is None:
            return T.op("dve", lambda e: e.tensor_scalar(out=out, in0=in0, scalar1=s1, scalar2=None, op0=op0), reads, writes)
        return T.op("dve", lambda e: e.tensor_scalar(out=out, in0=in0, scalar1=s1, scalar2=s2, op0=op0, op1=op1),
                    reads, writes)

    def mm(out, lhsT, rhs, start, stop, reads, writes):
        return T.op("pe", lambda e: e.matmul(out, lhsT=lhsT, rhs=rhs, start=start, stop=stop), reads, writes)

    def tr(out, in_, n, reads, writes):
        return T.op("pe", lambda e: e.transpose(out=out, in_=in_, identity=ident_b[0:n, 0:n]), list(reads) + [ident_b], writes)

    dma("sp", ident_f[:], ident_in[:, :], [], [ident_f], "c0")
    dma("sp", gmix_t[:], g_mix[:, :], [], [gmix_t], "c1")
    dma("sp", rows[:], rows_in[0:1, :].to_broadcast([128, 2048]), [], [rows], "c2")
    dma("sp", kmask[:], kmask_in[:, :], [], [kmask], "c3")
    dma("sp", smask[:], smask_in[:, :], [], [smask], "c3b")
    T.op("dve", lambda e: e.tensor_copy(out=ident_b[:], in_=ident_f[:]), [ident_f], [ident_b])
    for j in range(4):
        for hf in range(2):
            dma("pool", wrows[:, j, hf * 1536:(hf + 1) * 1536],
                conv_w[j:j + 1, hf * 1536:(hf + 1) * 1536].to_broadcast([128, 1536]), [], [wrows], "c4")
    ci = 7
    for C_ in gcs:
        for nm in gcs[C_]:
            dst = gcs[C_][nm]
            src = gc[C_][nm]
            dma("sp", dst[:], src[:, :], [], [dst], f"c{ci}")
            ci += 1
    act(negA[:], rows[:, 128:136], AF.Exp, [rows], [negA])
    ts(negA[:], negA[:], -1.0, None, ALU.mult, None, [negA], [negA])
    T.op("pool", lambda e: e.memset(ones_f[:], 1.0), [], [ones_f])
    tt(xs_bf[:, 0:64], rows[:, 144:208], rows[:, 208:272], ALU.mult, [rows], [xs_bf])
    T.op("dve", lambda e: e.tensor_reduce(out=lam_t[:, 0:1], in_=xs_bf[:, 0:64], axis=AX.X, op=ALU.add), [xs_bf], [lam_t])
    tt(xs_bf[:, 64:128], rows[:, 272:336], rows[:, 336:400], ALU.mult, [rows], [xs_bf])
    T.op("dve", lambda e: e.tensor_reduce(out=lam_t[:, 1:2], in_=xs_bf[:, 64:128], axis=AX.X, op=ALU.add), [xs_bf], [lam_t])
    act(lam_t[:, 0:2], lam_t[:, 0:2], AF.Exp, [lam_t], [lam_t])
    tt(lam_t[:, 2:3], lam_t[:, 0:1], lam_t[:, 1:2], ALU.subtract, [lam_t], [lam_t])
    ts(lam_t[:, 2:3], lam_t[:, 2:3], cfg.LAM_INIT, None, ALU.add, None, [lam_t], [lam_t])
    ts(lam_t[:, 3:4], lam_t[:, 2:3], -1.0, None, ALU.mult, None, [lam_t], [lam_t])
    for hp in range(16):
        dma("sp", xt[:, 0:128], subk[hp * 128:(hp + 1) * 128, :], [], [xt], "x")
        act(xs_bf[:, 0:128], xt[:, 0:128], AF.Copy, [xt], [xs_bf])
        tr(bfv(pbank[0])[:, 0:128], xs_bf[:, 0:128], 128, [xs_bf], [pbank[0]])
        act(skT[:, hp, :], bfv(pbank[0])[:, 0:128], AF.Copy, [pbank[0]], [skT])

    for (src, dst, ncol) in ((w_in, w_in_b, IN_W), (w_out, w_out_b, D), (w_pq, w_pq_b, 2048)):
        for c0 in range(0, ncol, 2048):
            c1 = min(c0 + 2048, ncol)
            dma("pool", dst[:, c0:c1], src[:, c0:c1], [], [wconv_b], "cvw")
    WSRC = {id(w_in): w_in_b, id(w_out): w_out_b, id(w_pq): w_pq_b}

    def convert_tables():
        for (src, dst) in ((peer_u, pu_b), (peer_v, pv_b)):
            for r0 in range(0, cfg.NKEYS * cfg.NKEYS, 2048):
                dma("pool", dst[r0:r0 + 2048, :], src[r0:r0 + 2048, :], [], [tconv_b], "cvt")

    wcount = {"n": 0}

    def load_w(src, c0, c1):
        i = wcount["n"] % 2
        wcount["n"] += 1
        wb = wbuf[i]
        n = c1 - c0
        srcb = WSRC[id(src)]
        T.op("pool", lambda e: e.dma_start(out=wb[:, :, 0:n],
                                          in_=srcb[:, c0:c1].rearrange("(kc p) n -> p kc n", p=128)),
             reads=[wconv_b], writes=[wb], dma=f"w{i}")
        return wb

    def rstd_of(col_ss, col_out, ntok, n):
        ts(stat[0:ntok, 6:7], stat[0:ntok, col_ss:col_ss + 1], 1.0 / n, cfg.EPS, ALU.mult, ALU.add, [stat], [stat])
        act(stat[0:ntok, 7:8], stat[0:ntok, 6:7], AF.Sqrt, [stat], [stat])
        T.op("dve", lambda e: e.reciprocal(out=stat[0:ntok, col_out:col_out + 1], in_=stat[0:ntok, 7:8]), [stat], [stat])

    def norm_T(src, srcb, dstT, ntok, gcol):
        act(junk[0:ntok, :], src, AF.Square, [srcb], [junk, stat], accum_out=stat[0:ntok, 0:1])
        rstd_of(0, 3, ntok, D)
        act(xs_bf[0:ntok, :], src, AF.Copy, [srcb, stat], [xs_bf], scale=stat[0:ntok, 3:4])
        pb = pbank[0]
        pbv = bfv(pb)
        for kc in range(KC):
            tr(pbv[:, kc * 128:kc * 128 + ntok], xs_bf[0:ntok, kc * 128:(kc + 1) * 128], ntok, [xs_bf], [pb])
        for kc in range(KC):
            if gcol is not None:
                ts(dstT[:, kc, 0:ntok], pbv[:, kc * 128:kc * 128 + ntok], gcol[:, kc:kc + 1], None, ALU.mult, None,
                   [pb, gmix_t], [dstT])
            else:
                act(dstT[:, kc, 0:ntok], pbv[:, kc * 128:kc * 128 + ntok], AF.Copy, [pb], [dstT])

    def stage_a(xsrc, ntok):
        dma("sp", xt[0:ntok, :], xsrc, [], [xt], "x")
        norm_T(xt[0:ntok, :], xt, xnT, ntok, gmix_t)

    pcount = {"n": 0}

    def proj_block(src, lhsT_buf, b0, b1, ntok, sink):
        n = b1 - b0
        wb = load_w(src, b0, b1)
        pb = pbank[1 + pcount["n"] % 2]
        pcount["n"] += 1
        for kc in range(KC):
            mm(pb[0:ntok, 0:n], lhsT_buf[:, kc, 0:ntok], wb[:, kc, 0:n], kc == 0, kc == KC - 1, [lhsT_buf, wb], [pb])
        sink(pb, b0, b1)

    def project(src, lhsT_buf, c0, c1, ntok, sink):
        for b0 in range(c0, c1, 512):
            proj_block(src, lhsT_buf, b0, min(b0 + 512, c1), ntok, sink)

    def inproj_sink(ntok, qi):
        def sink(pb, b0, b1):
            if b1 <= PO:
                act(qkv_b[qi][0:ntok, b0:b1], pb[0:ntok, 0:b1 - b0], AF.Copy, [pb], [qkv_b[qi]])
            else:
                act(proj[0:ntok, b0 - PO:b1 - PO], pb[0:ntok, 0:b1 - b0], AF.Copy, [pb], [proj])
        return sink

    def inproj_blocks(ranges, ntok, qi):
        out = []
        for (c0, c1) in ranges:
            for b0 in range(c0, c1, 512):
                out.append(lambda b0=b0, b1=min(b0 + 512, c1): proj_block(w_in, xnT, b0, b1, ntok, inproj_sink(ntok, qi)))
        return out

    def ba_copy(ntok, qi):
        T.op("dve", lambda e: e.tensor_copy(out=ba_raw[qi][0:ntok, :], in_=proj[0:ntok, 4096 - PO:4112 - PO]), [proj], [ba_raw[qi]])

    def qknorm(dst, c0, g0, ntok):
        v3 = lambda ap: ap.rearrange("p (g d) -> p g d", d=64)
        act(sq[0:ntok, :], proj[0:ntok, c0 - PO:c0 - PO + 1024], AF.Square, [proj], [sq])
        T.op("dve", lambda e: e.tensor_reduce(out=kst[0:ntok, :], in_=v3(sq[0:ntok, :]), axis=AX.X, op=ALU.add), [sq], [kst])
        ts(kst[0:ntok, :], kst[0:ntok, :], 1.0 / 64, cfg.EPS, ALU.mult, ALU.add, [kst], [kst])
        act(kst[0:ntok, :], kst[0:ntok, :], AF.Sqrt, [kst], [kst])
        T.op("dve", lambda e: e.reciprocal(out=kst[0:ntok, :], in_=kst[0:ntok, :]), [kst], [kst])
        tt(v3(dst[0:ntok, :]), v3(proj[0:ntok, c0 - PO:c0 - PO + 1024]), kst[0:ntok, :].unsqueeze(2).to_broadcast([ntok, 16, 64]),
           ALU.mult, [proj, kst], [dst])
        tt(v3(dst[0:ntok, :]), v3(dst[0:ntok, :]), rows[0:ntok, g0:g0 + 64].unsqueeze(1).to_broadcast([ntok, 16, 64]),
           ALU.mult, [dst, rows], [dst])

    def stage_c(k_ap, k_b, v_ap, v_b, ntok, kT_dst, kT_dst_b, v_dst, v_dst_b):
        if getattr(cfg, "DBG_NOC", 0):
            return
        act(kv_b[0:ntok, :], k_ap, AF.Copy, [k_b], [kv_b])
        pb = pbank[0]
        for h in range(8):
            tr(bfv(pb)[:, h * 128:h * 128 + ntok], kv_b[0:ntok, h * 128:(h + 1) * 128], ntok, [kv_b], [pb])
        act(kT_st[:, :, 0:ntok], bfv(pb)[:, :].rearrange("p (h t) -> p h t", t=128)[:, :, 0:ntok], AF.Copy, [pb], [kT_st])
        dma("sp", kT_dst.rearrange("h d t -> d h t"), kT_st[:, :, 0:ntok], [kT_st], [kT_dst_b], "ks")
        act(kv_b[0:ntok, :], v_ap, AF.Copy, [v_b, kT_st], [kv_b])
        dma("sp", v_dst.rearrange("h p d -> p h d"), kv_b[0:ntok, :].rearrange("p (h d) -> p h d", d=128), [kv_b], [v_dst_b], "vs")

    gb = {"n": 0}

    def gbank():
        b = pbank[3 + gb["n"] % 5]
        gb["n"] += 1
        return b

    G_ph = Phase()
    qkvc = G_ph.a("qkvc", 64, [QKV_W])
    ba = G_ph.a("ba", 64, [16])
    gst = G_ph.a("gst", 64, [96])
    eGlB = G_ph.a("eGlB", 128, [8])
    knf = G_ph.a("knf", 64, [8, 128])
    kn_b = G_ph.a("kn_b", 64, [8, 128], BF16)
    kb_b = G_ph.a("kb_b", 64, [8, 128], BF16)
    kd_b = G_ph.a("kd_b", 64, [8, 128], BF16)
    qn_b = G_ph.a("qn_b", 64, [8, 128], BF16)
    qg_b = G_ph.a("qg_b", 64, [8, 128], BF16)
    vb_f = G_ph.a("vb_f", 64, [8, 128])
    knT = G_ph.a("knT", 128, [8, 64], BF16)
    kbT = G_ph.a("kbT", 128, [8, 64], BF16)
    qnT = G_ph.a("qnT", 128, [8, 64], BF16)
    qgT = G_ph.a("qgT", 128, [8, 64], BF16)
    gtri = G_ph.a("gtri", 64, [8, 64])
    dif = G_ph.a("dif", 64, [8, 64])
    earg = G_ph.a("earg", 64, [8, 64])
    Dsl = G_ph.a("Dsl", 64, [8, 64])
    DTsu = G_ph.a("DTsu", 64, [8, 64])
    DTui = G_ph.a("DTui", 64, [8, 64])
    intraT = G_ph.a("intraT", 64, [8, 64], BF16)
    Nb = [G_ph.a(f"Nb{i}", 64, [8, 64], BF16) for i in range(2)]
    Mb = [G_ph.a(f"Mb{i}", 64, [8, 64], BF16) for i in range(2)]
    Pf = G_ph.a("Pf", 64, [8, 64])
    Qf = G_ph.a("Qf", 64, [8, 64])
    Pb = [G_ph.a(f"Pb{i}", 64, [8, 64], BF16) for i in range(2)]
    Qb = [G_ph.a(f"Qb{i}", 64, [8, 64], BF16) for i in range(2)]
    tmpks = G_ph.a("tmpks", 64, [8, 128])
    rhs2 = G_ph.a("rhs2", 64, [8, 128], BF16)
    vnew_b = G_ph.a("vnew_b", 64, [8, 128], BF16)
    o_ch = [G_ph.a(f"o_ch{i}", 64, [8, 128], BF16) for i in range(2)]

    def bc8(buf, lo, C, n):
        return buf[0:C, lo:lo + 8].unsqueeze(2).to_broadcast([C, 8, n])

    def hv(ap, d):
        return ap.rearrange("p (h d) -> p h d", d=d)

    def l2norm_cols(c0, col, C, scale):
        act(tmpks[0:C, :, :].rearrange("p h d -> p (h d)"), qkvc[0:C, c0:c0 + 1024], AF.Square, [qkvc], [tmpks])
        T.op("dve", lambda e: e.tensor_reduce(out=gst[0:C, col:col + 8], in_=tmpks[0:C, :, :], axis=AX.X, op=ALU.add),
             [tmpks], [gst])
        ts(gst[0:C, col:col + 8], gst[0:C, col:col + 8], cfg.EPS, None, ALU.add, None, [gst], [gst])
        act(gst[0:C, col:col + 8], gst[0:C, col:col + 8], AF.Sqrt, [gst], [gst])
        T.op("dve", lambda e: e.reciprocal(out=gst[0:C, col:col + 8], in_=gst[0:C, col:col + 8]), [gst], [gst])
        if scale != 1.0:
            ts(gst[0:C, col:col + 8], gst[0:C, col:col + 8], scale, None, ALU.mult, None, [gst], [gst])

    def gdn_chunk(C, ntok, c, own, qi, inj):
        G = gcs[C]

        def hook():
            if inj:
                inj.pop(0)()
        nd = {64: 5, 16: 3}[C]
        m3 = lambda ap: ap.rearrange("p (h c) -> p h c", c=C)
        g3 = lambda nm: G[nm][0:C, 0:C].unsqueeze(1).to_broadcast([C, 8, C])
        for cb in (range(0, 6) if own else range(2, 6)):
            pi = cb % 2
            cs_ = slice(cb * 512, (cb + 1) * 512)
            tt(prodb[pi][0:ntok], qkv_b[qi][0:ntok, cs_].unsqueeze(1).to_broadcast([ntok, 4, 512]), wrows[0:ntok, :, cs_],
               ALU.mult, [qkv_b[qi], wrows], [prodb[pi]])
            lst = [(G["shA"][0:ntok, (c * 4 + j) * C:(c * 4 + j + 1) * C], prodb[pi][0:ntok, j, :], [G["shA"], prodb[pi]])
                   for j in range(4)]
            if c == 0:
                tt(tailpb[pi][:], tail[:, cs_].unsqueeze(1).to_broadcast([3, 3, 512]), wrows[0:3, 0:3, cs_], ALU.mult,
                   [tail, wrows], [tailpb[pi]])
                lst += [(G["shB"][0:3, j * C:(j + 1) * C], tailpb[pi][0:3, j, :], [G["shB"], tailpb[pi]]) for j in range(3)]
            pb = gbank()
            for i, (l, r, rd) in enumerate(lst):
                mm(pb[0:C, :], l, r, i == 0, i == len(lst) - 1, rd, [pb])
            act(qkvc[0:C, cs_], pb[0:C, :], AF.Silu, [pb], [qkvc])
            hook()
        pb = gbank()
        mm(pb[0:C, 0:16], G["sel"][0:ntok, c * C:(c + 1) * C], ba_raw[qi][0:ntok, :], True, True, [G["sel"], ba_raw[qi]], [pb])
        T.op("dve", lambda e, pb=pb: e.tensor_copy(out=ba[0:C, :], in_=pb[0:C, 0:16]), [pb], [ba])
        act(gst[0:C, 0:8], ba[0:C, 0:8], AF.Sigmoid, [ba], [gst])
        tt(gst[0:C, 8:16], ba[0:C, 8:16], rows[0:C, 136:144], ALU.add, [ba, rows], [gst])
        act(gst[0:C, 8:16], gst[0:C, 8:16], AF.Exp, [gst], [gst])
        act(gst[0:C, 8:16], gst[0:C, 8:16], AF.Ln, [gst], [gst], bias=1.0)
        tt(gst[0:C, 8:16], gst[0:C, 8:16], negA[0:C, :], ALU.mult, [gst, negA], [gst])
        pb = gbank()
        mm(pb[0:C, 0:8], G["tri"][0:C, 0:C], gst[0:C, 8:16], True, True, [G["tri"], gst], [pb])
        mm(pb[0:C, 8:16], ones_f[0:C, 0:C], gst[0:C, 8:16], True, True, [ones_f, gst], [pb])
        mm(pb[0:128, 16:24], ones_f[0:C, 0:128], gst[0:C, 8:16], True, True, [ones_f, gst], [pb])
        T.op("dve", lambda e, pb=pb: e.tensor_copy(out=gst[0:C, 16:32], in_=pb[0:C, 0:16]), [pb], [gst])
        act(eGlB[:, :], pb[0:128, 16:24], AF.Exp, [pb], [eGlB])
        act(gst[0:C, 32:40], gst[0:C, 16:24], AF.Exp, [gst], [gst])
        tt(gst[0:C, 40:48], gst[0:C, 32:40], gst[0:C, 0:8], ALU.mult, [gst], [gst])
        tt(gst[0:C, 48:56], gst[0:C, 24:32], gst[0:C, 16:24], ALU.subtract, [gst], [gst])
        act(gst[0:C, 48:56], gst[0:C, 48:56], AF.Exp, [gst], [gst])
        hook()
        l2norm_cols(1024, 56, C, 1.0)
        kview = hv(qkvc[0:C, 1024:2048], 128)
        vview = hv(qkvc[0:C, 2048:3072], 128)
        tt(knf[0:C], kview, bc8(gst, 56, C, 128), ALU.mult, [qkvc, gst], [knf])
        act(kn_b[0:C], knf[0:C], AF.Copy, [knf], [kn_b])
        tt(kb_b[0:C], knf[0:C], bc8(gst, 0, C, 128), ALU.mult, [knf, gst], [kb_b])
        tt(kd_b[0:C], knf[0:C], bc8(gst, 48, C, 128), ALU.mult, [knf, gst], [kd_b])
        tt(vb_f[0:C], vview, bc8(gst, 0, C, 128), ALU.mult, [qkvc, gst], [vb_f])
        pairs = [(kn_b, knT), (kb_b, kbT)]
        if own:
            l2norm_cols(0, 64, C, 128.0 ** -0.5)
            qview = hv(qkvc[0:C, 0:1024], 128)
            tt(knf[0:C], qview, bc8(gst, 64, C, 128), ALU.mult, [qkvc, gst, kn_b, kb_b, kd_b], [knf])
            act(qn_b[0:C], knf[0:C], AF.Copy, [knf], [qn_b])
            tt(qg_b[0:C], knf[0:C], bc8(gst, 32, C, 128), ALU.mult, [knf, gst], [qg_b])
            pairs += [(qn_b, qnT), (qg_b, qgT)]
        for src, dstT in pairs:
            pb = gbank()
            pbv = bfv(pb)
            for h in range(8):
                tr(pbv[:, h * C:(h + 1) * C], src[0:C, h, :], C, [src], [pb])
            act(dstT[:, :, 0:C], m3(pbv[:, 0:8 * C]), AF.Copy, [pb], [dstT])
        hook()
        tt(gtri[0:C, :, 0:C], g3("tri"), bc8(gst, 8, C, C), ALU.mult, [G["tri"], gst], [gtri])
        pbG = gbank()
        for h in range(8):
            mm(pbG[0:C, h * C:(h + 1) * C], ones_f[0:C, 0:C], gtri[0:C, h, 0:C], True, True, [ones_f, gtri], [pbG])
        tt(dif[0:C, :, 0:C], m3(pbG[0:C, 0:8 * C]), bc8(gst, 16, C, C), ALU.subtract, [pbG, gst], [dif])
        ts(earg[0:C, :, 0:C], dif[0:C, :, 0:C], 0.0, -1.0, ALU.max, ALU.mult, [dif], [earg])
        act(earg[0:C, :, 0:C], earg[0:C, :, 0:C], AF.Exp, [earg], [earg])
        tt(Dsl[0:C, :, 0:C], earg[0:C, :, 0:C], g3("msl"), ALU.mult, [earg, G["msl"]], [Dsl])
        ts(earg[0:C, :, 0:C], dif[0:C, :, 0:C], 0.0, None, ALU.min, None, [dif, Dsl], [earg])
        act(earg[0:C, :, 0:C], earg[0:C, :, 0:C], AF.Exp, [earg], [earg])
        tt(DTsu[0:C, :, 0:C], earg[0:C, :, 0:C], g3("msu"), ALU.mult, [earg, G["msu"]], [DTsu])
        if own:
            tt(DTui[0:C, :, 0:C], earg[0:C, :, 0:C], g3("mui"), ALU.mult, [earg, G["mui"]], [DTui])
            pb = gbank()
            for h in range(8):
                mm(pb[0:C, h * C:(h + 1) * C], knT[:, h, 0:C], qnT[:, h, 0:C], True, True, [knT, qnT], [pb])
            tt(intraT[0:C, :, 0:C], m3(pb[0:C, 0:8 * C]), DTui[0:C, :, 0:C], ALU.mult, [pb, DTui], [intraT])
        hook()
        for (la, ra, Dm, outb, accf, accb) in ((kbT, knT, Dsl, Nb[0], Qf, Qb[0]), (knT, kbT, DTsu, Mb[0], Pf, Pb[0])):
            pb = gbank()
            for h in range(8):
                mm(pb[0:C, h * C:(h + 1) * C], la[:, h, 0:C], ra[:, h, 0:C], True, True, [la, ra], [pb])
            T.op("dve", lambda e, pb=pb, Dm=Dm, outb=outb: e.scalar_tensor_tensor(
                out=outb[0:C, :, 0:C], in0=m3(pb[0:C, 0:8 * C]), scalar=-1.0, in1=Dm[0:C, :, 0:C], op0=ALU.mult,
                op1=ALU.mult), [pb, Dm], [outb])
            tt(accf[0:C, :, 0:C], outb[0:C, :, 0:C], g3("id8"), ALU.add, [outb, G["id8"]], [accf])
            act(accb[0:C, :, 0:C], accf[0:C, :, 0:C], AF.Copy, [accf], [accb])
        hook()
        cur = 0
        for k in range(1, nd + 1):
            nxt = 1 - cur
            last = (k == nd)
            pbN, pbM = gbank(), gbank()
            for h in range(8):
                if not last:
                    mm(pbN[0:C, h * C:(h + 1) * C], Mb[cur][0:C, h, 0:C], Nb[cur][0:C, h, 0:C], True, True,
                       [Mb[cur], Nb[cur]], [pbN])
                mm(pbM[0:C, h * C:(h + 1) * C], Nb[cur][0:C, h, 0:C], Mb[cur][0:C, h, 0:C], True, True,
                   [Mb[cur], Nb[cur]], [pbM])
            if not last:
                act(Nb[nxt][0:C, :, 0:C], m3(pbN[0:C, 0:8 * C]), AF.Copy, [pbN], [Nb[nxt]])
            T.op("dve", lambda e, nxt=nxt, pbM=pbM: e.tensor_copy(out=Mb[nxt][0:C, :, 0:C], in_=m3(pbM[0:C, 0:8 * C])),
                 [pbM], [Mb[nxt]])
            pbP, pbQ = gbank(), gbank()
            for h in range(8):
                mm(pbP[0:C, h * C:(h + 1) * C], Qb[cur][0:C, h, 0:C], Mb[nxt][0:C, h, 0:C], True, True,
                   [Qb[cur], Mb[nxt]], [pbP])
                if not last:
                    mm(pbQ[0:C, h * C:(h + 1) * C], Pb[cur][0:C, h, 0:C], Nb[nxt][0:C, h, 0:C], True, True,
                       [Pb[cur], Nb[nxt]], [pbQ])
            tt(Pf[0:C, :, 0:C], m3(pbP[0:C, 0:8 * C]), Pf[0:C, :, 0:C], ALU.add, [pbP, Pf], [Pf])
            act(Pb[nxt][0:C, :, 0:C], Pf[0:C, :, 0:C], AF.Copy, [Pf], [Pb[nxt]])
            if not last:
                tt(Qf[0:C, :, 0:C], m3(pbQ[0:C, 0:8 * C]), Qf[0:C, :, 0:C], ALU.add, [pbQ, Qf], [Qf])
                act(Qb[nxt][0:C, :, 0:C], Qf[0:C, :, 0:C], AF.Copy, [Qf], [Qb[nxt]])
            cur = nxt
            hook()
        PT = Pb[cur]
        for half in range(2):
            pb = gbank()
            for hh in range(4):
                h = half * 4 + hh
                mm(pb[0:C, hh * 128:(hh + 1) * 128], knT[:, h, 0:C], S_b[:, h, :], True, True, [knT, S_b], [pb])
            hs = slice(half * 4, half * 4 + 4)
            tt(tmpks[0:C, hs, :], hv(pb[0:C, :], 128),
               gst[0:C, 40 + half * 4:44 + half * 4].unsqueeze(2).to_broadcast([C, 4, 128]), ALU.mult, [pb, gst], [tmpks])
        tt(rhs2[0:C], vb_f[0:C], tmpks[0:C], ALU.subtract, [vb_f, tmpks], [rhs2])
        for half in range(2):
            pb = gbank()
            for hh in range(4):
                h = half * 4 + hh
                mm(pb[0:C, hh * 128:(hh + 1) * 128], PT[0:C, h, 0:C], rhs2[0:C, h, :], True, True, [PT, rhs2], [pb])
            hs = slice(half * 4, half * 4 + 4)
            act(vnew_b[0:C, hs, :], hv(pb[0:C, :], 128), AF.Copy, [pb], [vnew_b])
        hook()
        if own:
            for half in range(2):
                pb = gbank()
                for hh in range(4):
                    h = half * 4 + hh
                    mm(pb[0:C, hh * 128:(hh + 1) * 128], qgT[:, h, 0:C], S_b[:, h, :], True, False, [qgT, S_b], [pb])
                    mm(pb[0:C, hh * 128:(hh + 1) * 128], intraT[0:C, h, 0:C], vnew_b[0:C, h, :], False, True,
                       [intraT, vnew_b], [pb])
                hs = slice(half * 4, half * 4 + 4)
                act(o_ch[c][0:C, hs, :], hv(pb[0:C, :], 128), AF.Copy, [pb], [o_ch[c]])
        tt(S_f[:], S_f[:], eGlB[:, :].unsqueeze(2).to_broadcast([128, 8, 128]), ALU.mult, [S_f, eGlB], [S_f])
        for half in range(2):
            pb = gbank()
            for hh in range(4):
                h = half * 4 + hh
                mm(pb[:, hh * 128:(hh + 1) * 128], kd_b[0:C, h, :], vnew_b[0:C, h, :], True, True, [kd_b, vnew_b], [pb])
            hs = slice(half * 4, half * 4 + 4)
            tt(S_f[:, hs, :], hv(pb[:, :], 128), S_f[:, hs, :], ALU.add, [pb, S_f], [S_f])
        act(S_b[:], S_f[:], AF.Copy, [S_f], [S_b])
        hook()

    def gdn_tile(C, ntok, own, qi, inj):
        nch = ntok // C
        for c in range(nch):
            gdn_chunk(C, ntok, c, own, qi, inj)
        while inj:
            inj.pop(0)()
        if own:
            G = gcs[C]
            for half in range(2):
                pb = gbank()
                for c in range(nch):
                    mm(pb[0:ntok, :], G["plc"][0:C, c * ntok:(c + 1) * ntok],
                       o_ch[c][0:C, half * 4:half * 4 + 4, :].rearrange("p h d -> p (h d)"), c == 0, c == nch - 1,
                       [G["plc"], o_ch[c]], [pb])
                act(o_a[0:ntok, half * 4:half * 4 + 4, :], hv(pb[0:ntok, :], 128), AF.Copy, [pb], [o_a])

    A_ph = Phase()
    o_bn = A_ph.a("o_bn", 128, [8, 128])
    qn_f = A_ph.a("qn_f", 128, [1024])
    qT = A_ph.a("qT", 128, [8, 128], BF16)
    KTb = [A_ph.a(f"KTb{i}", 128, [KB * 128], BF16) for i in range(2)]
    Vb = [A_ph.a(f"Vb{i}", 128, [KB, 132], BF16) for i in range(2)]
    PTb = [[A_ph.a(f"PT{m}{i}", 128, [512], BF16) for i in range(2)] for m in range(2)]
    o_b = A_ph.a("o_b", 128, [8, 128])
    ast = A_ph.a("ast", 128, [16])

    def attention(ntq, nkt, kT_of, kT_b, v_of, v_b, bias_of, diag_kt):
        qknorm(qn_f, COLS["fq"][0], 0, ntq)
        act(kv_b[0:ntq, :], qn_f[0:ntq, :], AF.Copy, [qn_f], [kv_b])
        pb = pbank[6]
        for h in range(8):
            tr(bfv(pb)[:, h * 128:h * 128 + ntq], kv_b[0:ntq, h * 128:(h + 1) * 128], ntq, [kv_b], [pb])
        act(qT[:, :, 0:ntq], bfv(pb)[:, :].rearrange("p (h t) -> p h t", t=128)[:, :, 0:ntq], AF.Copy, [pb], [qT])
        for i in range(2):
            T.op("pool", lambda e, i=i: e.memset(Vb[i][:, :, 128:129], 1.0), [], [Vb[i]])
        lc = 0
        gcount = 0
        pend = []

        def flush():
            while pend:
                pend.pop(0)()

        for h in range(8):
            Ob = [pbank[4 + 2 * (h % 2)], pbank[5 + 2 * (h % 2)]]
            for b0 in range(0, nkt, KB):
                nb = min(KB, nkt - b0)
                bi = lc % 2
                lc += 1
                dma("sp", KTb[bi][:, 0:nb * 128], kT_of(h, b0, b0 + nb), [kT_b], [KTb[bi]], f"kt{bi}")
                dma("sp", Vb[bi][:, 0:nb, 0:128], v_of(h, b0, b0 + nb), [v_b], [Vb[bi]], f"vv{bi}")
                for g0 in range(0, nb, 4):
                    ng = min(4, nb - g0)
                    par = gcount % 2
                    gcount += 1
                    pvs = []
                    for m in range(2):
                        Sb = pbank[m * 2 + par]
                        PT = PTb[m][par]
                        ms = slice(m * 64, (m + 1) * 64)
                        for i in range(ng):
                            kt = g0 + i
                            mm(Sb[:, i * ntq:(i + 1) * ntq], KTb[bi][ms, kt * 128:(kt + 1) * 128], qT[ms, h, 0:ntq], True, True,
                               [KTb[bi], qT], [Sb])
                        biases = [bias_of(b0 + g0 + i) for i in range(ng)]
                        if any(bb is not None for bb in biases):
                            for i in range(ng):
                                kw = {} if biases[i] is None else {"bias": biases[i]}
                                act(PT[:, i * ntq:(i + 1) * ntq], Sb[:, i * ntq:(i + 1) * ntq], AF.Exp, [Sb, kmask, smask], [PT],
                                    scale=0.125, **kw)
                        else:
                            act(PT[:, 0:ng * ntq], Sb[:, 0:ng * ntq], AF.Exp, [Sb], [PT], scale=0.125)
                        for i in range(ng):
                            kt_abs = b0 + g0 + i
                            if kt_abs == diag_kt:
                                T.op("dve", lambda e, PT=PT, i=i: e.memset(PT[64:128, i * ntq:i * ntq + 64], 0.0), [], [PT])

                        def pv(m=m, PT=PT, bi=bi, g0=g0, ng=ng, b0=b0, Ob=Ob):
                            for i in range(ng):
                                kt = g0 + i
                                kt_abs = b0 + kt
                                mm(Ob[m][0:ntq, 0:129], PT[:, i * ntq:(i + 1) * ntq], Vb[bi][:, kt, 0:129], kt_abs == 0,
                                   kt_abs == nkt - 1, [PT, Vb[bi]], [Ob[m]])
                        pvs.append(pv)
                    flush()
                    pend.extend(pvs)

            def fin(h=h, Ob=Ob):
                T.op("dve", lambda e: e.reciprocal(out=ast[0:ntq, 0:1], in_=Ob[0][0:ntq, 128:129]), [Ob[0]], [ast])
                T.op("dve", lambda e: e.reciprocal(out=ast[0:ntq, 1:2], in_=Ob[1][0:ntq, 128:129]), [Ob[1]], [ast])
                tt(ast[0:ntq, 1:2], ast[0:ntq, 1:2], lam_t[0:ntq, 3:4], ALU.mult, [ast, lam_t], [ast])
                ts(o_b[0:ntq, h, :], Ob[0][0:ntq, 0:128], ast[0:ntq, 0:1], None, ALU.mult, None, [Ob[0], ast], [o_b])
                T.op("dve", lambda e: e.scalar_tensor_tensor(out=o_b[0:ntq, h, :], in0=Ob[1][0:ntq, 0:128],
                                                             scalar=ast[0:ntq, 1:2], in1=o_b[0:ntq, h, :], op0=ALU.mult,
                                                             op1=ALU.add), [Ob[1], ast, o_b], [o_b])
            pend.append(fin)
        flush()
        head_rms(o_b, o_bn, ntq, 528, 1.0 - cfg.LAM_INIT, ast, 8)

    def head_rms(src, dst, ntok, grow0, mul, stbuf, scol):
        act(kn[0:ntok, :], src[0:ntok].rearrange("p h d -> p (h d)"), AF.Square, [src], [kn])
        T.op("dve", lambda e: e.tensor_reduce(out=stbuf[0:ntok, scol:scol + 8], in_=hv(kn[0:ntok, :], 128), axis=AX.X,
                                              op=ALU.add), [kn], [stbuf])
        ts(stbuf[0:ntok, scol:scol + 8], stbuf[0:ntok, scol:scol + 8], 1.0 / 128, cfg.EPS, ALU.mult, ALU.add, [stbuf], [stbuf])
        act(stbuf[0:ntok, scol:scol + 8], stbuf[0:ntok, scol:scol + 8], AF.Sqrt, [stbuf], [stbuf])
        T.op("dve", lambda e: e.reciprocal(out=stbuf[0:ntok, scol:scol + 8], in_=stbuf[0:ntok, scol:scol + 8]), [stbuf], [stbuf])
        if mul != 1.0:
            ts(stbuf[0:ntok, scol:scol + 8], stbuf[0:ntok, scol:scol + 8], mul, None, ALU.mult, None, [stbuf], [stbuf])
        tt(dst[0:ntok], src[0:ntok], stbuf[0:ntok, scol:scol + 8].unsqueeze(2).to_broadcast([ntok, 8, 128]), ALU.mult,
           [src, stbuf], [dst])
        tt(dst[0:ntok], dst[0:ntok], rows[0:ntok, grow0:grow0 + 128].unsqueeze(1).to_broadcast([ntok, 8, 128]), ALU.mult,
           [dst, rows], [dst])

    P_ph = Phase()
    assert A_ph.off >= 0
    P_ph.off = 1024
    o_an = P_ph.a("o_an", 128, [8, 128])
    sg = P_ph.a("sg", 128, [1024])
    mg_b = P_ph.a("mg_b", 128, [1024], BF16)
    mgT = P_ph.a("mgT", 128, [8, 128], BF16)
    h_f = P_ph.a("h_f", 128, [1024])
    hn_b = P_ph.a("hn_b", 128, [1024], BF16)
    hnT = P_ph.a("hnT", 128, [8, 128], BF16)
    pq_b = P_ph.a("pq_b", 128, [2048], BF16)
    pqT = P_ph.a("pqT", 128, [16, 128], BF16)
    sc = P_ph.a("sc", 128, [16, 128])
    sc2 = P_ph.a("sc2", 128, [256])
    sv = P_ph.a("sv", 128, [16, 16])
    si = P_ph.a("si", 128, [16, 16], U32)
    sif = P_ph.a("sif", 128, [16, 16])
    cand = P_ph.a("cand", 128, [8, 256])
    cid = P_ph.a("cid", 128, [8, 256])
    tv = P_ph.a("tv", 128, [8, 16])
    eid = P_ph.a("eid", 128, [128])
    eid_i = P_ph.a("eid_i", 128, [128], I32)
    gate = P_ph.a("gate", 128, [8, 16])
    pst = P_ph.a("pst", 128, [32])
    actv = P_ph.a("actv", 128, [128])
    wgt = P_ph.a("wgt", 128, [128])
    gbuf = [P_ph.a(f"gbuf{i}", 128, [1024], BF16) for i in range(4)]
    dgb = [P_ph.a(f"dgb{i}", 128, [128], BF16) for i in range(2)]
    pjunk = P_ph.a("pjunk", 128, [1024], BF16)
    for nm_ in ("sc", "cand", "cid"):
        for i_ in range(4):
            o_ = P_ph.offs[nm_] + i_ * 512
            gbuf.append(T.buf(arena[0:128, o_:o_ + 512].bitcast(BF16), f"gal_{nm_}{i_}"))
    NG = len(gbuf)


    def merge_peer(ntok, xsrc, ydst):
        head_rms(o_a, o_an, ntok, 400, 1.0, pst, 0)
        act(sg[0:ntok, :], proj[0:ntok, COLS["z"][0] - PO:COLS["z"][1] - PO], AF.Silu, [proj], [sg])
        tt(o_an[0:ntok].rearrange("p h d -> p (h d)"), o_an[0:ntok].rearrange("p h d -> p (h d)"), sg[0:ntok, :], ALU.mult,
           [o_an, sg], [o_an])
        act(sg[0:ntok, :], proj[0:ntok, COLS["ga"][0] - PO:COLS["ga"][1] - PO], AF.Sigmoid, [proj, o_an], [sg])
        tt(o_an[0:ntok].rearrange("p h d -> p (h d)"), o_an[0:ntok].rearrange("p h d -> p (h d)"), sg[0:ntok, :], ALU.mult,
           [o_an, sg], [o_an])
        act(sg[0:ntok, :], proj[0:ntok, COLS["gb"][0] - PO:COLS["gb"][1] - PO], AF.Sigmoid, [proj, o_an], [sg])
        tt(sg[0:ntok, :], sg[0:ntok, :], o_bn[0:ntok].rearrange("p h d -> p (h d)"), ALU.mult, [sg, o_bn], [sg])
        tt(mg_b[0:ntok, :], sg[0:ntok, :], o_an[0:ntok].rearrange("p h d -> p (h d)"), ALU.add, [sg, o_an], [mg_b])
        pb = pbank[0]
        for kc in range(KC):
            tr(bfv(pb)[:, kc * 128:kc * 128 + ntok], mg_b[0:ntok, kc * 128:(kc + 1) * 128], ntok, [mg_b], [pb])
        act(mgT[:, :, 0:ntok], bfv(pb)[:, :].rearrange("p (h t) -> p h t", t=128)[:, :, 0:ntok], AF.Copy, [pb], [mgT])
        dma("sp", h_f[0:ntok, :], xsrc, [], [h_f], "hx")
        project(w_out, mgT, 0, D, ntok,
                lambda pb, b0, b1: tt(h_f[0:ntok, b0:b1], pb[0:ntok, 0:b1 - b0], h_f[0:ntok, b0:b1], ALU.add, [pb, h_f], [h_f]))
        act(pjunk[0:ntok, :], h_f[0:ntok, :], AF.Square, [h_f], [pjunk, stat], accum_out=stat[0:ntok, 0:1])
        rstd_of(0, 3, ntok, D)
        act(sg[0:ntok, :], h_f[0:ntok, :], AF.Copy, [h_f, stat], [sg], scale=stat[0:ntok, 3:4])
        tt(hn_b[0:ntok, :], sg[0:ntok, :], rows[0:ntok, 1024:2048], ALU.mult, [sg, rows], [hn_b])
        pb = pbank[0]
        for kc in range(KC):
            tr(bfv(pb)[:, kc * 128:kc * 128 + ntok], hn_b[0:ntok, kc * 128:(kc + 1) * 128], ntok, [hn_b], [pb])
        act(hnT[:, :, 0:ntok], bfv(pb)[:, :].rearrange("p (h t) -> p h t", t=128)[:, :, 0:ntok], AF.Copy, [pb], [hnT])
        project(w_pq, hnT, 0, 2048, ntok,
                lambda pb, b0, b1: act(pq_b[0:ntok, b0:b1], pb[0:ntok, 0:b1 - b0], AF.Copy, [pb], [pq_b]))
        for half in range(2):
            pb = pbank[3 + half]
            for j in range(8):
                hp = half * 8 + j
                tr(bfv(pb)[:, j * 128:j * 128 + ntok], pq_b[0:ntok, hp * 128:(hp + 1) * 128], ntok, [pq_b], [pb])
            act(pqT[:, half * 8:half * 8 + 8, 0:ntok], bfv(pb)[:, :].rearrange("p (h t) -> p h t", t=128)[:, :, 0:ntok],
                AF.Copy, [pb], [pqT])
        for q4 in range(4):
            pb = pbank[3 + q4]
            for j in range(4):
                hp = q4 * 4 + j
                mm(pb[0:ntok, j * 128:(j + 1) * 128], pqT[:, hp, 0:ntok], skT[:, hp, :], True, True, [pqT, skT], [pb])
            act(sc[0:ntok, q4 * 4:q4 * 4 + 4, :], hv(pb[0:ntok, :], 128), AF.Copy, [pb], [sc])

        def top16(vals_of, src_ap, srcb, n, dstv, dsti, g):
            T.op("dve", lambda e: e.max(out=dstv[0:ntok, g, 0:8], in_=src_ap), [srcb], [dstv])
            if dsti is not None:
                T.op("dve", lambda e: e.max_index(out=dsti[0:ntok, g, 0:8], in_max=dstv[0:ntok, g, 0:8], in_values=src_ap),
                     [srcb, dstv], [dsti])
            T.op("dve", lambda e: e.match_replace(out=sc2[0:ntok, 0:n], in_to_replace=dstv[0:ntok, g, 0:8], in_values=src_ap,
                                                  imm_value=-1.0e30), [srcb, dstv], [sc2])
            T.op("dve", lambda e: e.max(out=dstv[0:ntok, g, 8:16], in_=sc2[0:ntok, 0:n]), [sc2], [dstv])
            if dsti is not None:
                T.op("dve", lambda e: e.max_index(out=dsti[0:ntok, g, 8:16], in_max=dstv[0:ntok, g, 8:16],
                                                  in_values=sc2[0:ntok, 0:n]), [sc2, dstv], [dsti])

        for g in range(16):
            top16(None, sc[0:ntok, g, :], sc, 128, sv, si, g)
        T.op("dve", lambda e: e.tensor_copy(out=sif[0:ntok], in_=si[0:ntok]), [si], [sif])
        sv4 = sv[0:ntok].rearrange("p (h t) k -> p h t k", t=2)
        sif4 = sif[0:ntok].rearrange("p (h t) k -> p h t k", t=2)
        c4 = lambda b: b[0:ntok].rearrange("p h (a b) -> p h a b", b=16)
        tt(c4(cand), sv4[:, :, 0, :].unsqueeze(3).to_broadcast([ntok, 8, 16, 16]),
           sv4[:, :, 1, :].unsqueeze(2).to_broadcast([ntok, 8, 16, 16]), ALU.add, [sv], [cand])
        ts(sif4[:, :, 0, :], sif4[:, :, 0, :], float(cfg.NKEYS), None, ALU.mult, None, [sif], [sif])
        tt(c4(cid), sif4[:, :, 0, :].unsqueeze(3).to_broadcast([ntok, 8, 16, 16]),
           sif4[:, :, 1, :].unsqueeze(2).to_broadcast([ntok, 8, 16, 16]), ALU.add, [sif], [cid])
        for hh in range(8):
            top16(None, cand[0:ntok, hh, :], cand, 256, tv, None, hh)
        for hh in range(8):
            for k in range(16):
                T.op("dve", lambda e, hh=hh, k=k: e.scalar_tensor_tensor(
                    out=sc2[0:ntok, 0:256], in0=cand[0:ntok, hh, :], scalar=tv[0:ntok, hh, k:k + 1], in1=cid[0:ntok, hh, :],
                    op0=ALU.is_equal, op1=ALU.mult, accum_out=eid[0:ntok, hh * 16 + k:hh * 16 + k + 1]),
                    [cand, tv, cid], [sc2, eid])
        ts(eid[0:ntok, :], eid[0:ntok, :], float(cfg.NKEYS * cfg.NKEYS - 1), 0.0, ALU.min, ALU.max, [eid], [eid])
        T.op("dve", lambda e: e.tensor_copy(out=eid_i[0:ntok, :], in_=eid[0:ntok, :]), [eid], [eid_i])
        ts(pst[0:ntok, 0:8], tv[0:ntok, :, 0], -1.0, None, ALU.mult, None, [tv], [pst])
        for hh in range(8):
            act(gate[0:ntok, hh, :], tv[0:ntok, hh, :], AF.Exp, [tv, pst], [gate, pst], bias=pst[0:ntok, hh:hh + 1],
                accum_out=pst[0:ntok, 8 + hh:9 + hh])
        T.op("dve", lambda e: e.reciprocal(out=pst[0:ntok, 16:24], in_=pst[0:ntok, 8:16]), [pst], [pst])
        tt(gate[0:ntok], gate[0:ntok], pst[0:ntok, 16:24].unsqueeze(2).to_broadcast([ntok, 8, 16]), ALU.mult, [gate, pst], [gate])
        T.barrier()
        for s_ in range(128):
            gbf = gbuf[s_ % NG]
            T.op("pool", lambda e, s_=s_, gbf=gbf: e.indirect_dma_start(
                out=gbf[0:ntok, :], out_offset=None, in_=pu_b[:, :],
                in_offset=bass.IndirectOffsetOnAxis(ap=eid_i[0:ntok, s_:s_ + 1], axis=0)), [eid_i, tconv_b], [gbf], dma=f"g{s_ % NG}")
            T.op("dve", lambda e, s_=s_, gbf=gbf: e.scalar_tensor_tensor(
                out=pjunk[0:ntok, :], in0=gbf[0:ntok, :], scalar=1.0, in1=hn_b[0:ntok, :], op0=ALU.mult,
                op1=ALU.mult, accum_out=actv[0:ntok, s_:s_ + 1]), [gbf, hn_b], [pjunk, actv])
        act(actv[0:ntok, :], actv[0:ntok, :], AF.Gelu, [actv], [actv])
        tt(wgt[0:ntok, :], actv[0:ntok, :], gate[0:ntok].rearrange("p h k -> p (h k)"), ALU.mult, [actv, gate], [wgt])
        Y = [pbank[1], pbank[2]]
        for s_ in range(128):
            gbf = gbuf[s_ % NG]
            dg = dgb[s_ % 2]
            T.op("pool", lambda e, s_=s_, gbf=gbf: e.indirect_dma_start(
                out=gbf[0:ntok, :], out_offset=None, in_=pv_b[:, :],
                in_offset=bass.IndirectOffsetOnAxis(ap=eid_i[0:ntok, s_:s_ + 1], axis=0)), [eid_i, tconv_b], [gbf], dma=f"g{s_ % NG}")
            act(dg[0:ntok, 0:ntok], ident_b[0:ntok, 0:ntok], AF.Copy, [ident_b, wgt], [dg], scale=wgt[0:ntok, s_:s_ + 1])
            for hb in range(2):
                mm(Y[hb][0:ntok, :], dg[0:ntok, 0:ntok], gbf[0:ntok, hb * 512:(hb + 1) * 512], s_ == 0, s_ == 127, [dg, gbf], [Y[hb]])
        for hb in range(2):
            tt(h_f[0:ntok, hb * 512:(hb + 1) * 512], Y[hb][0:ntok, :], h_f[0:ntok, hb * 512:(hb + 1) * 512], ALU.add,
               [Y[hb], h_f], [h_f])
        dma("sp", ydst, h_f[0:ntok, :], [h_f], [], "yo")

    T.op("pool", lambda e: e.memset(tail[:], 0.0), [], [tail])
    T.op("pool", lambda e: e.memset(S_f[:], 0.0), [], [S_f])
    T.op("pool", lambda e: e.memset(S_b[:], 0.0), [], [S_b])

    def own_tail(ntq, nkt, kT_of, kT_b, v_of, v_b, bias_of, diag_kt, xsrc, ydst):
        lvl = getattr(cfg, "DBG", 9)
        if lvl == 0:
            return
        T.barrier()
        if lvl == 1:
            attention(ntq, nkt, kT_of, kT_b, v_of, v_b, bias_of, diag_kt)
            T.barrier()
            return
        attention(ntq, nkt, kT_of, kT_b, v_of, v_b, bias_of, diag_kt)
        T.barrier()
        merge_peer(ntq, xsrc, ydst)
        T.barrier()

    FV = slice(COLS["fv"][0] - PO, COLS["fv"][1] - PO)

    def front_pieces(p):
        own = (p % 4 == 3)
        qi = p % 2
        xsrc = xp[p * 128:(p + 1) * 128, :]
        pcs = [lambda: stage_a(xsrc, 128)]
        ranges = [COLS["qkv"], (4096, 4112), (COLS["fk"][0], COLS["fv"][1])]
        if own:
            ranges += [COLS["z"], COLS["fq"], (COLS["ga"][0], COLS["gb"][1])]
        pcs += inproj_blocks(ranges, 128, qi)
        pcs.append(lambda: ba_copy(128, qi))
        pcs.append(lambda: qknorm(kn, COLS["fk"][0], 64, 128))
        ts_ = slice(p * 128, (p + 1) * 128)
        pcs.append(lambda: stage_c(kn[:, :], kn, proj[:, FV], proj, 128, kT_scr[:, :, ts_], kT_scr_b, v_scr[:, :, p, :], v_scr_b))
        if own:
            j = p // 4

            def outs():
                dma("sp", k_out[j * 128:(j + 1) * 128, :], kn[:, :], [kn], [], "ko")
                dma("sp", v_out[j * 128:(j + 1) * 128, :], proj[:, FV], [proj], [], "vo")
            pcs.append(outs)
        return pcs

    for f_ in front_pieces(0):
        f_()
    for p in range(NT):
        own = (p % 4 == 3)
        qi = p % 2
        if p == 1:
            convert_tables()
        xsrc = xp[p * 128:(p + 1) * 128, :]
        gdn_tile(64, 128, own, qi, front_pieces(p + 1) if p + 1 < NT else [])
        dma("sp", tail[:, :], qkv_b[qi][125:128, 0:QKV_W], [qkv_b[qi]] + tailpb, [tail], "tl")
        if p == NT - 1:
            for hf in range(2):
                dma("pool", conv_out[:, hf * 1536:(hf + 1) * 1536], qkv_b[qi][125:128, hf * 1536:(hf + 1) * 1536], [qkv_b[qi]], [], "co")
            dma("sp", s_fin.rearrange("h d e -> d h e"), S_f[:], [S_f], [], "so")
        if own:
            j = p // 4
            own_tail(128, p + 1,
                     lambda h, t0, t1: kT_scr[h, :, t0 * 128:t1 * 128], kT_scr_b,
                     lambda h, t0, t1: v_scr[h, :, t0:t1, :],
                     v_scr_b,
                     lambda kt: (kmask[:, kt:kt + 1] if kt < 3 else None), p,
                     xsrc, y_out[j * 128:(j + 1) * 128, :])

    T.op("pool", lambda e: e.memset(kT_st[:], 0.0), [], [kT_st])
    T.op("pool", lambda e: e.memset(kv_b[:], 0.0), [], [kv_b])
    dma("sp", kTs_scr[:, :, PAST:PAST + 128].rearrange("h d t -> d h t"), kT_st[:, :, :], [kT_st], [kTs_scr_b], "ks")
    dma("sp", vs_scr[:, :, NPT, :].rearrange("h p d -> p h d"), kv_b[:, :].rearrange("p (h d) -> p h d", d=128), [kv_b], [vs_scr_b], "vs")
    for sidx in range(NS):
        T.barrier()
        for kt in range(NPT):
            dma("sp", kn[:, :], ck_in[sidx, kt * 128:(kt + 1) * 128, :], [], [kn], "xk")
            dma("sp", xt[:, :], cv_in[sidx, kt * 128:(kt + 1) * 128, :], [], [xt], "x")
            ts_ = slice(kt * 128, (kt + 1) * 128)
            stage_c(kn[:, :], kn, xt[:, :], xt, 128, kTs_scr[:, :, ts_], kTs_scr_b, vs_scr[:, :, kt, :], vs_scr_b)
        xsrc = xs_in[sidx * LS:(sidx + 1) * LS, :]
        stage_a(xsrc, LS)
        for f_ in inproj_blocks([(0, IN_W)], LS, 0):
            f_()
        ba_copy(LS, 0)
        qknorm(kn, COLS["fk"][0], 64, LS)
        ts_ = slice(PAST, PAST + LS)
        stage_c(kn[0:LS, :], kn, proj[0:LS, FV], proj, LS, kTs_scr[:, :, ts_], kTs_scr_b,
                vs_scr[:, 0:LS, NPT, :], vs_scr_b)
        dma("sp", ks_out[sidx * LS:(sidx + 1) * LS, :], kn[0:LS, :], [kn], [], "ko")
        dma("sp", vs_out[sidx * LS:(sidx + 1) * LS, :], proj[0:LS, FV], [proj], [], "vo")
        for hf in range(2):
            dma("pool", cs_out[sidx, :, hf * 1536:(hf + 1) * 1536], qkv_b[0][LS - 3:LS, hf * 1536:(hf + 1) * 1536], [qkv_b[0]], [], "co")
        for hf in range(2):
            dma("pool", tail[:, hf * 1536:(hf + 1) * 1536], cs_in[sidx, :, hf * 1536:(hf + 1) * 1536], tailpb, [tail], "tl")
        dma("sp", S_f[:], ss_in[sidx].rearrange("h d e -> d h e"), [], [S_f], "si")
        act(S_b[:], S_f[:], AF.Copy, [S_f], [S_b])
        gdn_tile(LS, LS, True, 0, [])
        dma("sp", ss_out[sidx].rearrange("h d e -> d h e"), S_f[:], [S_f], [], "so")
        own_tail(LS, NPT + 1,
                 lambda h, t0, t1: kTs_scr[h, :, t0 * 128:t1 * 128], kTs_scr_b,
                 lambda h, t0, t1: vs_scr[h, :, t0:t1, :],
                 vs_scr_b,
                 lambda kt: (smask[:, 0:1] if kt == NPT else None), None,
                 xsrc, ys_out[sidx * LS:(sidx + 1) * LS, :])

    T.emit()
    st.close()
    return nc


def _gconst(C, ntok):
    bf = ml_dtypes.bfloat16
    nch = ntok // C
    shA = np.zeros((ntok, nch * 4 * C), np.float32)
    sel = np.zeros((ntok, nch * C), np.float32)
    plc = np.zeros((C, nch * ntok), np.float32)
    for c in range(nch):
        for j in range(4):
            for i in range(C):
                t = C * c + i + j - 3
                if 0 <= t < ntok:
                    shA[t, (c * 4 + j) * C + i] = 1.0
        for i in range(C):
            sel[C * c + i, c * C + i] = 1.0
            plc[i, c * ntok + C * c + i] = 1.0
    shB = np.zeros((3, 3 * C), np.float32)
    for j in range(3):
        for i in range(C):
            t = i + j
            if t < 3:
                shB[t, j * C + i] = 1.0
    ar = np.arange(C)
    tri = (ar[:, None] <= ar[None, :]).astype(np.float32)
    rep = lambda m: np.ascontiguousarray(m.astype(np.float32))
    return {f"shA{C}": shA.astype(bf), f"shB{C}": shB.astype(bf), f"sel{C}": sel, f"tri{C}": tri,
            f"msl{C}": rep((ar[:, None] > ar[None, :]).astype(np.float32)),
            f"msu{C}": rep((ar[None, :] > ar[:, None]).astype(np.float32)),
            f"mui{C}": rep((ar[None, :] >= ar[:, None]).astype(np.float32)),
            f"id8{C}": rep(np.eye(C, dtype=np.float32)), f"plc{C}": plc.astype(bf)}


def _prep(cfg, inputs):
    NT = cfg.SEQ // 128
    f = lambda k: np.asarray(inputs[k], np.float32)
    rows = np.zeros((1, 2048), np.float32)
    rows[0, 0:64] = f("diff_q_norm_g")[0]
    rows[0, 64:128] = f("diff_k_norm_g")[0]
    rows[0, 128:136] = f("delta_a_log")[0]
    rows[0, 136:144] = f("delta_dt_bias")[0]
    rows[0, 144:208] = f("diff_lambda_q1")[0]
    rows[0, 208:272] = f("diff_lambda_k1")[0]
    rows[0, 272:336] = f("diff_lambda_q2")[0]
    rows[0, 336:400] = f("diff_lambda_k2")[0]
    rows[0, 400:528] = f("delta_norm_g")[0]
    rows[0, 528:656] = f("diff_norm_g")[0]
    rows[0, 1024:2048] = f("norm_ffn_g")[0]
    common = {
        "w_in": np.ascontiguousarray(f("w_in")[0]),
        "w_out": np.ascontiguousarray(f("w_out")[0]),
        "w_pq": np.ascontiguousarray(f("peer_w_q")[0]),
        "subk": np.ascontiguousarray(f("peer_sub_keys")[0].reshape(16 * 128, 128)),
        "peer_u": np.ascontiguousarray(f("peer_u")[0]),
        "peer_v": np.ascontiguousarray(f("peer_v")[0]),
        "g_mix": np.ascontiguousarray(f("norm_mix_g")[0].reshape(cfg.D // 128, 128).T),
        "rows": rows,
        "ident": np.eye(128, dtype=np.float32),
        "conv_w": np.ascontiguousarray(f("conv_w")[0]),
        **_gconst(64, 128), **_gconst(cfg.DEC_SEQ, cfg.DEC_SEQ),
    }
    smask = np.full((128, 1), NEG, np.float32)
    smask[:cfg.DEC_SEQ] = 0.0
    common["smask"] = smask
    xpr = f("x_prompt")
    in_maps = []
    for c in range(8):
        b, r = c // 4, c % 4
        lead = 3 - r
        xpad = np.zeros((NT * 128, cfg.D), np.float32)
        nreal = NT - lead
        xpad[lead * 128:] = xpr[b, :nreal * 128]
        kmask = np.zeros((128, NT), np.float32)
        kmask[:, :lead] = NEG
        m = dict(common)
        m.update({
            "xp": xpad, "kmask": kmask,
            "xs": np.ascontiguousarray(f("x_sample")[2 * c:2 * c + 2].reshape(-1, cfg.D)),
            "ss_in": np.ascontiguousarray(f("state_delta_s")[0][2 * c:2 * c + 2]),
            "cs_in": np.ascontiguousarray(f("state_delta_conv")[0][2 * c:2 * c + 2]),
            "ck_in": np.ascontiguousarray(f("cache_diff_k")[0][2 * c:2 * c + 2].reshape(2, cfg.PAST, 1024)),
            "cv_in": np.ascontiguousarray(f("cache_diff_v")[0][2 * c:2 * c + 2].reshape(2, cfg.PAST, 1024)),
        })
        in_maps.append(m)
    return NT, in_maps


def kernel(**inputs):
    cfg = Cfg
    NT, in_maps = _prep(cfg, inputs)
    nc = build(cfg, NT)
    res = run_bass_kernel_spmd(nc, in_maps, core_ids=list(range(8))).results
    B, S, Dm, H = cfg.BATCH, cfg.SEQ, cfg.D, cfg.H
    DB, DS = cfg.DEC_BATCH, cfg.DEC_SEQ
    f32 = np.float32
    y_p = np.zeros((B, NT, 128, Dm), f32)
    y_s = np.zeros((DB, DS, Dm), f32)
    k_p = np.zeros((1, B, NT, 128, 1024), f32)
    v_p = np.zeros((1, B, NT, 128, 1024), f32)
    s_p = np.zeros((1, B, H, 128, 128), f32)
    c_p = np.zeros((1, B, 3, QKV_W), f32)
    k_s = np.zeros((1, DB, DS, 1024), f32)
    v_s = np.zeros((1, DB, DS, 1024), f32)
    s_s = np.zeros((1, DB, H, 128, 128), f32)
    c_s = np.zeros((1, DB, 3, QKV_W), f32)
    for c in range(8):
        b, r = c // 4, c % 4
        o = res[c]
        y_p[b, r::4] = o["y_own"].reshape(NT // 4, 128, Dm)
        k_p[0, b, r::4] = o["k_own"].reshape(NT // 4, 128, 1024)
        v_p[0, b, r::4] = o["v_own"].reshape(NT // 4, 128, 1024)
        if r == 3:
            c_p[0, b] = o["conv_fin"]
            s_p[0, b] = o["s_fin"]
        y_s[2 * c:2 * c + 2] = o["ys"].reshape(2, DS, Dm)
        k_s[0, 2 * c:2 * c + 2] = o["ks"].reshape(2, DS, 1024)
        v_s[0, 2 * c:2 * c + 2] = o["vs"].reshape(2, DS, 1024)
        s_s[0, 2 * c:2 * c + 2] = o["ss"]
        c_s[0, 2 * c:2 * c + 2] = o["cs"]
    return (y_p.reshape(B, S, Dm), y_s, k_p.reshape(1, B, S, H, 2, 64), v_p.reshape(1, B, S, H, 128), s_p, c_p,
            k_s.reshape(1, DB, DS, H, 2, 64), v_s.reshape(1, DB, DS, H, 128), s_s, c_s)
```

```python
import numpy as np
import ml_dtypes
import concourse.bass as bass
import concourse.mybir as mybir
from concourse.alu_op_type import AluOpType as ALU
from concourse.bass_utils import run_bass_kernel_spmd

F32 = mybir.dt.float32
BF16 = mybir.dt.bfloat16
I32 = mybir.dt.int32
U32 = mybir.dt.uint32
AF = mybir.ActivationFunctionType
AX = mybir.AxisListType


class Cfg:
    D = 1024
    BATCH = 2
    SEQ = 16384
    DEC_BATCH = 16
    DEC_SEQ = 16
    PAST = 1024
    H = 8
    NKEYS = 128
    TOPK = 16
    EPS = 1e-6
    LAM_INIT = 0.8 - 0.6 * 1.0


EPOCH = 12000
COMPUTE = ("pe", "act", "dve", "pool")


class Buf:
    def __init__(self, ap_owner, name):
        self.t = ap_owner
        self.name = name
        self.w = {}
        self.r = {}

    def __getitem__(self, idx):
        return self.t[idx]


class Op:
    __slots__ = ("eng", "fn", "deps", "idx", "dsem", "dcum", "key")


class Trk:
    def __init__(self, nc):
        self.nc = nc
        self.lists = {e: [] for e in COMPUTE + ("sp",)}
        self.dma_sems = {}
        self.nev = {}
        self.last = {}
        self.bufs = []

    def buf(self, t, name):
        b = Buf(t, name)
        self.bufs.append(b)
        return b

    def barrier(self):
        deps = list(self.last.values())
        for e in COMPUTE + ("sp",):
            self.op(e, lambda eng: eng.nop(), extra=deps)

    def op(self, eng, fn, reads=(), writes=(), dma=None, extra=()):
        o = Op()
        o.eng = eng
        o.fn = fn
        o.dsem = dma
        o.key = ("dma", dma) if dma else eng
        deps = []
        for b in reads:
            for k, w in b.w.items():
                if self._need(o, k, raw=True):
                    deps.append(w)
        for b in writes:
            for k, w in b.w.items():
                if self._need(o, k, raw=False):
                    deps.append(w)
            for k, r in b.r.items():
                if self._need(o, k, raw=False):
                    deps.append(r)
        deps.extend(extra)
        o.deps = deps
        if not extra:
            self.last[o.key] = o
        lst = self.lists[eng]
        lst.append(o)
        if not dma:
            o.idx = self.nev.get(eng, 0)
            self.nev[eng] = o.idx + 1
        if dma:
            ent = self.dma_sems.setdefault(dma, [None, 0])
            ent[1] += 16
            o.dcum = ent[1]
        for b in reads:
            b.r[o.key] = o
        for b in writes:
            b.w[o.key] = o
        return o

    def _need(self, o, k, raw):
        if o.dsem or (isinstance(k, tuple)):
            return True
        if k != o.eng:
            return True
        if o.eng == "pe":
            return False
        return True

    def emit(self):
        nc = self.nc
        import contextlib
        with contextlib.ExitStack() as st:
            sems = {}
            for e in COMPUTE + ("sp",):
                n = self.nev.get(e, 0) // EPOCH + 1
                sems[e] = [st.enter_context(nc.semaphore(f"s_{e}_{i}")) for i in range(n)]
            for name, ent in self.dma_sems.items():
                ent[0] = st.enter_context(nc.semaphore(f"d_{name}"))
            block = st.enter_context(nc.Block())

            def target(dep):
                if dep.dsem:
                    return self.dma_sems[dep.dsem][0], dep.dcum
                return sems[dep.eng][dep.idx // EPOCH], dep.idx % EPOCH + 1

            def run(ename, eng):
                waited = {}
                for o in self.lists[ename]:
                    for d in o.deps:
                        s, v = target(d)
                        kk = id(s)
                        if waited.get(kk, 0) >= v:
                            continue
                        waited[kk] = v
                        eng.wait_ge(s, v)
                    ins = o.fn(eng)
                    if o.dsem:
                        ins.then_inc(self.dma_sems[o.dsem][0], 16)
                    else:
                        ins.then_inc(sems[ename][o.idx // EPOCH], 1)
                if ename == "sp":
                    for name, ent in self.dma_sems.items():
                        eng.wait_ge(ent[0], ent[1])
                    for e2 in COMPUTE:
                        n = self.nev.get(e2, 0)
                        if n:
                            eng.wait_ge(sems[e2][(n - 1) // EPOCH], (n - 1) % EPOCH + 1)

            @block.tensor
            def _(e):
                run("pe", e)

            @block.scalar
            def _(e):
                run("act", e)

            @block.vector
            def _(e):
                run("dve", e)

            @block.gpsimd
            def _(e):
                run("pool", e)

            @block.sync
            def _(e):
                run("sp", e)


QKV_W = 3072
COLS = dict(qkv=(0, 3072), z=(3072, 4096), beta=(4096, 4104), a=(4104, 4112),
            fq=(4112, 5136), fk=(5136, 6160), fv=(6160, 7184), ga=(7184, 8208), gb=(8208, 9232))
IN_W = 9232
ARENA_F32 = 18880
KB = 16
NEG = -1.0e4


def build(cfg, NT):
    nc = bass.Bass("TRN2", target_bir_lowering=False)
    D = cfg.D
    KC = D // 128
    T = Trk(nc)
    import contextlib
    st = contextlib.ExitStack()

    def din(name, shape, dt=F32):
        return nc.dram_tensor(name, list(shape), dt, kind="ExternalInput").ap()

    def dout(name, shape, dt=F32):
        return nc.dram_tensor(name, list(shape), dt, kind="ExternalOutput").ap()

    def dscr(name, shape, dt=BF16):
        return nc.dram_tensor(name, list(shape), dt, kind="Internal").ap()

    def sb(name, shape, dt=F32):
        t = st.enter_context(nc.sbuf_tensor(name, list(shape), dt))
        return T.buf(t, name)

    def ps(name, shape, dt=F32):
        t = st.enter_context(nc.psum_tensor(name, list(shape), dt))
        return T.buf(t, name)

    NOWN = NT // 4
    NS = cfg.DEC_BATCH // 8
    LS = cfg.DEC_SEQ
    PAST = cfg.PAST
    NPT = PAST // 128
    xp = din("xp", [NT * 128, D])
    kmask_in = din("kmask", [128, NT])
    xs_in = din("xs", [NS * LS, D])
    w_in = din("w_in", [D, IN_W])
    w_out = din("w_out", [D, D])
    w_pq = din("w_pq", [D, 2048])
    subk = din("subk", [16 * 128, 128])
    peer_u = din("peer_u", [cfg.NKEYS * cfg.NKEYS, D])
    peer_v = din("peer_v", [cfg.NKEYS * cfg.NKEYS, D])
    g_mix = din("g_mix", [128, KC])
    rows_in = din("rows", [1, 2048])
    ident_in = din("ident", [128, 128])
    conv_w = din("conv_w", [4, QKV_W])
    ss_in = din("ss_in", [NS, 8, 128, 128])
    cs_in = din("cs_in", [NS, 3, QKV_W])
    ck_in = din("ck_in", [NS, PAST, 1024])
    cv_in = din("cv_in", [NS, PAST, 1024])
    smask_in = din("smask", [128, 1])
    y_out = dout("y_own", [NOWN * 128, D])
    k_out = dout("k_own", [NOWN * 128, 1024])
    v_out = dout("v_own", [NOWN * 128, 1024])
    conv_out = dout("conv_fin", [3, QKV_W])
    s_fin = dout("s_fin", [8, 128, 128])
    ys_out = dout("ys", [NS * LS, D])
    ks_out = dout("ks", [NS * LS, 1024])
    vs_out = dout("vs", [NS * LS, 1024])
    cs_out = dout("cs", [NS, 3, QKV_W])
    ss_out = dout("ss", [NS, 8, 128, 128])
    gc = {}
    for C_ in (64, LS):
        ntk = 128 if C_ == 64 else LS
        nch = ntk // C_
        gc[C_] = dict(
            shA=din(f"shA{C_}", [ntk, nch * 4 * C_], BF16), shB=din(f"shB{C_}", [3, 3 * C_], BF16),
            sel=din(f"sel{C_}", [ntk, nch * C_]), tri=din(f"tri{C_}", [C_, C_]),
            msl=din(f"msl{C_}", [C_, C_]),
            msu=din(f"msu{C_}", [C_, C_]), mui=din(f"mui{C_}", [C_, C_]),
            id8=din(f"id8{C_}", [C_, C_]), plc=din(f"plc{C_}", [C_, nch * ntk], BF16))
    kT_scr = dscr("kT_scr", [8, 128, NT * 128])
    v_scr = dscr("v_scr", [8, 128, NT, 128])
    kTs_scr = dscr("kTs_scr", [8, 128, PAST + 128])
    vs_scr = dscr("vs_scr", [8, 128, NPT + 1, 128])
    w_in_b = dscr("w_in_b", [D, IN_W])
    w_out_b = dscr("w_out_b", [D, D])
    w_pq_b = dscr("w_pq_b", [D, 2048])
    uv_b = dscr("uv_b", [cfg.NKEYS * cfg.NKEYS, 2 * D])
    wconv_b = T.buf(None, "wconv")
    tconv_b = T.buf(None, "tconv")
    kT_scr_b = T.buf(None, "kT_scr")
    v_scr_b = T.buf(None, "v_scr")
    kTs_scr_b = T.buf(None, "kTs_scr")
    vs_scr_b = T.buf(None, "vs_scr")

    ident_f = sb("ident_f", [128, 128])
    ident_b = sb("ident_b", [128, 128], BF16)
    ones_f = sb("ones_f", [128, 128])
    gmix_t = sb("gmix_t", [128, KC])
    rows = sb("rows_t", [128, 2048])
    qg_t = rows
    negA = sb("negA", [128, 8])
    lam_t = sb("lam_t", [128, 8])
    kmask = sb("kmask_t", [128, NT])
    smask = sb("smask_t", [128, 1])
    xt = sb("xt", [128, D])
    xs_bf = sb("xs_bf", [128, D], BF16)
    junk = xs_bf
    xnT = sb("xnT", [128, KC, 128], BF16)
    stat = sb("stat", [128, 8])
    wbuf = [sb(f"wbuf{i}", [128, KC, 512], BF16) for i in range(2)]
    PO = QKV_W
    proj = sb("proj", [128, IN_W - PO])
    qkv_b = [sb(f"qkv_b{i}", [128, QKV_W], BF16) for i in range(2)]
    ba_raw = [sb(f"ba_raw{i}", [128, 16]) for i in range(2)]
    cur = {"i": 0}
    kn = sb("kn", [128, 1024])
    sq = xt
    kst = sb("kst", [128, 16])
    wrows = sb("wrows", [128, 4, QKV_W], BF16)
    skT = sb("skT", [128, 16, 128], BF16)
    gcs = {}
    for C_ in (64, LS):
        ntk = 128 if C_ == 64 else LS
        nch = ntk // C_
        gcs[C_] = dict(shA=sb(f"t_shA{C_}", [ntk, nch * 4 * C_], BF16), shB=sb(f"t_shB{C_}", [3, 3 * C_], BF16),
                       sel=sb(f"t_sel{C_}", [ntk, nch * C_]), tri=sb(f"t_tri{C_}", [C_, C_]),
                       msl=sb(f"t_msl{C_}", [C_, C_]),
                       msu=sb(f"t_msu{C_}", [C_, C_]), mui=sb(f"t_mui{C_}", [C_, C_]),
                       id8=sb(f"t_id8{C_}", [C_, C_]), plc=sb(f"t_plc{C_}", [C_, nch * ntk], BF16))
    prodb = [sb("prodb0", [128, 4, 512], BF16)] * 2
    tail = sb("tail", [3, QKV_W], BF16)
    tailpb = [sb("tailpb0", [3, 3, 512], BF16)] * 2
    S_f = sb("S_f", [128, 8, 128])
    S_b = sb("S_b", [128, 8, 128], BF16)
    o_a = sb("o_a", [128, 8, 128])
    kv_b = sb("kv_b", [128, 1024], BF16)
    kT_st = sb("kT_st", [128, 8, 128], BF16)
    arena = st.enter_context(nc.sbuf_tensor("arena", [128, ARENA_F32], F32))
    pbank = [ps(f"pb{i}", [128, 512]) for i in range(8)]

    def bfv(pb):
        return pb.t[:].bitcast(BF16)

    class Phase:
        def __init__(self):
            self.off = 0

        def a(self, name, parts, shape, dt=F32):
            n = 1
            for x in shape:
                n *= x
            words = (n * (2 if dt == BF16 else 4) + 3) // 4
            words = (words + 7) // 8 * 8
            assert self.off + words <= ARENA_F32, (name, self.off, words)
            v = arena[0:parts, self.off:self.off + words]
            self.offs = getattr(self, "offs", {})
            self.offs[name] = self.off
            self.off += words
            if dt != F32:
                v = v.bitcast(dt)
            v = v[:, 0:n]
            if len(shape) == 2:
                v = v.rearrange("p (a b) -> p a b", b=shape[1])
            elif len(shape) == 3:
                v = v.rearrange("p (a b c) -> p a b c", b=shape[1], c=shape[2])
            return T.buf(v, name)

    def dma(eng, out, in_, reads, writes, sem):
        return T.op(eng, lambda e: e.dma_start(out=out, in_=in_), reads=reads, writes=writes, dma=sem)

    def act(out, in_, func, reads, writes, **kw):
        return T.op("act", lambda e: e.activation(out=out, in_=in_, func=func, **kw), reads, writes)

    def tt(out, in0, in1, op, reads, writes, eng="dve"):
        return T.op(eng, lambda e: e.tensor_tensor(out=out, in0=in0, in1=in1, op=op), reads, writes)

    def ts(out, in0, s1, s2, op0, op1, reads, writes):
        if op1 is None:
            return T.op("dve", lambda e: e.tensor_scalar(out=out, in0=in0, scalar1=s1, scalar2=None, op0=op0), reads, writes)
        return T.op("dve", lambda e: e.tensor_scalar(out=out, in0=in0, scalar1=s1, scalar2=s2, op0=op0, op1=op1),
                    reads, writes)

    def mm(out, lhsT, rhs, start, stop, reads, writes):
        return T.op("pe", lambda e: e.matmul(out, lhsT=lhsT, rhs=rhs, start=start, stop=stop), reads, writes)

    def tr(out, in_, n, reads, writes):
        return T.op("pe", lambda e: e.transpose(out=out, in_=in_, identity=ident_b[0:n, 0:n]), list(reads) + [ident_b], writes)

    dma("sp", ident_f[:], ident_in[:, :], [], [ident_f], "c0")
    dma("sp", gmix_t[:], g_mix[:, :], [], [gmix_t], "c1")
    dma("sp", rows[:], rows_in[0:1, :].to_broadcast([128, 2048]), [], [rows], "c2")
    dma("sp", kmask[:], kmask_in[:, :], [], [kmask], "c3")
    dma("sp", smask[:], smask_in[:, :], [], [smask], "c3b")
    T.op("dve", lambda e: e.tensor_copy(out=ident_b[:], in_=ident_f[:]), [ident_f], [ident_b])
    for j in range(4):
        for hf in range(2):
            dma("pool", wrows[:, j, hf * 1536:(hf + 1) * 1536],
                conv_w[j:j + 1, hf * 1536:(hf + 1) * 1536].to_broadcast([128, 1536]), [], [wrows], "c4")
    ci = 7
    for C_ in gcs:
        for nm in gcs[C_]:
            dst = gcs[C_][nm]
            src = gc[C_][nm]
            dma("sp", dst[:], src[:, :], [], [dst], f"c{ci}")
            ci += 1
    act(negA[:], rows[:, 128:136], AF.Exp, [rows], [negA])
    ts(negA[:], negA[:], -1.0, None, ALU.mult, None, [negA], [negA])
    T.op("pool", lambda e: e.memset(ones_f[:], 1.0), [], [ones_f])
    tt(xs_bf[:, 0:64], rows[:, 144:208], rows[:, 208:272], ALU.mult, [rows], [xs_bf])
    T.op("dve", lambda e: e.tensor_reduce(out=lam_t[:, 0:1], in_=xs_bf[:, 0:64], axis=AX.X, op=ALU.add), [xs_bf], [lam_t])
    tt(xs_bf[:, 64:128], rows[:, 272:336], rows[:, 336:400], ALU.mult, [rows], [xs_bf])
    T.op("dve", lambda e: e.tensor_reduce(out=lam_t[:, 1:2], in_=xs_bf[:, 64:128], axis=AX.X, op=ALU.add), [xs_bf], [lam_t])
    act(lam_t[:, 0:2], lam_t[:, 0:2], AF.Exp, [lam_t], [lam_t])
    tt(lam_t[:, 2:3], lam_t[:, 0:1], lam_t[:, 1:2], ALU.subtract, [lam_t], [lam_t])
    ts(lam_t[:, 2:3], lam_t[:, 2:3], cfg.LAM_INIT, None, ALU.add, None, [lam_t], [lam_t])
    ts(lam_t[:, 3:4], lam_t[:, 2:3], -1.0, None, ALU.mult, None, [lam_t], [lam_t])
    for hp in range(16):
        dma("sp", xt[:, 0:128], subk[hp * 128:(hp + 1) * 128, :], [], [xt], "x")
        act(xs_bf[:, 0:128], xt[:, 0:128], AF.Copy, [xt], [xs_bf])
        tr(bfv(pbank[0])[:, 0:128], xs_bf[:, 0:128], 128, [xs_bf], [pbank[0]])
        act(skT[:, hp, :], bfv(pbank[0])[:, 0:128], AF.Copy, [pbank[0]], [skT])

    for (src, dst, ncol) in ((w_in, w_in_b, IN_W), (w_out, w_out_b, D), (w_pq, w_pq_b, 2048)):
        for c0 in range(0, ncol, 2048):
            c1 = min(c0 + 2048, ncol)
            dma("pool", dst[:, c0:c1], src[:, c0:c1], [], [wconv_b], "cvw")
    WSRC = {id(w_in): w_in_b, id(w_out): w_out_b, id(w_pq): w_pq_b}

    def convert_tables():
        for (src, c0) in ((peer_u, 0), (peer_v, D)):
            for r0 in range(0, cfg.NKEYS * cfg.NKEYS, 2048):
                dma("pool", uv_b[r0:r0 + 2048, c0:c0 + D], src[r0:r0 + 2048, :], [], [tconv_b], "cvt")

    wcount = {"n": 0}

    def load_w(src, c0, c1):
        i = wcount["n"] % 2
        wcount["n"] += 1
        wb = wbuf[i]
        n = c1 - c0
        srcb = WSRC[id(src)]
        T.op("pool", lambda e: e.dma_start(out=wb[:, :, 0:n],
                                          in_=srcb[:, c0:c1].rearrange("(kc p) n -> p kc n", p=128)),
             reads=[wconv_b], writes=[wb], dma=f"w{i}")
        return wb

    def rstd_of(col_ss, col_out, ntok, n):
        ts(stat[0:ntok, 6:7], stat[0:ntok, col_ss:col_ss + 1], 1.0 / n, cfg.EPS, ALU.mult, ALU.add, [stat], [stat])
        act(stat[0:ntok, 7:8], stat[0:ntok, 6:7], AF.Sqrt, [stat], [stat])
        T.op("dve", lambda e: e.reciprocal(out=stat[0:ntok, col_out:col_out + 1], in_=stat[0:ntok, 7:8]), [stat], [stat])

    def norm_T(src, srcb, dstT, ntok, gcol):
        act(junk[0:ntok, :], src, AF.Square, [srcb], [junk, stat], accum_out=stat[0:ntok, 0:1])
        rstd_of(0, 3, ntok, D)
        act(xs_bf[0:ntok, :], src, AF.Copy, [srcb, stat], [xs_bf], scale=stat[0:ntok, 3:4])
        pb = pbank[0]
        pbv = bfv(pb)
        for kc in range(KC):
            tr(pbv[:, kc * 128:kc * 128 + ntok], xs_bf[0:ntok, kc * 128:(kc + 1) * 128], ntok, [xs_bf], [pb])
        for kc in range(KC):
            if gcol is not None:
                ts(dstT[:, kc, 0:ntok], pbv[:, kc * 128:kc * 128 + ntok], gcol[:, kc:kc + 1], None, ALU.mult, None,
                   [pb, gmix_t], [dstT])
            else:
                act(dstT[:, kc, 0:ntok], pbv[:, kc * 128:kc * 128 + ntok], AF.Copy, [pb], [dstT])

    def stage_a(xsrc, ntok):
        dma("sp", xt[0:ntok, :], xsrc, [], [xt], "x")
        norm_T(xt[0:ntok, :], xt, xnT, ntok, gmix_t)

    pcount = {"n": 0}

    def proj_block(src, lhsT_buf, b0, b1, ntok, sink):
        n = b1 - b0
        wb = load_w(src, b0, b1)
        pb = pbank[1 + pcount["n"] % 2]
        pcount["n"] += 1
        for kc in range(KC):
            mm(pb[0:ntok, 0:n], lhsT_buf[:, kc, 0:ntok], wb[:, kc, 0:n], kc == 0, kc == KC - 1, [lhsT_buf, wb], [pb])
        sink(pb, b0, b1)

    def project(src, lhsT_buf, c0, c1, ntok, sink):
        for b0 in range(c0, c1, 512):
            proj_block(src, lhsT_buf, b0, min(b0 + 512, c1), ntok, sink)

    def inproj_sink(ntok, qi):
        def sink(pb, b0, b1):
            if b1 <= PO:
                act(qkv_b[qi][0:ntok, b0:b1], pb[0:ntok, 0:b1 - b0], AF.Copy, [pb], [qkv_b[qi]])
            else:
                act(proj[0:ntok, b0 - PO:b1 - PO], pb[0:ntok, 0:b1 - b0], AF.Copy, [pb], [proj])
        return sink

    def inproj_blocks(ranges, ntok, qi):
        out = []
        for (c0, c1) in ranges:
            for b0 in range(c0, c1, 512):
                out.append(lambda b0=b0, b1=min(b0 + 512, c1): proj_block(w_in, xnT, b0, b1, ntok, inproj_sink(ntok, qi)))
        return out

    def ba_copy(ntok, qi):
        T.op("dve", lambda e: e.tensor_copy(out=ba_raw[qi][0:ntok, :], in_=proj[0:ntok, 4096 - PO:4112 - PO]), [proj], [ba_raw[qi]])

    def qknorm(dst, c0, g0, ntok):
        v3 = lambda ap: ap.rearrange("p (g d) -> p g d", d=64)
        act(sq[0:ntok, :], proj[0:ntok, c0 - PO:c0 - PO + 1024], AF.Square, [proj], [sq])
        T.op("dve", lambda e: e.tensor_reduce(out=kst[0:ntok, :], in_=v3(sq[0:ntok, :]), axis=AX.X, op=ALU.add), [sq], [kst])
        ts(kst[0:ntok, :], kst[0:ntok, :], 1.0 / 64, cfg.EPS, ALU.mult, ALU.add, [kst], [kst])
        act(kst[0:ntok, :], kst[0:ntok, :], AF.Sqrt, [kst], [kst])
        T.op("dve", lambda e: e.reciprocal(out=kst[0:ntok, :], in_=kst[0:ntok, :]), [kst], [kst])
        tt(v3(dst[0:ntok, :]), v3(proj[0:ntok, c0 - PO:c0 - PO + 1024]), kst[0:ntok, :].unsqueeze(2).to_broadcast([ntok, 16, 64]),
           ALU.mult, [proj, kst], [dst])
        tt(v3(dst[0:ntok, :]), v3(dst[0:ntok, :]), rows[0:ntok, g0:g0 + 64].unsqueeze(1).to_broadcast([ntok, 16, 64]),
           ALU.mult, [dst, rows], [dst])

    def stage_c(k_ap, k_b, v_ap, v_b, ntok, kT_dst, kT_dst_b, v_dst, v_dst_b):
        if getattr(cfg, "DBG_NOC", 0):
            return
        act(kv_b[0:ntok, :], k_ap, AF.Copy, [k_b], [kv_b])
        pb = pbank[0]
        for h in range(8):
            tr(bfv(pb)[:, h * 128:h * 128 + ntok], kv_b[0:ntok, h * 128:(h + 1) * 128], ntok, [kv_b], [pb])
        act(kT_st[:, :, 0:ntok], bfv(pb)[:, :].rearrange("p (h t) -> p h t", t=128)[:, :, 0:ntok], AF.Copy, [pb], [kT_st])
        dma("sp", kT_dst.rearrange("h d t -> d h t"), kT_st[:, :, 0:ntok], [kT_st], [kT_dst_b], "ks")
        act(kv_b[0:ntok, :], v_ap, AF.Copy, [v_b, kT_st], [kv_b])
        dma("sp", v_dst.rearrange("h p d -> p h d"), kv_b[0:ntok, :].rearrange("p (h d) -> p h d", d=128), [kv_b], [v_dst_b], "vs")

    gb = {"n": 0}

    def gbank():
        b = pbank[3 + gb["n"] % 5]
        gb["n"] += 1
        return b

    G_ph = Phase()
    qkvc = G_ph.a("qkvc", 64, [QKV_W])
    ba = G_ph.a("ba", 64, [16])
    gst = G_ph.a("gst", 64, [96])
    eGlB = G_ph.a("eGlB", 128, [8])
    knf = G_ph.a("knf", 64, [8, 128])
    kn_b = G_ph.a("kn_b", 64, [8, 128], BF16)
    kb_b = G_ph.a("kb_b", 64, [8, 128], BF16)
    kd_b = G_ph.a("kd_b", 64, [8, 128], BF16)
    qn_b = G_ph.a("qn_b", 64, [8, 128], BF16)
    qg_b = G_ph.a("qg_b", 64, [8, 128], BF16)
    vb_f = G_ph.a("vb_f", 64, [8, 128])
    knT = G_ph.a("knT", 128, [8, 64], BF16)
    kbT = G_ph.a("kbT", 128, [8, 64], BF16)
    qnT = G_ph.a("qnT", 128, [8, 64], BF16)
    qgT = G_ph.a("qgT", 128, [8, 64], BF16)
    gtri = G_ph.a("gtri", 64, [8, 64])
    dif = G_ph.a("dif", 64, [8, 64])
    earg = G_ph.a("earg", 64, [8, 64])
    Dsl = G_ph.a("Dsl", 64, [8, 64])
    DTsu = G_ph.a("DTsu", 64, [8, 64])
    DTui = G_ph.a("DTui", 64, [8, 64])
    intraT = G_ph.a("intraT", 64, [8, 64], BF16)
    Nb = [G_ph.a(f"Nb{i}", 64, [8, 64], BF16) for i in range(2)]
    Mb = [G_ph.a(f"Mb{i}", 64, [8, 64], BF16) for i in range(2)]
    Pf = G_ph.a("Pf", 64, [8, 64])
    Qf = G_ph.a("Qf", 64, [8, 64])
    Pb = [G_ph.a(f"Pb{i}", 64, [8, 64], BF16) for i in range(2)]
    Qb = [G_ph.a(f"Qb{i}", 64, [8, 64], BF16) for i in range(2)]
    tmpks = G_ph.a("tmpks", 64, [8, 128])
    rhs2 = G_ph.a("rhs2", 64, [8, 128], BF16)
    vnew_b = G_ph.a("vnew_b", 64, [8, 128], BF16)
    o_ch = [G_ph.a(f"o_ch{i}", 64, [8, 128], BF16) for i in range(2)]

    def bc8(buf, lo, C, n):
        return buf[0:C, lo:lo + 8].unsqueeze(2).to_broadcast([C, 8, n])

    def hv(ap, d):
        return ap.rearrange("p (h d) -> p h d", d=d)

    def l2norm_cols(c0, col, C, scale):
        act(tmpks[0:C, :, :].rearrange("p h d -> p (h d)"), qkvc[0:C, c0:c0 + 1024], AF.Square, [qkvc], [tmpks])
        T.op("dve", lambda e: e.tensor_reduce(out=gst[0:C, col:col + 8], in_=tmpks[0:C, :, :], axis=AX.X, op=ALU.add),
             [tmpks], [gst])
        ts(gst[0:C, col:col + 8], gst[0:C, col:col + 8], cfg.EPS, None, ALU.add, None, [gst], [gst])
        act(gst[0:C, col:col + 8], gst[0:C, col:col + 8], AF.Sqrt, [gst], [gst])
        T.op("dve", lambda e: e.reciprocal(out=gst[0:C, col:col + 8], in_=gst[0:C, col:col + 8]), [gst], [gst])
        if scale != 1.0:
            ts(gst[0:C, col:col + 8], gst[0:C, col:col + 8], scale, None, ALU.mult, None, [gst], [gst])

    def gdn_chunk(C, ntok, c, own, qi, inj):
        G = gcs[C]

        def hook():
            if inj:
                inj.pop(0)()
        nd = {64: 5, 16: 3}[C]
        m3 = lambda ap: ap.rearrange("p (h c) -> p h c", c=C)
        g3 = lambda nm: G[nm][0:C, 0:C].unsqueeze(1).to_broadcast([C, 8, C])
        for cb in (range(0, 6) if own else range(2, 6)):
            pi = cb % 2
            cs_ = slice(cb * 512, (cb + 1) * 512)
            tt(prodb[pi][0:ntok], qkv_b[qi][0:ntok, cs_].unsqueeze(1).to_broadcast([ntok, 4, 512]), wrows[0:ntok, :, cs_],
               ALU.mult, [qkv_b[qi], wrows], [prodb[pi]])
            lst = [(G["shA"][0:ntok, (c * 4 + j) * C:(c * 4 + j + 1) * C], prodb[pi][0:ntok, j, :], [G["shA"], prodb[pi]])
                   for j in range(4)]
            if c == 0:
                tt(tailpb[pi][:], tail[:, cs_].unsqueeze(1).to_broadcast([3, 3, 512]), wrows[0:3, 0:3, cs_], ALU.mult,
                   [tail, wrows], [tailpb[pi]])
                lst += [(G["shB"][0:3, j * C:(j + 1) * C], tailpb[pi][0:3, j, :], [G["shB"], tailpb[pi]]) for j in range(3)]
            pb = gbank()
            for i, (l, r, rd) in enumerate(lst):
                mm(pb[0:C, :], l, r, i == 0, i == len(lst) - 1, rd, [pb])
            act(qkvc[0:C, cs_], pb[0:C, :], AF.Silu, [pb], [qkvc])
            hook()
        pb = gbank()
        mm(pb[0:C, 0:16], G["sel"][0:ntok, c * C:(c + 1) * C], ba_raw[qi][0:ntok, :], True, True, [G["sel"], ba_raw[qi]], [pb])
        T.op("dve", lambda e, pb=pb: e.tensor_copy(out=ba[0:C, :], in_=pb[0:C, 0:16]), [pb], [ba])
        act(gst[0:C, 0:8], ba[0:C, 0:8], AF.Sigmoid, [ba], [gst])
        tt(gst[0:C, 8:16], ba[0:C, 8:16], rows[0:C, 136:144], ALU.add, [ba, rows], [gst])
        act(gst[0:C, 8:16], gst[0:C, 8:16], AF.Exp, [gst], [gst])
        act(gst[0:C, 8:16], gst[0:C, 8:16], AF.Ln, [gst], [gst], bias=1.0)
        tt(gst[0:C, 8:16], gst[0:C, 8:16], negA[0:C, :], ALU.mult, [gst, negA], [gst])
        pb = gbank()
        mm(pb[0:C, 0:8], G["tri"][0:C, 0:C], gst[0:C, 8:16], True, True, [G["tri"], gst], [pb])
        mm(pb[0:C, 8:16], ones_f[0:C, 0:C], gst[0:C, 8:16], True, True, [ones_f, gst], [pb])
        mm(pb[0:128, 16:24], ones_f[0:C, 0:128], gst[0:C, 8:16], True, True, [ones_f, gst], [pb])
        T.op("dve", lambda e, pb=pb: e.tensor_copy(out=gst[0:C, 16:32], in_=pb[0:C, 0:16]), [pb], [gst])
        act(eGlB[:, :], pb[0:128, 16:24], AF.Exp, [pb], [eGlB])
        act(gst[0:C, 32:40], gst[0:C, 16:24], AF.Exp, [gst], [gst])
        tt(gst[0:C, 40:48], gst[0:C, 32:40], gst[0:C, 0:8], ALU.mult, [gst], [gst])
        tt(gst[0:C, 48:56], gst[0:C, 24:32], gst[0:C, 16:24], ALU.subtract, [gst], [gst])
        act(gst[0:C, 48:56], gst[0:C, 48:56], AF.Exp, [gst], [gst])
        hook()
        l2norm_cols(1024, 56, C, 1.0)
        kview = hv(qkvc[0:C, 1024:2048], 128)
        vview = hv(qkvc[0:C, 2048:3072], 128)
        tt(knf[0:C], kview, bc8(gst, 56, C, 128), ALU.mult, [qkvc, gst], [knf])
        act(kn_b[0:C], knf[0:C], AF.Copy, [knf], [kn_b])
        tt(kb_b[0:C], knf[0:C], bc8(gst, 0, C, 128), ALU.mult, [knf, gst], [kb_b])
        tt(kd_b[0:C], knf[0:C], bc8(gst, 48, C, 128), ALU.mult, [knf, gst], [kd_b])
        tt(vb_f[0:C], vview, bc8(gst, 0, C, 128), ALU.mult, [qkvc, gst], [vb_f])
        pairs = [(kn_b, knT), (kb_b, kbT)]
        if own:
            l2norm_cols(0, 64, C, 128.0 ** -0.5)
            qview = hv(qkvc[0:C, 0:1024], 128)
            tt(knf[0:C], qview, bc8(gst, 64, C, 128), ALU.mult, [qkvc, gst, kn_b, kb_b, kd_b], [knf])
            act(qn_b[0:C], knf[0:C], AF.Copy, [knf], [qn_b])
            tt(qg_b[0:C], knf[0:C], bc8(gst, 32, C, 128), ALU.mult, [knf, gst], [qg_b])
            pairs += [(qn_b, qnT), (qg_b, qgT)]
        for src, dstT in pairs:
            pb = gbank()
            pbv = bfv(pb)
            for h in range(8):
                tr(pbv[:, h * C:(h + 1) * C], src[0:C, h, :], C, [src], [pb])
            act(dstT[:, :, 0:C], m3(pbv[:, 0:8 * C]), AF.Copy, [pb], [dstT])
        hook()
        tt(gtri[0:C, :, 0:C], g3("tri"), bc8(gst, 8, C, C), ALU.mult, [G["tri"], gst], [gtri])
        pbG = gbank()
        for h in range(8):
            mm(pbG[0:C, h * C:(h + 1) * C], ones_f[0:C, 0:C], gtri[0:C, h, 0:C], True, True, [ones_f, gtri], [pbG])
        tt(dif[0:C, :, 0:C], m3(pbG[0:C, 0:8 * C]), bc8(gst, 16, C, C), ALU.subtract, [pbG, gst], [dif])
        ts(earg[0:C, :, 0:C], dif[0:C, :, 0:C], 0.0, -1.0, ALU.max, ALU.mult, [dif], [earg])
        act(earg[0:C, :, 0:C], earg[0:C, :, 0:C], AF.Exp, [earg], [earg])
        tt(Dsl[0:C, :, 0:C], earg[0:C, :, 0:C], g3("msl"), ALU.mult, [earg, G["msl"]], [Dsl])
        ts(earg[0:C, :, 0:C], dif[0:C, :, 0:C], 0.0, None, ALU.min, None, [dif, Dsl], [earg])
        act(earg[0:C, :, 0:C], earg[0:C, :, 0:C], AF.Exp, [earg], [earg])
        tt(DTsu[0:C, :, 0:C], earg[0:C, :, 0:C], g3("msu"), ALU.mult, [earg, G["msu"]], [DTsu])
        if own:
            tt(DTui[0:C, :, 0:C], earg[0:C, :, 0:C], g3("mui"), ALU.mult, [earg, G["mui"]], [DTui])
            pb = gbank()
            for h in range(8):
                mm(pb[0:C, h * C:(h + 1) * C], knT[:, h, 0:C], qnT[:, h, 0:C], True, True, [knT, qnT], [pb])
            tt(intraT[0:C, :, 0:C], m3(pb[0:C, 0:8 * C]), DTui[0:C, :, 0:C], ALU.mult, [pb, DTui], [intraT])
        hook()
        for (la, ra, Dm, outb, accf, accb) in ((kbT, knT, Dsl, Nb[0], Qf, Qb[0]), (knT, kbT, DTsu, Mb[0], Pf, Pb[0])):
            pb = gbank()
            for h in range(8):
                mm(pb[0:C, h * C:(h + 1) * C], la[:, h, 0:C], ra[:, h, 0:C], True, True, [la, ra], [pb])
            T.op("dve", lambda e, pb=pb, Dm=Dm, outb=outb: e.scalar_tensor_tensor(
                out=outb[0:C, :, 0:C], in0=m3(pb[0:C, 0:8 * C]), scalar=-1.0, in1=Dm[0:C, :, 0:C], op0=ALU.mult,
                op1=ALU.mult), [pb, Dm], [outb])
            tt(accf[0:C, :, 0:C], outb[0:C, :, 0:C], g3("id8"), ALU.add, [outb, G["id8"]], [accf])
            act(accb[0:C, :, 0:C], accf[0:C, :, 0:C], AF.Copy, [accf], [accb])
        hook()
        cur = 0
        for k in range(1, nd + 1):
            nxt = 1 - cur
            last = (k == nd)
            pbN, pbM = gbank(), gbank()
            for h in range(8):
                if not last:
                    mm(pbN[0:C, h * C:(h + 1) * C], Mb[cur][0:C, h, 0:C], Nb[cur][0:C, h, 0:C], True, True,
                       [Mb[cur], Nb[cur]], [pbN])
                mm(pbM[0:C, h * C:(h + 1) * C], Nb[cur][0:C, h, 0:C], Mb[cur][0:C, h, 0:C], True, True,
                   [Mb[cur], Nb[cur]], [pbM])
            if not last:
                act(Nb[nxt][0:C, :, 0:C], m3(pbN[0:C, 0:8 * C]), AF.Copy, [pbN], [Nb[nxt]])
            T.op("dve", lambda e, nxt=nxt, pbM=pbM: e.tensor_copy(out=Mb[nxt][0:C, :, 0:C], in_=m3(pbM[0:C, 0:8 * C])),
                 [pbM], [Mb[nxt]])
            pbP, pbQ = gbank(), gbank()
            for h in range(8):
                mm(pbP[0:C, h * C:(h + 1) * C], Qb[cur][0:C, h, 0:C], Mb[nxt][0:C, h, 0:C], True, True,
                   [Qb[cur], Mb[nxt]], [pbP])
                if not last:
                    mm(pbQ[0:C, h * C:(h + 1) * C], Pb[cur][0:C, h, 0:C], Nb[nxt][0:C, h, 0:C], True, True,
                       [Pb[cur], Nb[nxt]], [pbQ])
            tt(Pf[0:C, :, 0:C], m3(pbP[0:C, 0:8 * C]), Pf[0:C, :, 0:C], ALU.add, [pbP, Pf], [Pf])
            act(Pb[nxt][0:C, :, 0:C], Pf[0:C, :, 0:C], AF.Copy, [Pf], [Pb[nxt]])
            if not last:
                tt(Qf[0:C, :, 0:C], m3(pbQ[0:C, 0:8 * C]), Qf[0:C, :, 0:C], ALU.add, [pbQ, Qf], [Qf])
                act(Qb[nxt][0:C, :, 0:C], Qf[0:C, :, 0:C], AF.Copy, [Qf], [Qb[nxt]])
            cur = nxt
            hook()
        PT = Pb[cur]
        for half in range(2):
            pb = gbank()
            for hh in range(4):
                h = half * 4 + hh
                mm(pb[0:C, hh * 128:(hh + 1) * 128], knT[:, h, 0:C], S_b[:, h, :], True, True, [knT, S_b], [pb])
            hs = slice(half * 4, half * 4 + 4)
            tt(tmpks[0:C, hs, :], hv(pb[0:C, :], 128),
               gst[0:C, 40 + half * 4:44 + half * 4].unsqueeze(2).to_broadcast([C, 4, 128]), ALU.mult, [pb, gst], [tmpks])
        tt(rhs2[0:C], vb_f[0:C], tmpks[0:C], ALU.subtract, [vb_f, tmpks], [rhs2])
        for half in range(2):
            pb = gbank()
            for hh in range(4):
                h = half * 4 + hh
                mm(pb[0:C, hh * 128:(hh + 1) * 128], PT[0:C, h, 0:C], rhs2[0:C, h, :], True, True, [PT, rhs2], [pb])
            hs = slice(half * 4, half * 4 + 4)
            act(vnew_b[0:C, hs, :], hv(pb[0:C, :], 128), AF.Copy, [pb], [vnew_b])
        hook()
        if own:
            for half in range(2):
                pb = gbank()
                for hh in range(4):
                    h = half * 4 + hh
                    mm(pb[0:C, hh * 128:(hh + 1) * 128], qgT[:, h, 0:C], S_b[:, h, :], True, False, [qgT, S_b], [pb])
                    mm(pb[0:C, hh * 128:(hh + 1) * 128], intraT[0:C, h, 0:C], vnew_b[0:C, h, :], False, True,
                       [intraT, vnew_b], [pb])
                hs = slice(half * 4, half * 4 + 4)
                act(o_ch[c][0:C, hs, :], hv(pb[0:C, :], 128), AF.Copy, [pb], [o_ch[c]])
        tt(S_f[:], S_f[:], eGlB[:, :].unsqueeze(2).to_broadcast([128, 8, 128]), ALU.mult, [S_f, eGlB], [S_f])
        for half in range(2):
            pb = gbank()
            for hh in range(4):
                h = half * 4 + hh
                mm(pb[:, hh * 128:(hh + 1) * 128], kd_b[0:C, h, :], vnew_b[0:C, h, :], True, True, [kd_b, vnew_b], [pb])
            hs = slice(half * 4, half * 4 + 4)
            tt(S_f[:, hs, :], hv(pb[:, :], 128), S_f[:, hs, :], ALU.add, [pb, S_f], [S_f])
        act(S_b[:], S_f[:], AF.Copy, [S_f], [S_b])
        hook()

    def gdn_tile(C, ntok, own, qi, inj):
        nch = ntok // C
        for c in range(nch):
            gdn_chunk(C, ntok, c, own, qi, inj)
        while inj:
            inj.pop(0)()
        if own:
            G = gcs[C]
            for half in range(2):
                pb = gbank()
                for c in range(nch):
                    mm(pb[0:ntok, :], G["plc"][0:C, c * ntok:(c + 1) * ntok],
                       o_ch[c][0:C, half * 4:half * 4 + 4, :].rearrange("p h d -> p (h d)"), c == 0, c == nch - 1,
                       [G["plc"], o_ch[c]], [pb])
                act(o_a[0:ntok, half * 4:half * 4 + 4, :], hv(pb[0:ntok, :], 128), AF.Copy, [pb], [o_a])

    A_ph = Phase()
    o_bn = A_ph.a("o_bn", 128, [8, 128])
    qn_f = A_ph.a("qn_f", 128, [1024])
    qT = A_ph.a("qT", 128, [8, 128], BF16)
    KTb = [A_ph.a(f"KTb{i}", 128, [KB * 128], BF16) for i in range(2)]
    Vb = [A_ph.a(f"Vb{i}", 128, [KB, 132], BF16) for i in range(2)]
    PTb = [[A_ph.a(f"PT{m}{i}", 128, [512], BF16) for i in range(2)] for m in range(2)]
    o_b = A_ph.a("o_b", 128, [8, 128])
    ast = A_ph.a("ast", 128, [16])

    def attention(ntq, nkt, kT_of, kT_b, v_of, v_b, bias_of, diag_kt):
        qknorm(qn_f, COLS["fq"][0], 0, ntq)
        act(kv_b[0:ntq, :], qn_f[0:ntq, :], AF.Copy, [qn_f], [kv_b])
        pb = pbank[6]
        for h in range(8):
            tr(bfv(pb)[:, h * 128:h * 128 + ntq], kv_b[0:ntq, h * 128:(h + 1) * 128], ntq, [kv_b], [pb])
        act(qT[:, :, 0:ntq], bfv(pb)[:, :].rearrange("p (h t) -> p h t", t=128)[:, :, 0:ntq], AF.Copy, [pb], [qT])
        for i in range(2):
            T.op("pool", lambda e, i=i: e.memset(Vb[i][:, :, 128:129], 1.0), [], [Vb[i]])
        lc = 0
        gcount = 0
        pend = []

        def flush():
            while pend:
                pend.pop(0)()

        for h in range(8):
            Ob = [pbank[4 + 2 * (h % 2)], pbank[5 + 2 * (h % 2)]]
            for b0 in range(0, nkt, KB):
                nb = min(KB, nkt - b0)
                bi = lc % 2
                lc += 1
                dma("sp", KTb[bi][:, 0:nb * 128], kT_of(h, b0, b0 + nb), [kT_b], [KTb[bi]], f"kt{bi}")
                dma("sp", Vb[bi][:, 0:nb, 0:128], v_of(h, b0, b0 + nb), [v_b], [Vb[bi]], f"vv{bi}")
                for g0 in range(0, nb, 4):
                    ng = min(4, nb - g0)
                    par = gcount % 2
                    gcount += 1
                    pvs = []
                    for m in range(2):
                        Sb = pbank[m * 2 + par]
                        PT = PTb[m][par]
                        ms = slice(m * 64, (m + 1) * 64)
                        for i in range(ng):
                            kt = g0 + i
                            mm(Sb[:, i * ntq:(i + 1) * ntq], KTb[bi][ms, kt * 128:(kt + 1) * 128], qT[ms, h, 0:ntq], True, True,
                               [KTb[bi], qT], [Sb])
                        biases = [bias_of(b0 + g0 + i) for i in range(ng)]
                        if any(bb is not None for bb in biases):
                            for i in range(ng):
                                kw = {} if biases[i] is None else {"bias": biases[i]}
                                act(PT[:, i * ntq:(i + 1) * ntq], Sb[:, i * ntq:(i + 1) * ntq], AF.Exp, [Sb, kmask, smask], [PT],
                                    scale=0.125, **kw)
                        else:
                            act(PT[:, 0:ng * ntq], Sb[:, 0:ng * ntq], AF.Exp, [Sb], [PT], scale=0.125)
                        for i in range(ng):
                            kt_abs = b0 + g0 + i
                            if kt_abs == diag_kt:
                                T.op("dve", lambda e, PT=PT, i=i: e.memset(PT[64:128, i * ntq:i * ntq + 64], 0.0), [], [PT])

                        def pv(m=m, PT=PT, bi=bi, g0=g0, ng=ng, b0=b0, Ob=Ob):
                            for i in range(ng):
                                kt = g0 + i
                                kt_abs = b0 + kt
                                mm(Ob[m][0:ntq, 0:129], PT[:, i * ntq:(i + 1) * ntq], Vb[bi][:, kt, 0:129], kt_abs == 0,
                                   kt_abs == nkt - 1, [PT, Vb[bi]], [Ob[m]])
                        pvs.append(pv)
                    flush()
                    pend.extend(pvs)

            def fin(h=h, Ob=Ob):
                T.op("dve", lambda e: e.reciprocal(out=ast[0:ntq, 0:1], in_=Ob[0][0:ntq, 128:129]), [Ob[0]], [ast])
                T.op("dve", lambda e: e.reciprocal(out=ast[0:ntq, 1:2], in_=Ob[1][0:ntq, 128:129]), [Ob[1]], [ast])
                tt(ast[0:ntq, 1:2], ast[0:ntq, 1:2], lam_t[0:ntq, 3:4], ALU.mult, [ast, lam_t], [ast])
                ts(o_b[0:ntq, h, :], Ob[0][0:ntq, 0:128], ast[0:ntq, 0:1], None, ALU.mult, None, [Ob[0], ast], [o_b])
                T.op("dve", lambda e: e.scalar_tensor_tensor(out=o_b[0:ntq, h, :], in0=Ob[1][0:ntq, 0:128],
                                                             scalar=ast[0:ntq, 1:2], in1=o_b[0:ntq, h, :], op0=ALU.mult,
                                                             op1=ALU.add), [Ob[1], ast, o_b], [o_b])
            pend.append(fin)
        flush()
        head_rms(o_b, o_bn, ntq, 528, 1.0 - cfg.LAM_INIT, ast, 8)

    def head_rms(src, dst, ntok, grow0, mul, stbuf, scol):
        act(kn[0:ntok, :], src[0:ntok].rearrange("p h d -> p (h d)"), AF.Square, [src], [kn])
        T.op("dve", lambda e: e.tensor_reduce(out=stbuf[0:ntok, scol:scol + 8], in_=hv(kn[0:ntok, :], 128), axis=AX.X,
                                              op=ALU.add), [kn], [stbuf])
        ts(stbuf[0:ntok, scol:scol + 8], stbuf[0:ntok, scol:scol + 8], 1.0 / 128, cfg.EPS, ALU.mult, ALU.add, [stbuf], [stbuf])
        act(stbuf[0:ntok, scol:scol + 8], stbuf[0:ntok, scol:scol + 8], AF.Sqrt, [stbuf], [stbuf])
        T.op("dve", lambda e: e.reciprocal(out=stbuf[0:ntok, scol:scol + 8], in_=stbuf[0:ntok, scol:scol + 8]), [stbuf], [stbuf])
        if mul != 1.0:
            ts(stbuf[0:ntok, scol:scol + 8], stbuf[0:ntok, scol:scol + 8], mul, None, ALU.mult, None, [stbuf], [stbuf])
        tt(dst[0:ntok], src[0:ntok], stbuf[0:ntok, scol:scol + 8].unsqueeze(2).to_broadcast([ntok, 8, 128]), ALU.mult,
           [src, stbuf], [dst])
        tt(dst[0:ntok], dst[0:ntok], rows[0:ntok, grow0:grow0 + 128].unsqueeze(1).to_broadcast([ntok, 8, 128]), ALU.mult,
           [dst, rows], [dst])

    P_ph = Phase()
    assert A_ph.off >= 0
    P_ph.off = 1024
    o_an = P_ph.a("o_an", 128, [8, 128])
    sg = P_ph.a("sg", 128, [1024])
    mg_b = P_ph.a("mg_b", 128, [1024], BF16)
    mgT = P_ph.a("mgT", 128, [8, 128], BF16)
    h_f = P_ph.a("h_f", 128, [1024])
    hn_b = P_ph.a("hn_b", 128, [1024], BF16)
    hnT = P_ph.a("hnT", 128, [8, 128], BF16)
    pq_b = P_ph.a("pq_b", 128, [2048], BF16)
    pqT = P_ph.a("pqT", 128, [16, 128], BF16)
    sc = P_ph.a("sc", 128, [16, 128])
    sc2 = P_ph.a("sc2", 128, [256])
    sv = P_ph.a("sv", 128, [16, 16])
    si = P_ph.a("si", 128, [16, 16], U32)
    sif = P_ph.a("sif", 128, [16, 16])
    cand = P_ph.a("cand", 128, [8, 256])
    cid = P_ph.a("cid", 128, [8, 256])
    tv = P_ph.a("tv", 128, [8, 16])
    eid = P_ph.a("eid", 128, [128])
    eid_i = P_ph.a("eid_i", 128, [128], I32)
    gate = P_ph.a("gate", 128, [8, 16])
    pst = P_ph.a("pst", 128, [32])
    actv = P_ph.a("actv", 128, [128])
    wgt = P_ph.a("wgt", 128, [128])
    gbuf = [P_ph.a(f"gbuf{i}", 128, [2048], BF16) for i in range(2)]
    dgb = [P_ph.a(f"dgb{i}", 128, [128], BF16) for i in range(2)]
    pjunk = P_ph.a("pjunk", 128, [1024], BF16)
    for nm_ in ("sc", "cand", "cid"):
        for i_ in range(2):
            o_ = P_ph.offs[nm_] + i_ * 1024
            gbuf.append(T.buf(arena[0:128, o_:o_ + 1024].bitcast(BF16), f"gal_{nm_}{i_}"))
    NG = len(gbuf)


    def merge_peer(ntok, xsrc, ydst):
        head_rms(o_a, o_an, ntok, 400, 1.0, pst, 0)
        act(sg[0:ntok, :], proj[0:ntok, COLS["z"][0] - PO:COLS["z"][1] - PO], AF.Silu, [proj], [sg])
        tt(o_an[0:ntok].rearrange("p h d -> p (h d)"), o_an[0:ntok].rearrange("p h d -> p (h d)"), sg[0:ntok, :], ALU.mult,
           [o_an, sg], [o_an])
        act(sg[0:ntok, :], proj[0:ntok, COLS["ga"][0] - PO:COLS["ga"][1] - PO], AF.Sigmoid, [proj, o_an], [sg])
        tt(o_an[0:ntok].rearrange("p h d -> p (h d)"), o_an[0:ntok].rearrange("p h d -> p (h d)"), sg[0:ntok, :], ALU.mult,
           [o_an, sg], [o_an])
        act(sg[0:ntok, :], proj[0:ntok, COLS["gb"][0] - PO:COLS["gb"][1] - PO], AF.Sigmoid, [proj, o_an], [sg])
        tt(sg[0:ntok, :], sg[0:ntok, :], o_bn[0:ntok].rearrange("p h d -> p (h d)"), ALU.mult, [sg, o_bn], [sg])
        tt(mg_b[0:ntok, :], sg[0:ntok, :], o_an[0:ntok].rearrange("p h d -> p (h d)"), ALU.add, [sg, o_an], [mg_b])
        pb = pbank[0]
        for kc in range(KC):
            tr(bfv(pb)[:, kc * 128:kc * 128 + ntok], mg_b[0:ntok, kc * 128:(kc + 1) * 128], ntok, [mg_b], [pb])
        act(mgT[:, :, 0:ntok], bfv(pb)[:, :].rearrange("p (h t) -> p h t", t=128)[:, :, 0:ntok], AF.Copy, [pb], [mgT])
        dma("sp", h_f[0:ntok, :], xsrc, [], [h_f], "hx")
        project(w_out, mgT, 0, D, ntok,
                lambda pb, b0, b1: tt(h_f[0:ntok, b0:b1], pb[0:ntok, 0:b1 - b0], h_f[0:ntok, b0:b1], ALU.add, [pb, h_f], [h_f]))
        act(pjunk[0:ntok, :], h_f[0:ntok, :], AF.Square, [h_f], [pjunk, stat], accum_out=stat[0:ntok, 0:1])
        rstd_of(0, 3, ntok, D)
        act(sg[0:ntok, :], h_f[0:ntok, :], AF.Copy, [h_f, stat], [sg], scale=stat[0:ntok, 3:4])
        tt(hn_b[0:ntok, :], sg[0:ntok, :], rows[0:ntok, 1024:2048], ALU.mult, [sg, rows], [hn_b])
        pb = pbank[0]
        for kc in range(KC):
            tr(bfv(pb)[:, kc * 128:kc * 128 + ntok], hn_b[0:ntok, kc * 128:(kc + 1) * 128], ntok, [hn_b], [pb])
        act(hnT[:, :, 0:ntok], bfv(pb)[:, :].rearrange("p (h t) -> p h t", t=128)[:, :, 0:ntok], AF.Copy, [pb], [hnT])
        project(w_pq, hnT, 0, 2048, ntok,
                lambda pb, b0, b1: act(pq_b[0:ntok, b0:b1], pb[0:ntok, 0:b1 - b0], AF.Copy, [pb], [pq_b]))
        for half in range(2):
            pb = pbank[3 + half]
            for j in range(8):
                hp = half * 8 + j
                tr(bfv(pb)[:, j * 128:j * 128 + ntok], pq_b[0:ntok, hp * 128:(hp + 1) * 128], ntok, [pq_b], [pb])
            act(pqT[:, half * 8:half * 8 + 8, 0:ntok], bfv(pb)[:, :].rearrange("p (h t) -> p h t", t=128)[:, :, 0:ntok],
                AF.Copy, [pb], [pqT])
        for q4 in range(4):
            pb = pbank[3 + q4]
            for j in range(4):
                hp = q4 * 4 + j
                mm(pb[0:ntok, j * 128:(j + 1) * 128], pqT[:, hp, 0:ntok], skT[:, hp, :], True, True, [pqT, skT], [pb])
            act(sc[0:ntok, q4 * 4:q4 * 4 + 4, :], hv(pb[0:ntok, :], 128), AF.Copy, [pb], [sc])

        def top16(vals_of, src_ap, srcb, n, dstv, dsti, g):
            T.op("dve", lambda e: e.max(out=dstv[0:ntok, g, 0:8], in_=src_ap), [srcb], [dstv])
            if dsti is not None:
                T.op("dve", lambda e: e.max_index(out=dsti[0:ntok, g, 0:8], in_max=dstv[0:ntok, g, 0:8], in_values=src_ap),
                     [srcb, dstv], [dsti])
            T.op("dve", lambda e: e.match_replace(out=sc2[0:ntok, 0:n], in_to_replace=dstv[0:ntok, g, 0:8], in_values=src_ap,
                                                  imm_value=-1.0e30), [srcb, dstv], [sc2])
            T.op("dve", lambda e: e.max(out=dstv[0:ntok, g, 8:16], in_=sc2[0:ntok, 0:n]), [sc2], [dstv])
            if dsti is not None:
                T.op("dve", lambda e: e.max_index(out=dsti[0:ntok, g, 8:16], in_max=dstv[0:ntok, g, 8:16],
                                                  in_values=sc2[0:ntok, 0:n]), [sc2, dstv], [dsti])

        for g in range(16):
            top16(None, sc[0:ntok, g, :], sc, 128, sv, si, g)
        T.op("dve", lambda e: e.tensor_copy(out=sif[0:ntok], in_=si[0:ntok]), [si], [sif])
        sv4 = sv[0:ntok].rearrange("p (h t) k -> p h t k", t=2)
        sif4 = sif[0:ntok].rearrange("p (h t) k -> p h t k", t=2)
        c4 = lambda b: b[0:ntok].rearrange("p h (a b) -> p h a b", b=16)
        tt(c4(cand), sv4[:, :, 0, :].unsqueeze(3).to_broadcast([ntok, 8, 16, 16]),
           sv4[:, :, 1, :].unsqueeze(2).to_broadcast([ntok, 8, 16, 16]), ALU.add, [sv], [cand])
        ts(sif4[:, :, 0, :], sif4[:, :, 0, :], float(cfg.NKEYS), None, ALU.mult, None, [sif], [sif])
        tt(c4(cid), sif4[:, :, 0, :].unsqueeze(3).to_broadcast([ntok, 8, 16, 16]),
           sif4[:, :, 1, :].unsqueeze(2).to_broadcast([ntok, 8, 16, 16]), ALU.add, [sif], [cid])
        for hh in range(8):
            top16(None, cand[0:ntok, hh, :], cand, 256, tv, None, hh)
        for hh in range(8):
            for k in range(16):
                T.op("dve", lambda e, hh=hh, k=k: e.scalar_tensor_tensor(
                    out=sc2[0:ntok, 0:256], in0=cand[0:ntok, hh, :], scalar=tv[0:ntok, hh, k:k + 1], in1=cid[0:ntok, hh, :],
                    op0=ALU.is_equal, op1=ALU.mult, accum_out=eid[0:ntok, hh * 16 + k:hh * 16 + k + 1]),
                    [cand, tv, cid], [sc2, eid])
        ts(eid[0:ntok, :], eid[0:ntok, :], float(cfg.NKEYS * cfg.NKEYS - 1), 0.0, ALU.min, ALU.max, [eid], [eid])
        T.op("dve", lambda e: e.tensor_copy(out=eid_i[0:ntok, :], in_=eid[0:ntok, :]), [eid], [eid_i])
        ts(pst[0:ntok, 0:8], tv[0:ntok, :, 0], -1.0, None, ALU.mult, None, [tv], [pst])
        for hh in range(8):
            act(gate[0:ntok, hh, :], tv[0:ntok, hh, :], AF.Exp, [tv, pst], [gate, pst], bias=pst[0:ntok, hh:hh + 1],
                accum_out=pst[0:ntok, 8 + hh:9 + hh])
        T.op("dve", lambda e: e.reciprocal(out=pst[0:ntok, 16:24], in_=pst[0:ntok, 8:16]), [pst], [pst])
        tt(gate[0:ntok], gate[0:ntok], pst[0:ntok, 16:24].unsqueeze(2).to_broadcast([ntok, 8, 16]), ALU.mult, [gate, pst], [gate])
        T.barrier()
        Y = [pbank[1], pbank[2]]
        actvB = [T.buf(actv.t, f"actvB{k}") for k in range(4)]
        g1B = [T.buf(wgt.t, f"g1B{k}") for k in range(4)]
        gflat = gate[0:ntok].rearrange("p h k -> p (h k)")

        def st_gather(s_):
            gbf = gbuf[s_ % NG]
            T.op("pool", lambda e: e.indirect_dma_start(
                out=gbf[0:ntok, :], out_offset=None, in_=uv_b[:, :],
                in_offset=bass.IndirectOffsetOnAxis(ap=eid_i[0:ntok, s_:s_ + 1], axis=0)), [eid_i, tconv_b], [gbf], dma=f"g{s_ % NG}")
            T.op("dve", lambda e: e.scalar_tensor_tensor(
                out=pjunk[0:ntok, :], in0=gbf[0:ntok, 0:D], scalar=1.0, in1=hn_b[0:ntok, :], op0=ALU.mult,
                op1=ALU.mult, accum_out=actv[0:ntok, s_:s_ + 1]), [gbf, hn_b], [pjunk, actvB[s_ % 4]])

        def st_gelu(s_):
            act(wgt[0:ntok, s_:s_ + 1], actv[0:ntok, s_:s_ + 1], AF.Gelu, [actvB[s_ % 4]], [g1B[s_ % 4]])

        def st_acc(s_):
            gbf = gbuf[s_ % NG]
            dg = dgb[s_ % 2]
            T.op("dve", lambda e: e.tensor_scalar(out=dg[0:ntok, 0:ntok], in0=ident_b[0:ntok, 0:ntok],
                                                  scalar1=wgt[0:ntok, s_:s_ + 1], scalar2=gflat[:, s_:s_ + 1],
                                                  op0=ALU.mult, op1=ALU.mult), [ident_b, g1B[s_ % 4], gate], [dg])
            for hb in range(2):
                mm(Y[hb][0:ntok, :], dg[0:ntok, 0:ntok], gbf[0:ntok, D + hb * 512:D + (hb + 1) * 512], s_ == 0, s_ == 127,
                   [dg, gbf], [Y[hb]])

        for it in range(128 + 2):
            if it < 128:
                st_gather(it)
            if 0 <= it - 1 < 128:
                st_gelu(it - 1)
            if 0 <= it - 2 < 128:
                st_acc(it - 2)
        for hb in range(2):
            tt(h_f[0:ntok, hb * 512:(hb + 1) * 512], Y[hb][0:ntok, :], h_f[0:ntok, hb * 512:(hb + 1) * 512], ALU.add,
               [Y[hb], h_f], [h_f])
        dma("sp", ydst, h_f[0:ntok, :], [h_f], [], "yo")

    T.op("pool", lambda e: e.memset(tail[:], 0.0), [], [tail])
    T.op("pool", lambda e: e.memset(S_f[:], 0.0), [], [S_f])
    T.op("pool", lambda e: e.memset(S_b[:], 0.0), [], [S_b])

    def own_tail(ntq, nkt, kT_of, kT_b, v_of, v_b, bias_of, diag_kt, xsrc, ydst):
        lvl = getattr(cfg, "DBG", 9)
        if lvl == 0:
            return
        T.barrier()
        if lvl == 1:
            attention(ntq, nkt, kT_of, kT_b, v_of, v_b, bias_of, diag_kt)
            T.barrier()
            return
        attention(ntq, nkt, kT_of, kT_b, v_of, v_b, bias_of, diag_kt)
        T.barrier()
        merge_peer(ntq, xsrc, ydst)
        T.barrier()

    FV = slice(COLS["fv"][0] - PO, COLS["fv"][1] - PO)

    def front_pieces(p):
        own = (p % 4 == 3)
        qi = p % 2
        xsrc = xp[p * 128:(p + 1) * 128, :]
        pcs = [lambda: stage_a(xsrc, 128)]
        ranges = [COLS["qkv"], (4096, 4112), (COLS["fk"][0], COLS["fv"][1])]
        if own:
            ranges += [COLS["z"], COLS["fq"], (COLS["ga"][0], COLS["gb"][1])]
        pcs += inproj_blocks(ranges, 128, qi)
        pcs.append(lambda: ba_copy(128, qi))
        pcs.append(lambda: qknorm(kn, COLS["fk"][0], 64, 128))
        ts_ = slice(p * 128, (p + 1) * 128)
        pcs.append(lambda: stage_c(kn[:, :], kn, proj[:, FV], proj, 128, kT_scr[:, :, ts_], kT_scr_b, v_scr[:, :, p, :], v_scr_b))
        if own:
            j = p // 4

            def outs():
                dma("sp", k_out[j * 128:(j + 1) * 128, :], kn[:, :], [kn], [], "ko")
                dma("sp", v_out[j * 128:(j + 1) * 128, :], proj[:, FV], [proj], [], "vo")
            pcs.append(outs)
        return pcs

    for f_ in front_pieces(0):
        f_()
    for p in range(NT):
        own = (p % 4 == 3)
        qi = p % 2
        if p == 1:
            convert_tables()
        xsrc = xp[p * 128:(p + 1) * 128, :]
        gdn_tile(64, 128, own, qi, front_pieces(p + 1) if p + 1 < NT else [])
        dma("sp", tail[:, :], qkv_b[qi][125:128, 0:QKV_W], [qkv_b[qi]] + tailpb, [tail], "tl")
        if p == NT - 1:
            for hf in range(2):
                dma("pool", conv_out[:, hf * 1536:(hf + 1) * 1536], qkv_b[qi][125:128, hf * 1536:(hf + 1) * 1536], [qkv_b[qi]], [], "co")
            dma("sp", s_fin.rearrange("h d e -> d h e"), S_f[:], [S_f], [], "so")
        if own:
            j = p // 4
            own_tail(128, p + 1,
                     lambda h, t0, t1: kT_scr[h, :, t0 * 128:t1 * 128], kT_scr_b,
                     lambda h, t0, t1: v_scr[h, :, t0:t1, :],
                     v_scr_b,
                     lambda kt: (kmask[:, kt:kt + 1] if kt < 3 else None), p,
                     xsrc, y_out[j * 128:(j + 1) * 128, :])

    T.op("pool", lambda e: e.memset(kT_st[:], 0.0), [], [kT_st])
    T.op("pool", lambda e: e.memset(kv_b[:], 0.0), [], [kv_b])
    dma("sp", kTs_scr[:, :, PAST:PAST + 128].rearrange("h d t -> d h t"), kT_st[:, :, :], [kT_st], [kTs_scr_b], "ks")
    dma("sp", vs_scr[:, :, NPT, :].rearrange("h p d -> p h d"), kv_b[:, :].rearrange("p (h d) -> p h d", d=128), [kv_b], [vs_scr_b], "vs")
    for sidx in range(NS):
        T.barrier()
        for kt in range(NPT):
            dma("sp", kn[:, :], ck_in[sidx, kt * 128:(kt + 1) * 128, :], [], [kn], "xk")
            dma("sp", xt[:, :], cv_in[sidx, kt * 128:(kt + 1) * 128, :], [], [xt], "x")
            ts_ = slice(kt * 128, (kt + 1) * 128)
            stage_c(kn[:, :], kn, xt[:, :], xt, 128, kTs_scr[:, :, ts_], kTs_scr_b, vs_scr[:, :, kt, :], vs_scr_b)
        xsrc = xs_in[sidx * LS:(sidx + 1) * LS, :]
        stage_a(xsrc, LS)
        for f_ in inproj_blocks([(0, IN_W)], LS, 0):
            f_()
        ba_copy(LS, 0)
        qknorm(kn, COLS["fk"][0], 64, LS)
        ts_ = slice(PAST, PAST + LS)
        stage_c(kn[0:LS, :], kn, proj[0:LS, FV], proj, LS, kTs_scr[:, :, ts_], kTs_scr_b,
                vs_scr[:, 0:LS, NPT, :], vs_scr_b)
        dma("sp", ks_out[sidx * LS:(sidx + 1) * LS, :], kn[0:LS, :], [kn], [], "ko")
        dma("sp", vs_out[sidx * LS:(sidx + 1) * LS, :], proj[0:LS, FV], [proj], [], "vo")
        for hf in range(2):
            dma("pool", cs_out[sidx, :, hf * 1536:(hf + 1) * 1536], qkv_b[0][LS - 3:LS, hf * 1536:(hf + 1) * 1536], [qkv_b[0]], [], "co")
        for hf in range(2):
            dma("pool", tail[:, hf * 1536:(hf + 1) * 1536], cs_in[sidx, :, hf * 1536:(hf + 1) * 1536], tailpb, [tail], "tl")
        dma("sp", S_f[:], ss_in[sidx].rearrange("h d e -> d h e"), [], [S_f], "si")
        act(S_b[:], S_f[:], AF.Copy, [S_f], [S_b])
        gdn_tile(LS, LS, True, 0, [])
        dma("sp", ss_out[sidx].rearrange("h d e -> d h e"), S_f[:], [S_f], [], "so")
        own_tail(LS, NPT + 1,
                 lambda h, t0, t1: kTs_scr[h, :, t0 * 128:t1 * 128], kTs_scr_b,
                 lambda h, t0, t1: vs_scr[h, :, t0:t1, :],
                 vs_scr_b,
                 lambda kt: (smask[:, 0:1] if kt == NPT else None), None,
                 xsrc, ys_out[sidx * LS:(sidx + 1) * LS, :])

    T.emit()
    st.close()
    return nc


def _gconst(C, ntok):
    bf = ml_dtypes.bfloat16
    nch = ntok // C
    shA = np.zeros((ntok, nch * 4 * C), np.float32)
    sel = np.zeros((ntok, nch * C), np.float32)
    plc = np.zeros((C, nch * ntok), np.float32)
    for c in range(nch):
        for j in range(4):
            for i in range(C):
                t = C * c + i + j - 3
                if 0 <= t < ntok:
                    shA[t, (c * 4 + j) * C + i] = 1.0
        for i in range(C):
            sel[C * c + i, c * C + i] = 1.0
            plc[i, c * ntok + C * c + i] = 1.0
    shB = np.zeros((3, 3 * C), np.float32)
    for j in range(3):
        for i in range(C):
            t = i + j
            if t < 3:
                shB[t, j * C + i] = 1.0
    ar = np.arange(C)
    tri = (ar[:, None] <= ar[None, :]).astype(np.float32)
    rep = lambda m: np.ascontiguousarray(m.astype(np.float32))
    return {f"shA{C}": shA.astype(bf), f"shB{C}": shB.astype(bf), f"sel{C}": sel, f"tri{C}": tri,
            f"msl{C}": rep((ar[:, None] > ar[None, :]).astype(np.float32)),
            f"msu{C}": rep((ar[None, :] > ar[:, None]).astype(np.float32)),
            f"mui{C}": rep((ar[None, :] >= ar[:, None]).astype(np.float32)),
            f"id8{C}": rep(np.eye(C, dtype=np.float32)), f"plc{C}": plc.astype(bf)}


def _prep(cfg, inputs):
    NT = cfg.SEQ // 128
    f = lambda k: np.asarray(inputs[k], np.float32)
    rows = np.zeros((1, 2048), np.float32)
    rows[0, 0:64] = f("diff_q_norm_g")[0]
    rows[0, 64:128] = f("diff_k_norm_g")[0]
    rows[0, 128:136] = f("delta_a_log")[0]
    rows[0, 136:144] = f("delta_dt_bias")[0]
    rows[0, 144:208] = f("diff_lambda_q1")[0]
    rows[0, 208:272] = f("diff_lambda_k1")[0]
    rows[0, 272:336] = f("diff_lambda_q2")[0]
    rows[0, 336:400] = f("diff_lambda_k2")[0]
    rows[0, 400:528] = f("delta_norm_g")[0]
    rows[0, 528:656] = f("diff_norm_g")[0]
    rows[0, 1024:2048] = f("norm_ffn_g")[0]
    common = {
        "w_in": np.ascontiguousarray(f("w_in")[0]),
        "w_out": np.ascontiguousarray(f("w_out")[0]),
        "w_pq": np.ascontiguousarray(f("peer_w_q")[0]),
        "subk": np.ascontiguousarray(f("peer_sub_keys")[0].reshape(16 * 128, 128)),
        "peer_u": np.ascontiguousarray(f("peer_u")[0]),
        "peer_v": np.ascontiguousarray(f("peer_v")[0]),
        "g_mix": np.ascontiguousarray(f("norm_mix_g")[0].reshape(cfg.D // 128, 128).T),
        "rows": rows,
        "ident": np.eye(128, dtype=np.float32),
        "conv_w": np.ascontiguousarray(f("conv_w")[0]),
        **_gconst(64, 128), **_gconst(cfg.DEC_SEQ, cfg.DEC_SEQ),
    }
    smask = np.full((128, 1), NEG, np.float32)
    smask[:cfg.DEC_SEQ] = 0.0
    common["smask"] = smask
    xpr = f("x_prompt")
    in_maps = []
    for c in range(8):
        b, r = c // 4, c % 4
        lead = 3 - r
        xpad = np.zeros((NT * 128, cfg.D), np.float32)
        nreal = NT - lead
        xpad[lead * 128:] = xpr[b, :nreal * 128]
        kmask = np.zeros((128, NT), np.float32)
        kmask[:, :lead] = NEG
        m = dict(common)
        m.update({
            "xp": xpad, "kmask": kmask,
            "xs": np.ascontiguousarray(f("x_sample")[2 * c:2 * c + 2].reshape(-1, cfg.D)),
            "ss_in": np.ascontiguousarray(f("state_delta_s")[0][2 * c:2 * c + 2]),
            "cs_in": np.ascontiguousarray(f("state_delta_conv")[0][2 * c:2 * c + 2]),
            "ck_in": np.ascontiguousarray(f("cache_diff_k")[0][2 * c:2 * c + 2].reshape(2, cfg.PAST, 1024)),
            "cv_in": np.ascontiguousarray(f("cache_diff_v")[0][2 * c:2 * c + 2].reshape(2, cfg.PAST, 1024)),
        })
        in_maps.append(m)
    return NT, in_maps


def kernel(**inputs):
    cfg = Cfg
    NT, in_maps = _prep(cfg, inputs)
    nc = build(cfg, NT)
    res = run_bass_kernel_spmd(nc, in_maps, core_ids=list(range(8))).results
    B, S, Dm, H = cfg.BATCH, cfg.SEQ, cfg.D, cfg.H
    DB, DS = cfg.DEC_BATCH, cfg.DEC_SEQ
    f32 = np.float32
    y_p = np.zeros((B, NT, 128, Dm), f32)
    y_s = np.zeros((DB, DS, Dm), f32)
    k_p = np.zeros((1, B, NT, 128, 1024), f32)
    v_p = np.zeros((1, B, NT, 128, 1024), f32)
    s_p = np.zeros((1, B, H, 128, 128), f32)
    c_p = np.zeros((1, B, 3, QKV_W), f32)
    k_s = np.zeros((1, DB, DS, 1024), f32)
    v_s = np.zeros((1, DB, DS, 1024), f32)
    s_s = np.zeros((1, DB, H, 128, 128), f32)
    c_s = np.zeros((1, DB, 3, QKV_W), f32)
    for c in range(8):
        b, r = c // 4, c % 4
        o = res[c]
        y_p[b, r::4] = o["y_own"].reshape(NT // 4, 128, Dm)
        k_p[0, b, r::4] = o["k_own"].reshape(NT // 4, 128, 1024)
        v_p[0, b, r::4] = o["v_own"].reshape(NT // 4, 128, 1024)
        if r == 3:
            c_p[0, b] = o["conv_fin"]
            s_p[0, b] = o["s_fin"]
        y_s[2 * c:2 * c + 2] = o["ys"].reshape(2, DS, Dm)
        k_s[0, 2 * c:2 * c + 2] = o["ks"].reshape(2, DS, 1024)
        v_s[0, 2 * c:2 * c + 2] = o["vs"].reshape(2, DS, 1024)
        s_s[0, 2 * c:2 * c + 2] = o["ss"]
        c_s[0, 2 * c:2 * c + 2] = o["cs"]
    return (y_p.reshape(B, S, Dm), y_s, k_p.reshape(1, B, S, H, 2, 64), v_p.reshape(1, B, S, H, 128), s_p, c_p,
            k_s.reshape(1, DB, DS, H, 2, 64), v_s.reshape(1, DB, DS, H, 128), s_s, c_s)
```

```python
import numpy as np
import ml_dtypes
import concourse.bass as bass
import concourse.mybir as mybir
from concourse.alu_op_type import AluOpType as ALU
from concourse.bass_utils import run_bass_kernel_spmd

F32 = mybir.dt.float32
BF16 = mybir.dt.bfloat16
I32 = mybir.dt.int32
U32 = mybir.dt.uint32
AF = mybir.ActivationFunctionType
AX = mybir.AxisListType


class Cfg:
    D = 1024
    BATCH = 2
    SEQ = 16384
    DEC_BATCH = 16
    DEC_SEQ = 16
    PAST = 1024
    H = 8
    NKEYS = 128
    TOPK = 16
    EPS = 1e-6
    LAM_INIT = 0.8 - 0.6 * 1.0


EPOCH = 12000
COMPUTE = ("pe", "act", "dve", "pool")


class Buf:
    def __init__(self, ap_owner, name):
        self.t = ap_owner
        self.name = name
        self.w = {}
        self.r = {}

    def __getitem__(self, idx):
        return self.t[idx]


class Op:
    __slots__ = ("eng", "fn", "deps", "idx", "dsem", "dcum", "key")


class Trk:
    def __init__(self, nc):
        self.nc = nc
        self.lists = {e: [] for e in COMPUTE + ("sp",)}
        self.dma_sems = {}
        self.nev = {}
        self.last = {}
        self.bufs = []

    def buf(self, t, name):
        b = Buf(t, name)
        self.bufs.append(b)
        return b

    def barrier(self):
        deps = list(self.last.values())
        for e in COMPUTE + ("sp",):
            self.op(e, lambda eng: eng.nop(), extra=deps)

    def op(self, eng, fn, reads=(), writes=(), dma=None, extra=()):
        o = Op()
        o.eng = eng
        o.fn = fn
        o.dsem = dma
        o.key = ("dma", dma) if dma else eng
        deps = []
        for b in reads:
            for k, w in b.w.items():
                if self._need(o, k, raw=True):
                    deps.append(w)
        for b in writes:
            for k, w in b.w.items():
                if self._need(o, k, raw=False):
                    deps.append(w)
            for k, r in b.r.items():
                if self._need(o, k, raw=False):
                    deps.append(r)
        deps.extend(extra)
        o.deps = deps
        if not extra:
            self.last[o.key] = o
        lst = self.lists[eng]
        lst.append(o)
        if not dma:
            o.idx = self.nev.get(eng, 0)
            self.nev[eng] = o.idx + 1
        if dma:
            ent = self.dma_sems.setdefault(dma, [None, 0])
            ent[1] += 16
            o.dcum = ent[1]
        for b in reads:
            b.r[o.key] = o
        for b in writes:
            b.w[o.key] = o
        return o

    def _need(self, o, k, raw):
        if o.dsem or (isinstance(k, tuple)):
            return True
        if k != o.eng:
            return True
        if o.eng == "pe":
            return False
        return True

    def emit(self):
        nc = self.nc
        import contextlib
        with contextlib.ExitStack() as st:
            sems = {}
            for e in COMPUTE + ("sp",):
                n = self.nev.get(e, 0) // EPOCH + 1
                sems[e] = [st.enter_context(nc.semaphore(f"s_{e}_{i}")) for i in range(n)]
            for name, ent in self.dma_sems.items():
                ent[0] = st.enter_context(nc.semaphore(f"d_{name}"))
            block = st.enter_context(nc.Block())

            def target(dep):
                if dep.dsem:
                    return self.dma_sems[dep.dsem][0], dep.dcum
                return sems[dep.eng][dep.idx // EPOCH], dep.idx % EPOCH + 1

            def run(ename, eng):
                waited = {}
                for o in self.lists[ename]:
                    for d in o.deps:
                        s, v = target(d)
                        kk = id(s)
                        if waited.get(kk, 0) >= v:
                            continue
                        waited[kk] = v
                        eng.wait_ge(s, v)
                    ins = o.fn(eng)
                    if o.dsem:
                        ins.then_inc(self.dma_sems[o.dsem][0], 16)
                    else:
                        ins.then_inc(sems[ename][o.idx // EPOCH], 1)
                if ename == "sp":
                    for name, ent in self.dma_sems.items():
                        eng.wait_ge(ent[0], ent[1])
                    for e2 in COMPUTE:
                        n = self.nev.get(e2, 0)
                        if n:
                            eng.wait_ge(sems[e2][(n - 1) // EPOCH], (n - 1) % EPOCH + 1)

            @block.tensor
            def _(e):
                run("pe", e)

            @block.scalar
            def _(e):
                run("act", e)

            @block.vector
            def _(e):
                run("dve", e)

            @block.gpsimd
            def _(e):
                run("pool", e)

            @block.sync
            def _(e):
                run("sp", e)


QKV_W = 3072
COLS = dict(qkv=(0, 3072), z=(3072, 4096), beta=(4096, 4104), a=(4104, 4112),
            fq=(4112, 5136), fk=(5136, 6160), fv=(6160, 7184), ga=(7184, 8208), gb=(8208, 9232))
IN_W = 9232
ARENA_F32 = 18880
KB = 16
NEG = -1.0e4


def build(cfg, NT):
    nc = bass.Bass("TRN2", target_bir_lowering=False)
    D = cfg.D
    KC = D // 128
    T = Trk(nc)
    import contextlib
    st = contextlib.ExitStack()

    def din(name, shape, dt=F32):
        return nc.dram_tensor(name, list(shape), dt, kind="ExternalInput").ap()

    def dout(name, shape, dt=F32):
        return nc.dram_tensor(name, list(shape), dt, kind="ExternalOutput").ap()

    def dscr(name, shape, dt=BF16):
        return nc.dram_tensor(name, list(shape), dt, kind="Internal").ap()

    def sb(name, shape, dt=F32):
        t = st.enter_context(nc.sbuf_tensor(name, list(shape), dt))
        return T.buf(t, name)

    def ps(name, shape, dt=F32):
        t = st.enter_context(nc.psum_tensor(name, list(shape), dt))
        return T.buf(t, name)

    NOWN = NT // 4
    NS = cfg.DEC_BATCH // 8
    LS = cfg.DEC_SEQ
    PAST = cfg.PAST
    NPT = PAST // 128
    xp = din("xp", [NT * 128, D])
    kmask_in = din("kmask", [128, NT])
    xs_in = din("xs", [NS * LS, D])
    w_in = din("w_in", [D, IN_W])
    w_out = din("w_out", [D, D])
    w_pq = din("w_pq", [D, 2048])
    subk = din("subk", [16 * 128, 128])
    peer_u = din("peer_u", [cfg.NKEYS * cfg.NKEYS, D])
    peer_v = din("peer_v", [cfg.NKEYS * cfg.NKEYS, D])
    g_mix = din("g_mix", [128, KC])
    rows_in = din("rows", [1, 2048])
    ident_in = din("ident", [128, 128])
    conv_w = din("conv_w", [4, QKV_W])
    ss_in = din("ss_in", [NS, 8, 128, 128])
    cs_in = din("cs_in", [NS, 3, QKV_W])
    ck_in = din("ck_in", [NS, PAST, 1024])
    cv_in = din("cv_in", [NS, PAST, 1024])
    smask_in = din("smask", [128, 1])
    y_out = dout("y_own", [NOWN * 128, D])
    k_out = dout("k_own", [NOWN * 128, 1024])
    v_out = dout("v_own", [NOWN * 128, 1024])
    conv_out = dout("conv_fin", [3, QKV_W])
    s_fin = dout("s_fin", [8, 128, 128])
    ys_out = dout("ys", [NS * LS, D])
    ks_out = dout("ks", [NS * LS, 1024])
    vs_out = dout("vs", [NS * LS, 1024])
    cs_out = dout("cs", [NS, 3, QKV_W])
    ss_out = dout("ss", [NS, 8, 128, 128])
    gc = {}
    for C_ in (64, LS):
        ntk = 128 if C_ == 64 else LS
        nch = ntk // C_
        gc[C_] = dict(
            shA=din(f"shA{C_}", [ntk, nch * 4 * C_], BF16), shB=din(f"shB{C_}", [3, 3 * C_], BF16),
            sel=din(f"sel{C_}", [ntk, nch * C_]), tri=din(f"tri{C_}", [C_, C_]),
            msl=din(f"msl{C_}", [C_, C_]),
            msu=din(f"msu{C_}", [C_, C_]), mui=din(f"mui{C_}", [C_, C_]),
            id8=din(f"id8{C_}", [C_, C_]), plc=din(f"plc{C_}", [C_, nch * ntk], BF16))
    kT_scr = dscr("kT_scr", [8, 128, NT * 128])
    v_scr = dscr("v_scr", [8, 128, NT, 128])
    kTs_scr = dscr("kTs_scr", [8, 128, PAST + 128])
    vs_scr = dscr("vs_scr", [8, 128, NPT + 1, 128])
    w_in_b = dscr("w_in_b", [D, IN_W])
    w_out_b = dscr("w_out_b", [D, D])
    w_pq_b = dscr("w_pq_b", [D, 2048])
    uv_b = dscr("uv_b", [cfg.NKEYS * cfg.NKEYS, 2 * D])
    wconv_b = T.buf(None, "wconv")
    tconv_b = T.buf(None, "tconv")
    kT_scr_b = T.buf(None, "kT_scr")
    v_scr_b = T.buf(None, "v_scr")
    kTs_scr_b = T.buf(None, "kTs_scr")
    vs_scr_b = T.buf(None, "vs_scr")

    ident_f = sb("ident_f", [128, 128])
    ident_b = sb("ident_b", [128, 128], BF16)
    ones_f = sb("ones_f", [128, 128])
    gmix_t = sb("gmix_t", [128, KC])
    rows = sb("rows_t", [128, 2048])
    qg_t = rows
    negA = sb("negA", [128, 8])
    lam_t = sb("lam_t", [128, 8])
    kmask = sb("kmask_t", [128, NT])
    smask = sb("smask_t", [128, 1])
    xt = sb("xt", [128, D])
    xs_bf = sb("xs_bf", [128, D], BF16)
    junk = xs_bf
    xnT = sb("xnT", [128, KC, 128], BF16)
    stat = sb("stat", [128, 8])
    wbuf = [sb(f"wbuf{i}", [128, KC, 512], BF16) for i in range(2)]
    PO = QKV_W
    proj = sb("proj", [128, IN_W - PO])
    qkv_b = [sb(f"qkv_b{i}", [128, QKV_W], BF16) for i in range(2)]
    ba_raw = [sb(f"ba_raw{i}", [128, 16]) for i in range(2)]
    cur = {"i": 0}
    kn = sb("kn", [128, 1024])
    sq = xt
    kst = sb("kst", [128, 16])
    wrows = sb("wrows", [128, 4, QKV_W], BF16)
    skT = sb("skT", [128, 16, 128], BF16)
    gcs = {}
    for C_ in (64, LS):
        ntk = 128 if C_ == 64 else LS
        nch = ntk // C_
        gcs[C_] = dict(shA=sb(f"t_shA{C_}", [ntk, nch * 4 * C_], BF16), shB=sb(f"t_shB{C_}", [3, 3 * C_], BF16),
                       sel=sb(f"t_sel{C_}", [ntk, nch * C_]), tri=sb(f"t_tri{C_}", [C_, C_]),
                       msl=sb(f"t_msl{C_}", [C_, C_]),
                       msu=sb(f"t_msu{C_}", [C_, C_]), mui=sb(f"t_mui{C_}", [C_, C_]),
                       id8=sb(f"t_id8{C_}", [C_, C_]), plc=sb(f"t_plc{C_}", [C_, nch * ntk], BF16))
    prodb = [sb("prodb0", [128, 4, 512], BF16)] * 2
    tail = sb("tail", [3, QKV_W], BF16)
    tailpb = [sb("tailpb0", [3, 3, 512], BF16)] * 2
    S_f = sb("S_f", [128, 8, 128])
    S_b = sb("S_b", [128, 8, 128], BF16)
    o_a = sb("o_a", [128, 8, 128])
    kv_b = sb("kv_b", [128, 1024], BF16)
    kT_st = sb("kT_st", [128, 8, 128], BF16)
    arena = st.enter_context(nc.sbuf_tensor("arena", [128, ARENA_F32], F32))
    pbank = [ps(f"pb{i}", [128, 512]) for i in range(8)]

    def bfv(pb):
        return pb.t[:].bitcast(BF16)

    class Phase:
        def __init__(self):
            self.off = 0

        def a(self, name, parts, shape, dt=F32):
            n = 1
            for x in shape:
                n *= x
            words = (n * (2 if dt == BF16 else 4) + 3) // 4
            words = (words + 7) // 8 * 8
            assert self.off + words <= ARENA_F32, (name, self.off, words)
            v = arena[0:parts, self.off:self.off + words]
            self.offs = getattr(self, "offs", {})
            self.offs[name] = self.off
            self.off += words
            if dt != F32:
                v = v.bitcast(dt)
            v = v[:, 0:n]
            if len(shape) == 2:
                v = v.rearrange("p (a b) -> p a b", b=shape[1])
            elif len(shape) == 3:
                v = v.rearrange("p (a b c) -> p a b c", b=shape[1], c=shape[2])
            return T.buf(v, name)

    def dma(eng, out, in_, reads, writes, sem):
        return T.op(eng, lambda e: e.dma_start(out=out, in_=in_), reads=reads, writes=writes, dma=sem)

    def act(out, in_, func, reads, writes, **kw):
        return T.op("act", lambda e: e.activation(out=out, in_=in_, func=func, **kw), reads, writes)

    def tt(out, in0, in1, op, reads, writes, eng="dve"):
        return T.op(eng, lambda e: e.tensor_tensor(out=out, in0=in0, in1=in1, op=op), reads, writes)

    def ts(out, in0, s1, s2, op0, op1, reads, writes):
        if op1 is None:
            return T.op("dve", lambda e: e.tensor_scalar(out=out, in0=in0, scalar1=s1, scalar2=None, op0=op0), reads, writes)
        return T.op("dve", lambda e: e.tensor_scalar(out=out, in0=in0, scalar1=s1, scalar2=s2, op0=op0, op1=op1),
                    reads, writes)

    def mm(out, lhsT, rhs, start, stop, reads, writes):
        return T.op("pe", lambda e: e.matmul(out, lhsT=lhsT, rhs=rhs, start=start, stop=stop), reads, writes)

    def tr(out, in_, n, reads, writes):
        return T.op("pe", lambda e: e.transpose(out=out, in_=in_, identity=ident_b[0:n, 0:n]), list(reads) + [ident_b], writes)

    dma("sp", ident_f[:], ident_in[:, :], [], [ident_f], "c0")
    dma("sp", gmix_t[:], g_mix[:, :], [], [gmix_t], "c1")
    dma("sp", rows[:], rows_in[0:1, :].to_broadcast([128, 2048]), [], [rows], "c2")
    dma("sp", kmask[:], kmask_in[:, :], [], [kmask], "c3")
    dma("sp", smask[:], smask_in[:, :], [], [smask], "c3b")
    T.op("dve", lambda e: e.tensor_copy(out=ident_b[:], in_=ident_f[:]), [ident_f], [ident_b])
    for j in range(4):
        for hf in range(2):
            dma("pool", wrows[:, j, hf * 1536:(hf + 1) * 1536],
                conv_w[j:j + 1, hf * 1536:(hf + 1) * 1536].to_broadcast([128, 1536]), [], [wrows], "c4")
    ci = 7
    for C_ in gcs:
        for nm in gcs[C_]:
            dst = gcs[C_][nm]
            src = gc[C_][nm]
            dma("sp", dst[:], src[:, :], [], [dst], f"c{ci}")
            ci += 1
    act(negA[:], rows[:, 128:136], AF.Exp, [rows], [negA])
    ts(negA[:], negA[:], -1.0, None, ALU.mult, None, [negA], [negA])
    T.op("pool", lambda e: e.memset(ones_f[:], 1.0), [], [ones_f])
    tt(xs_bf[:, 0:64], rows[:, 144:208], rows[:, 208:272], ALU.mult, [rows], [xs_bf])
    T.op("dve", lambda e: e.tensor_reduce(out=lam_t[:, 0:1], in_=xs_bf[:, 0:64], axis=AX.X, op=ALU.add), [xs_bf], [lam_t])
    tt(xs_bf[:, 64:128], rows[:, 272:336], rows[:, 336:400], ALU.mult, [rows], [xs_bf])
    T.op("dve", lambda e: e.tensor_reduce(out=lam_t[:, 1:2], in_=xs_bf[:, 64:128], axis=AX.X, op=ALU.add), [xs_bf], [lam_t])
    act(lam_t[:, 0:2], lam_t[:, 0:2], AF.Exp, [lam_t], [lam_t])
    tt(lam_t[:, 2:3], lam_t[:, 0:1], lam_t[:, 1:2], ALU.subtract, [lam_t], [lam_t])
    ts(lam_t[:, 2:3], lam_t[:, 2:3], cfg.LAM_INIT, None, ALU.add, None, [lam_t], [lam_t])
    ts(lam_t[:, 3:4], lam_t[:, 2:3], -1.0, None, ALU.mult, None, [lam_t], [lam_t])
    for hp in range(16):
        dma("sp", xt[:, 0:128], subk[hp * 128:(hp + 1) * 128, :], [], [xt], "x")
        act(xs_bf[:, 0:128], xt[:, 0:128], AF.Copy, [xt], [xs_bf])
        tr(bfv(pbank[0])[:, 0:128], xs_bf[:, 0:128], 128, [xs_bf], [pbank[0]])
        act(skT[:, hp, :], bfv(pbank[0])[:, 0:128], AF.Copy, [pbank[0]], [skT])

    for (src, dst, ncol) in ((w_in, w_in_b, IN_W), (w_out, w_out_b, D), (w_pq, w_pq_b, 2048)):
        for c0 in range(0, ncol, 2048):
            c1 = min(c0 + 2048, ncol)
            dma("pool", dst[:, c0:c1], src[:, c0:c1], [], [wconv_b], "cvw")
    WSRC = {id(w_in): w_in_b, id(w_out): w_out_b, id(w_pq): w_pq_b}

    def convert_tables():
        for (src, c0) in ((peer_u, 0), (peer_v, D)):
            for r0 in range(0, cfg.NKEYS * cfg.NKEYS, 2048):
                dma("pool", uv_b[r0:r0 + 2048, c0:c0 + D], src[r0:r0 + 2048, :], [], [tconv_b], "cvt")

    wcount = {"n": 0}

    def load_w(src, c0, c1):
        i = wcount["n"] % 2
        wcount["n"] += 1
        wb = wbuf[i]
        n = c1 - c0
        srcb = WSRC[id(src)]
        T.op("pool", lambda e: e.dma_start(out=wb[:, :, 0:n],
                                          in_=srcb[:, c0:c1].rearrange("(kc p) n -> p kc n", p=128)),
             reads=[wconv_b], writes=[wb], dma=f"w{i}")
        return wb

    def rstd_of(col_ss, col_out, ntok, n):
        ts(stat[0:ntok, 6:7], stat[0:ntok, col_ss:col_ss + 1], 1.0 / n, cfg.EPS, ALU.mult, ALU.add, [stat], [stat])
        act(stat[0:ntok, 7:8], stat[0:ntok, 6:7], AF.Sqrt, [stat], [stat])
        T.op("dve", lambda e: e.reciprocal(out=stat[0:ntok, col_out:col_out + 1], in_=stat[0:ntok, 7:8]), [stat], [stat])

    def norm_T(src, srcb, dstT, ntok, gcol):
        act(junk[0:ntok, :], src, AF.Square, [srcb], [junk, stat], accum_out=stat[0:ntok, 0:1])
        rstd_of(0, 3, ntok, D)
        act(xs_bf[0:ntok, :], src, AF.Copy, [srcb, stat], [xs_bf], scale=stat[0:ntok, 3:4])
        pb = pbank[0]
        pbv = bfv(pb)
        for kc in range(KC):
            tr(pbv[:, kc * 128:kc * 128 + ntok], xs_bf[0:ntok, kc * 128:(kc + 1) * 128], ntok, [xs_bf], [pb])
        for kc in range(KC):
            if gcol is not None:
                ts(dstT[:, kc, 0:ntok], pbv[:, kc * 128:kc * 128 + ntok], gcol[:, kc:kc + 1], None, ALU.mult, None,
                   [pb, gmix_t], [dstT])
            else:
                act(dstT[:, kc, 0:ntok], pbv[:, kc * 128:kc * 128 + ntok], AF.Copy, [pb], [dstT])

    def stage_a(xsrc, ntok):
        dma("sp", xt[0:ntok, :], xsrc, [], [xt], "x")
        norm_T(xt[0:ntok, :], xt, xnT, ntok, gmix_t)

    pcount = {"n": 0}

    def proj_block(src, lhsT_buf, b0, b1, ntok, sink):
        n = b1 - b0
        wb = load_w(src, b0, b1)
        pb = pbank[1 + pcount["n"] % 2]
        pcount["n"] += 1
        for kc in range(KC):
            mm(pb[0:ntok, 0:n], lhsT_buf[:, kc, 0:ntok], wb[:, kc, 0:n], kc == 0, kc == KC - 1, [lhsT_buf, wb], [pb])
        sink(pb, b0, b1)

    def project(src, lhsT_buf, c0, c1, ntok, sink):
        for b0 in range(c0, c1, 512):
            proj_block(src, lhsT_buf, b0, min(b0 + 512, c1), ntok, sink)

    def inproj_sink(ntok, qi):
        def sink(pb, b0, b1):
            if b1 <= PO:
                act(qkv_b[qi][0:ntok, b0:b1], pb[0:ntok, 0:b1 - b0], AF.Copy, [pb], [qkv_b[qi]])
            else:
                act(proj[0:ntok, b0 - PO:b1 - PO], pb[0:ntok, 0:b1 - b0], AF.Copy, [pb], [proj])
        return sink

    def inproj_blocks(ranges, ntok, qi):
        out = []
        for (c0, c1) in ranges:
            for b0 in range(c0, c1, 512):
                out.append(lambda b0=b0, b1=min(b0 + 512, c1): proj_block(w_in, xnT, b0, b1, ntok, inproj_sink(ntok, qi)))
        return out

    def ba_copy(ntok, qi):
        T.op("dve", lambda e: e.tensor_copy(out=ba_raw[qi][0:ntok, :], in_=proj[0:ntok, 4096 - PO:4112 - PO]), [proj], [ba_raw[qi]])

    def qknorm(dst, c0, g0, ntok):
        v3 = lambda ap: ap.rearrange("p (g d) -> p g d", d=64)
        act(sq[0:ntok, :], proj[0:ntok, c0 - PO:c0 - PO + 1024], AF.Square, [proj], [sq])
        T.op("dve", lambda e: e.tensor_reduce(out=kst[0:ntok, :], in_=v3(sq[0:ntok, :]), axis=AX.X, op=ALU.add), [sq], [kst])
        ts(kst[0:ntok, :], kst[0:ntok, :], 1.0 / 64, cfg.EPS, ALU.mult, ALU.add, [kst], [kst])
        act(kst[0:ntok, :], kst[0:ntok, :], AF.Sqrt, [kst], [kst])
        T.op("dve", lambda e: e.reciprocal(out=kst[0:ntok, :], in_=kst[0:ntok, :]), [kst], [kst])
        tt(v3(dst[0:ntok, :]), v3(proj[0:ntok, c0 - PO:c0 - PO + 1024]), kst[0:ntok, :].unsqueeze(2).to_broadcast([ntok, 16, 64]),
           ALU.mult, [proj, kst], [dst])
        tt(v3(dst[0:ntok, :]), v3(dst[0:ntok, :]), rows[0:ntok, g0:g0 + 64].unsqueeze(1).to_broadcast([ntok, 16, 64]),
           ALU.mult, [dst, rows], [dst])

    def stage_c(k_ap, k_b, v_ap, v_b, ntok, kT_dst, kT_dst_b, v_dst, v_dst_b):
        if getattr(cfg, "DBG_NOC", 0):
            return
        act(kv_b[0:ntok, :], k_ap, AF.Copy, [k_b], [kv_b])
        pb = pbank[0]
        for h in range(8):
            tr(bfv(pb)[:, h * 128:h * 128 + ntok], kv_b[0:ntok, h * 128:(h + 1) * 128], ntok, [kv_b], [pb])
        act(kT_st[:, :, 0:ntok], bfv(pb)[:, :].rearrange("p (h t) -> p h t", t=128)[:, :, 0:ntok], AF.Copy, [pb], [kT_st])
        dma("sp", kT_dst.rearrange("h d t -> d h t"), kT_st[:, :, 0:ntok], [kT_st], [kT_dst_b], "ks")
        act(kv_b[0:ntok, :], v_ap, AF.Copy, [v_b, kT_st], [kv_b])
        dma("sp", v_dst.rearrange("h p d -> p h d"), kv_b[0:ntok, :].rearrange("p (h d) -> p h d", d=128), [kv_b], [v_dst_b], "vs")

    gb = {"n": 0}

    def gbank():
        b = pbank[3 + gb["n"] % 5]
        gb["n"] += 1
        return b

    G_ph = Phase()
    qkvc = G_ph.a("qkvc", 64, [QKV_W])
    ba = G_ph.a("ba", 64, [16])
    gst = G_ph.a("gst", 64, [96])
    eGlB = G_ph.a("eGlB", 128, [8])
    knf = G_ph.a("knf", 64, [8, 128])
    kn_b = G_ph.a("kn_b", 64, [8, 128], BF16)
    kb_b = G_ph.a("kb_b", 64, [8, 128], BF16)
    kd_b = G_ph.a("kd_b", 64, [8, 128], BF16)
    qn_b = G_ph.a("qn_b", 64, [8, 128], BF16)
    qg_b = G_ph.a("qg_b", 64, [8, 128], BF16)
    vb_f = G_ph.a("vb_f", 64, [8, 128])
    knT = G_ph.a("knT", 128, [8, 64], BF16)
    kbT = G_ph.a("kbT", 128, [8, 64], BF16)
    qnT = G_ph.a("qnT", 128, [8, 64], BF16)
    qgT = G_ph.a("qgT", 128, [8, 64], BF16)
    gtri = G_ph.a("gtri", 64, [8, 64])
    dif = G_ph.a("dif", 64, [8, 64])
    earg = G_ph.a("earg", 64, [8, 64])
    Dsl = G_ph.a("Dsl", 64, [8, 64])
    DTsu = G_ph.a("DTsu", 64, [8, 64])
    DTui = G_ph.a("DTui", 64, [8, 64])
    intraT = G_ph.a("intraT", 64, [8, 64], BF16)
    Nb = [G_ph.a(f"Nb{i}", 64, [8, 64], BF16) for i in range(2)]
    Mb = [G_ph.a(f"Mb{i}", 64, [8, 64], BF16) for i in range(2)]
    Pf = G_ph.a("Pf", 64, [8, 64])
    Qf = G_ph.a("Qf", 64, [8, 64])
    Pb = [G_ph.a(f"Pb{i}", 64, [8, 64], BF16) for i in range(2)]
    Qb = [G_ph.a(f"Qb{i}", 64, [8, 64], BF16) for i in range(2)]
    tmpks = G_ph.a("tmpks", 64, [8, 128])
    rhs2 = G_ph.a("rhs2", 64, [8, 128], BF16)
    vnew_b = G_ph.a("vnew_b", 64, [8, 128], BF16)
    o_ch = [G_ph.a(f"o_ch{i}", 64, [8, 128], BF16) for i in range(2)]

    def bc8(buf, lo, C, n):
        return buf[0:C, lo:lo + 8].unsqueeze(2).to_broadcast([C, 8, n])

    def hv(ap, d):
        return ap.rearrange("p (h d) -> p h d", d=d)

    def l2norm_cols(c0, col, C, scale):
        act(tmpks[0:C, :, :].rearrange("p h d -> p (h d)"), qkvc[0:C, c0:c0 + 1024], AF.Square, [qkvc], [tmpks])
        T.op("dve", lambda e: e.tensor_reduce(out=gst[0:C, col:col + 8], in_=tmpks[0:C, :, :], axis=AX.X, op=ALU.add),
             [tmpks], [gst])
        ts(gst[0:C, col:col + 8], gst[0:C, col:col + 8], cfg.EPS, None, ALU.add, None, [gst], [gst])
        act(gst[0:C, col:col + 8], gst[0:C, col:col + 8], AF.Sqrt, [gst], [gst])
        T.op("dve", lambda e: e.reciprocal(out=gst[0:C, col:col + 8], in_=gst[0:C, col:col + 8]), [gst], [gst])
        if scale != 1.0:
            ts(gst[0:C, col:col + 8], gst[0:C, col:col + 8], scale, None, ALU.mult, None, [gst], [gst])

    def gdn_chunk(C, ntok, c, own, qi, inj):
        G = gcs[C]

        def hook():
            if inj:
                inj.pop(0)()
        nd = {64: 5, 16: 3}[C]
        m3 = lambda ap: ap.rearrange("p (h c) -> p h c", c=C)
        g3 = lambda nm: G[nm][0:C, 0:C].unsqueeze(1).to_broadcast([C, 8, C])
        for cb in (range(0, 6) if own else range(2, 6)):
            pi = cb % 2
            cs_ = slice(cb * 512, (cb + 1) * 512)
            tt(prodb[pi][0:ntok], qkv_b[qi][0:ntok, cs_].unsqueeze(1).to_broadcast([ntok, 4, 512]), wrows[0:ntok, :, cs_],
               ALU.mult, [qkv_b[qi], wrows], [prodb[pi]])
            lst = [(G["shA"][0:ntok, (c * 4 + j) * C:(c * 4 + j + 1) * C], prodb[pi][0:ntok, j, :], [G["shA"], prodb[pi]])
                   for j in range(4)]
            if c == 0:
                tt(tailpb[pi][:], tail[:, cs_].unsqueeze(1).to_broadcast([3, 3, 512]), wrows[0:3, 0:3, cs_], ALU.mult,
                   [tail, wrows], [tailpb[pi]])
                lst += [(G["shB"][0:3, j * C:(j + 1) * C], tailpb[pi][0:3, j, :], [G["shB"], tailpb[pi]]) for j in range(3)]
            pb = gbank()
            for i, (l, r, rd) in enumerate(lst):
                mm(pb[0:C, :], l, r, i == 0, i == len(lst) - 1, rd, [pb])
            act(qkvc[0:C, cs_], pb[0:C, :], AF.Silu, [pb], [qkvc])
            hook()
        pb = gbank()
        mm(pb[0:C, 0:16], G["sel"][0:ntok, c * C:(c + 1) * C], ba_raw[qi][0:ntok, :], True, True, [G["sel"], ba_raw[qi]], [pb])
        T.op("dve", lambda e, pb=pb: e.tensor_copy(out=ba[0:C, :], in_=pb[0:C, 0:16]), [pb], [ba])
        act(gst[0:C, 0:8], ba[0:C, 0:8], AF.Sigmoid, [ba], [gst])
        tt(gst[0:C, 8:16], ba[0:C, 8:16], rows[0:C, 136:144], ALU.add, [ba, rows], [gst])
        act(gst[0:C, 8:16], gst[0:C, 8:16], AF.Exp, [gst], [gst])
        act(gst[0:C, 8:16], gst[0:C, 8:16], AF.Ln, [gst], [gst], bias=1.0)
        tt(gst[0:C, 8:16], gst[0:C, 8:16], negA[0:C, :], ALU.mult, [gst, negA], [gst])
        pb = gbank()
        mm(pb[0:C, 0:8], G["tri"][0:C, 0:C], gst[0:C, 8:16], True, True, [G["tri"], gst], [pb])
        mm(pb[0:C, 8:16], ones_f[0:C, 0:C], gst[0:C, 8:16], True, True, [ones_f, gst], [pb])
        mm(pb[0:128, 16:24], ones_f[0:C, 0:128], gst[0:C, 8:16], True, True, [ones_f, gst], [pb])
        T.op("dve", lambda e, pb=pb: e.tensor_copy(out=gst[0:C, 16:32], in_=pb[0:C, 0:16]), [pb], [gst])
        act(eGlB[:, :], pb[0:128, 16:24], AF.Exp, [pb], [eGlB])
        act(gst[0:C, 32:40], gst[0:C, 16:24], AF.Exp, [gst], [gst])
        tt(gst[0:C, 40:48], gst[0:C, 32:40], gst[0:C, 0:8], ALU.mult, [gst], [gst])
        tt(gst[0:C, 48:56], gst[0:C, 24:32], gst[0:C, 16:24], ALU.subtract, [gst], [gst])
        act(gst[0:C, 48:56], gst[0:C, 48:56], AF.Exp, [gst], [gst])
        hook()
        l2norm_cols(1024, 56, C, 1.0)
        kview = hv(qkvc[0:C, 1024:2048], 128)
        vview = hv(qkvc[0:C, 2048:3072], 128)
        tt(knf[0:C], kview, bc8(gst, 56, C, 128), ALU.mult, [qkvc, gst], [knf])
        act(kn_b[0:C], knf[0:C], AF.Copy, [knf], [kn_b])
        tt(kb_b[0:C], knf[0:C], bc8(gst, 0, C, 128), ALU.mult, [knf, gst], [kb_b])
        tt(kd_b[0:C], knf[0:C], bc8(gst, 48, C, 128), ALU.mult, [knf, gst], [kd_b])
        tt(vb_f[0:C], vview, bc8(gst, 0, C, 128), ALU.mult, [qkvc, gst], [vb_f])
        pairs = [(kn_b, knT), (kb_b, kbT)]
        if own:
            l2norm_cols(0, 64, C, 128.0 ** -0.5)
            qview = hv(qkvc[0:C, 0:1024], 128)
            tt(knf[0:C], qview, bc8(gst, 64, C, 128), ALU.mult, [qkvc, gst, kn_b, kb_b, kd_b], [knf])
            act(qn_b[0:C], knf[0:C], AF.Copy, [knf], [qn_b])
            tt(qg_b[0:C], knf[0:C], bc8(gst, 32, C, 128), ALU.mult, [knf, gst], [qg_b])
            pairs += [(qn_b, qnT), (qg_b, qgT)]
        for src, dstT in pairs:
            pb = gbank()
            pbv = bfv(pb)
            for h in range(8):
                tr(pbv[:, h * C:(h + 1) * C], src[0:C, h, :], C, [src], [pb])
            act(dstT[:, :, 0:C], m3(pbv[:, 0:8 * C]), AF.Copy, [pb], [dstT])
        hook()
        tt(gtri[0:C, :, 0:C], g3("tri"), bc8(gst, 8, C, C), ALU.mult, [G["tri"], gst], [gtri])
        pbG = gbank()
        for h in range(8):
            mm(pbG[0:C, h * C:(h + 1) * C], ones_f[0:C, 0:C], gtri[0:C, h, 0:C], True, True, [ones_f, gtri], [pbG])
        tt(dif[0:C, :, 0:C], m3(pbG[0:C, 0:8 * C]), bc8(gst, 16, C, C), ALU.subtract, [pbG, gst], [dif])
        ts(earg[0:C, :, 0:C], dif[0:C, :, 0:C], 0.0, -1.0, ALU.max, ALU.mult, [dif], [earg])
        act(earg[0:C, :, 0:C], earg[0:C, :, 0:C], AF.Exp, [earg], [earg])
        tt(Dsl[0:C, :, 0:C], earg[0:C, :, 0:C], g3("msl"), ALU.mult, [earg, G["msl"]], [Dsl])
        ts(earg[0:C, :, 0:C], dif[0:C, :, 0:C], 0.0, None, ALU.min, None, [dif, Dsl], [earg])
        act(earg[0:C, :, 0:C], earg[0:C, :, 0:C], AF.Exp, [earg], [earg])
        tt(DTsu[0:C, :, 0:C], earg[0:C, :, 0:C], g3("msu"), ALU.mult, [earg, G["msu"]], [DTsu])
        if own:
            tt(DTui[0:C, :, 0:C], earg[0:C, :, 0:C], g3("mui"), ALU.mult, [earg, G["mui"]], [DTui])
            pb = gbank()
            for h in range(8):
                mm(pb[0:C, h * C:(h + 1) * C], knT[:, h, 0:C], qnT[:, h, 0:C], True, True, [knT, qnT], [pb])
            tt(intraT[0:C, :, 0:C], m3(pb[0:C, 0:8 * C]), DTui[0:C, :, 0:C], ALU.mult, [pb, DTui], [intraT])
        hook()
        for (la, ra, Dm, outb, accf, accb) in ((kbT, knT, Dsl, Nb[0], Qf, Qb[0]), (knT, kbT, DTsu, Mb[0], Pf, Pb[0])):
            pb = gbank()
            for h in range(8):
                mm(pb[0:C, h * C:(h + 1) * C], la[:, h, 0:C], ra[:, h, 0:C], True, True, [la, ra], [pb])
            T.op("dve", lambda e, pb=pb, Dm=Dm, outb=outb: e.scalar_tensor_tensor(
                out=outb[0:C, :, 0:C], in0=m3(pb[0:C, 0:8 * C]), scalar=-1.0, in1=Dm[0:C, :, 0:C], op0=ALU.mult,
                op1=ALU.mult), [pb, Dm], [outb])
            tt(accf[0:C, :, 0:C], outb[0:C, :, 0:C], g3("id8"), ALU.add, [outb, G["id8"]], [accf])
            act(accb[0:C, :, 0:C], accf[0:C, :, 0:C], AF.Copy, [accf], [accb])
        hook()
        cur = 0
        for k in range(1, nd + 1):
            nxt = 1 - cur
            last = (k == nd)
            pbN, pbM = gbank(), gbank()
            for h in range(8):
                if not last:
                    mm(pbN[0:C, h * C:(h + 1) * C], Mb[cur][0:C, h, 0:C], Nb[cur][0:C, h, 0:C], True, True,
                       [Mb[cur], Nb[cur]], [pbN])
                mm(pbM[0:C, h * C:(h + 1) * C], Nb[cur][0:C, h, 0:C], Mb[cur][0:C, h, 0:C], True, True,
                   [Mb[cur], Nb[cur]], [pbM])
            if not last:
                act(Nb[nxt][0:C, :, 0:C], m3(pbN[0:C, 0:8 * C]), AF.Copy, [pbN], [Nb[nxt]])
            T.op("dve", lambda e, nxt=nxt, pbM=pbM: e.tensor_copy(out=Mb[nxt][0:C, :, 0:C], in_=m3(pbM[0:C, 0:8 * C])),
                 [pbM], [Mb[nxt]])
            pbP, pbQ = gbank(), gbank()
            for h in range(8):
                mm(pbP[0:C, h * C:(h + 1) * C], Qb[cur][0:C, h, 0:C], Mb[nxt][0:C, h, 0:C], True, True,
                   [Qb[cur], Mb[nxt]], [pbP])
                if not last:
                    mm(pbQ[0:C, h * C:(h + 1) * C], Pb[cur][0:C, h, 0:C], Nb[nxt][0:C, h, 0:C], True, True,
                       [Pb[cur], Nb[nxt]], [pbQ])
            tt(Pf[0:C, :, 0:C], m3(pbP[0:C, 0:8 * C]), Pf[0:C, :, 0:C], ALU.add, [pbP, Pf], [Pf])
            act(Pb[nxt][0:C, :, 0:C], Pf[0:C, :, 0:C], AF.Copy, [Pf], [Pb[nxt]])
            if not last:
                tt(Qf[0:C, :, 0:C], m3(pbQ[0:C, 0:8 * C]), Qf[0:C, :, 0:C], ALU.add, [pbQ, Qf], [Qf])
                act(Qb[nxt][0:C, :, 0:C], Qf[0:C, :, 0:C], AF.Copy, [Qf], [Qb[nxt]])
            cur = nxt
            hook()
        PT = Pb[cur]
        for half in range(2):
            pb = gbank()
            for hh in range(4):
                h = half * 4 + hh
                mm(pb[0:C, hh * 128:(hh + 1) * 128], knT[:, h, 0:C], S_b[:, h, :], True, True, [knT, S_b], [pb])
            hs = slice(half * 4, half * 4 + 4)
            tt(tmpks[0:C, hs, :], hv(pb[0:C, :], 128),
               gst[0:C, 40 + half * 4:44 + half * 4].unsqueeze(2).to_broadcast([C, 4, 128]), ALU.mult, [pb, gst], [tmpks])
        tt(rhs2[0:C], vb_f[0:C], tmpks[0:C], ALU.subtract, [vb_f, tmpks], [rhs2])
        for half in range(2):
            pb = gbank()
            for hh in range(4):
                h = half * 4 + hh
                mm(pb[0:C, hh * 128:(hh + 1) * 128], PT[0:C, h, 0:C], rhs2[0:C, h, :], True, True, [PT, rhs2], [pb])
            hs = slice(half * 4, half * 4 + 4)
            act(vnew_b[0:C, hs, :], hv(pb[0:C, :], 128), AF.Copy, [pb], [vnew_b])
        hook()
        if own:
            for half in range(2):
                pb = gbank()
                for hh in range(4):
                    h = half * 4 + hh
                    mm(pb[0:C, hh * 128:(hh + 1) * 128], qgT[:, h, 0:C], S_b[:, h, :], True, False, [qgT, S_b], [pb])
                    mm(pb[0:C, hh * 128:(hh + 1) * 128], intraT[0:C, h, 0:C], vnew_b[0:C, h, :], False, True,
                       [intraT, vnew_b], [pb])
                hs = slice(half * 4, half * 4 + 4)
                act(o_ch[c][0:C, hs, :], hv(pb[0:C, :], 128), AF.Copy, [pb], [o_ch[c]])
        tt(S_f[:], S_f[:], eGlB[:, :].unsqueeze(2).to_broadcast([128, 8, 128]), ALU.mult, [S_f, eGlB], [S_f])
        for half in range(2):
            pb = gbank()
            for hh in range(4):
                h = half * 4 + hh
                mm(pb[:, hh * 128:(hh + 1) * 128], kd_b[0:C, h, :], vnew_b[0:C, h, :], True, True, [kd_b, vnew_b], [pb])
            hs = slice(half * 4, half * 4 + 4)
            tt(S_f[:, hs, :], hv(pb[:, :], 128), S_f[:, hs, :], ALU.add, [pb, S_f], [S_f])
        act(S_b[:], S_f[:], AF.Copy, [S_f], [S_b])
        hook()

    def gdn_tile(C, ntok, own, qi, inj):
        nch = ntok // C
        for c in range(nch):
            gdn_chunk(C, ntok, c, own, qi, inj)
        while inj:
            inj.pop(0)()
        if own:
            G = gcs[C]
            for half in range(2):
                pb = gbank()
                for c in range(nch):
                    mm(pb[0:ntok, :], G["plc"][0:C, c * ntok:(c + 1) * ntok],
                       o_ch[c][0:C, half * 4:half * 4 + 4, :].rearrange("p h d -> p (h d)"), c == 0, c == nch - 1,
                       [G["plc"], o_ch[c]], [pb])
                act(o_a[0:ntok, half * 4:half * 4 + 4, :], hv(pb[0:ntok, :], 128), AF.Copy, [pb], [o_a])

    A_ph = Phase()
    o_bn = A_ph.a("o_bn", 128, [8, 128])
    qn_f = A_ph.a("qn_f", 128, [1024])
    qT = A_ph.a("qT", 128, [8, 128], BF16)
    KTb = [A_ph.a(f"KTb{i}", 128, [KB * 128], BF16) for i in range(2)]
    Vb = [A_ph.a(f"Vb{i}", 128, [KB, 132], BF16) for i in range(2)]
    PTb = [[A_ph.a(f"PT{m}{i}", 128, [512], BF16) for i in range(2)] for m in range(2)]
    o_b = A_ph.a("o_b", 128, [8, 128])
    ast = A_ph.a("ast", 128, [16])

    def attention(ntq, nkt, kT_of, kT_b, v_of, v_b, bias_of, diag_kt):
        qknorm(qn_f, COLS["fq"][0], 0, ntq)
        act(kv_b[0:ntq, :], qn_f[0:ntq, :], AF.Copy, [qn_f], [kv_b])
        pb = pbank[6]
        for h in range(8):
            tr(bfv(pb)[:, h * 128:h * 128 + ntq], kv_b[0:ntq, h * 128:(h + 1) * 128], ntq, [kv_b], [pb])
        act(qT[:, :, 0:ntq], bfv(pb)[:, :].rearrange("p (h t) -> p h t", t=128)[:, :, 0:ntq], AF.Copy, [pb], [qT])
        for i in range(2):
            T.op("pool", lambda e, i=i: e.memset(Vb[i][:, :, 128:129], 1.0), [], [Vb[i]])
        lc = 0
        gcount = 0
        pend = []

        def flush():
            while pend:
                pend.pop(0)()

        for h in range(8):
            Ob = [pbank[4 + 2 * (h % 2)], pbank[5 + 2 * (h % 2)]]
            for b0 in range(0, nkt, KB):
                nb = min(KB, nkt - b0)
                bi = lc % 2
                lc += 1
                dma("sp", KTb[bi][:, 0:nb * 128], kT_of(h, b0, b0 + nb), [kT_b], [KTb[bi]], f"kt{bi}")
                dma("sp", Vb[bi][:, 0:nb, 0:128], v_of(h, b0, b0 + nb), [v_b], [Vb[bi]], f"vv{bi}")
                for g0 in range(0, nb, 4):
                    ng = min(4, nb - g0)
                    par = gcount % 2
                    gcount += 1
                    pvs = []
                    for m in range(2):
                        Sb = pbank[m * 2 + par]
                        PT = PTb[m][par]
                        ms = slice(m * 64, (m + 1) * 64)
                        for i in range(ng):
                            kt = g0 + i
                            mm(Sb[:, i * ntq:(i + 1) * ntq], KTb[bi][ms, kt * 128:(kt + 1) * 128], qT[ms, h, 0:ntq], True, True,
                               [KTb[bi], qT], [Sb])
                        biases = [bias_of(b0 + g0 + i) for i in range(ng)]
                        if any(bb is not None for bb in biases):
                            for i in range(ng):
                                kw = {} if biases[i] is None else {"bias": biases[i]}
                                act(PT[:, i * ntq:(i + 1) * ntq], Sb[:, i * ntq:(i + 1) * ntq], AF.Exp, [Sb, kmask, smask], [PT],
                                    scale=0.125, **kw)
                        else:
                            act(PT[:, 0:ng * ntq], Sb[:, 0:ng * ntq], AF.Exp, [Sb], [PT], scale=0.125)
                        for i in range(ng):
                            kt_abs = b0 + g0 + i
                            if kt_abs == diag_kt:
                                T.op("dve", lambda e, PT=PT, i=i: e.memset(PT[64:128, i * ntq:i * ntq + 64], 0.0), [], [PT])

                        def pv(m=m, PT=PT, bi=bi, g0=g0, ng=ng, b0=b0, Ob=Ob):
                            for i in range(ng):
                                kt = g0 + i
                                kt_abs = b0 + kt
                                mm(Ob[m][0:ntq, 0:129], PT[:, i * ntq:(i + 1) * ntq], Vb[bi][:, kt, 0:129], kt_abs == 0,
                                   kt_abs == nkt - 1, [PT, Vb[bi]], [Ob[m]])
                        pvs.append(pv)
                    flush()
                    pend.extend(pvs)

            def fin(h=h, Ob=Ob):
                T.op("dve", lambda e: e.reciprocal(out=ast[0:ntq, 0:1], in_=Ob[0][0:ntq, 128:129]), [Ob[0]], [ast])
                T.op("dve", lambda e: e.reciprocal(out=ast[0:ntq, 1:2], in_=Ob[1][0:ntq, 128:129]), [Ob[1]], [ast])
                tt(ast[0:ntq, 1:2], ast[0:ntq, 1:2], lam_t[0:ntq, 3:4], ALU.mult, [ast, lam_t], [ast])
                ts(o_b[0:ntq, h, :], Ob[0][0:ntq, 0:128], ast[0:ntq, 0:1], None, ALU.mult, None, [Ob[0], ast], [o_b])
                T.op("dve", lambda e: e.scalar_tensor_tensor(out=o_b[0:ntq, h, :], in0=Ob[1][0:ntq, 0:128],
                                                             scalar=ast[0:ntq, 1:2], in1=o_b[0:ntq, h, :], op0=ALU.mult,
                                                             op1=ALU.add), [Ob[1], ast, o_b], [o_b])
            pend.append(fin)
        flush()
        head_rms(o_b, o_bn, ntq, 528, 1.0 - cfg.LAM_INIT, ast, 8)

    def head_rms(src, dst, ntok, grow0, mul, stbuf, scol):
        act(kn[0:ntok, :], src[0:ntok].rearrange("p h d -> p (h d)"), AF.Square, [src], [kn])
        T.op("dve", lambda e: e.tensor_reduce(out=stbuf[0:ntok, scol:scol + 8], in_=hv(kn[0:ntok, :], 128), axis=AX.X,
                                              op=ALU.add), [kn], [stbuf])
        ts(stbuf[0:ntok, scol:scol + 8], stbuf[0:ntok, scol:scol + 8], 1.0 / 128, cfg.EPS, ALU.mult, ALU.add, [stbuf], [stbuf])
        act(stbuf[0:ntok, scol:scol + 8], stbuf[0:ntok, scol:scol + 8], AF.Sqrt, [stbuf], [stbuf])
        T.op("dve", lambda e: e.reciprocal(out=stbuf[0:ntok, scol:scol + 8], in_=stbuf[0:ntok, scol:scol + 8]), [stbuf], [stbuf])
        if mul != 1.0:
            ts(stbuf[0:ntok, scol:scol + 8], stbuf[0:ntok, scol:scol + 8], mul, None, ALU.mult, None, [stbuf], [stbuf])
        tt(dst[0:ntok], src[0:ntok], stbuf[0:ntok, scol:scol + 8].unsqueeze(2).to_broadcast([ntok, 8, 128]), ALU.mult,
           [src, stbuf], [dst])
        tt(dst[0:ntok], dst[0:ntok], rows[0:ntok, grow0:grow0 + 128].unsqueeze(1).to_broadcast([ntok, 8, 128]), ALU.mult,
           [dst, rows], [dst])

    P_ph = Phase()
    assert A_ph.off >= 0
    P_ph.off = 1024
    o_an = P_ph.a("o_an", 128, [8, 128])
    sg = P_ph.a("sg", 128, [1024])
    mg_b = P_ph.a("mg_b", 128, [1024], BF16)
    mgT = P_ph.a("mgT", 128, [8, 128], BF16)
    h_f = P_ph.a("h_f", 128, [1024])
    hn_b = P_ph.a("hn_b", 128, [1024], BF16)
    hnT = P_ph.a("hnT", 128, [8, 128], BF16)
    pq_b = P_ph.a("pq_b", 128, [2048], BF16)
    pqT = P_ph.a("pqT", 128, [16, 128], BF16)
    sc = P_ph.a("sc", 128, [16, 128])
    sc2 = P_ph.a("sc2", 128, [256])
    sv = P_ph.a("sv", 128, [16, 16])
    si = P_ph.a("si", 128, [16, 16], U32)
    sif = P_ph.a("sif", 128, [16, 16])
    cand = P_ph.a("cand", 128, [8, 256])
    cid = P_ph.a("cid", 128, [8, 256])
    tv = P_ph.a("tv", 128, [8, 16])
    eid = P_ph.a("eid", 128, [128])
    eid_i = P_ph.a("eid_i", 128, [128], I32)
    gate = P_ph.a("gate", 128, [8, 16])
    pst = P_ph.a("pst", 128, [32])
    actv = P_ph.a("actv", 128, [128])
    wgt = P_ph.a("wgt", 128, [128])
    gbuf = [P_ph.a(f"gbuf{i}", 128, [2048], BF16) for i in range(2)]
    dgb = [P_ph.a(f"dgb{i}", 128, [128], BF16) for i in range(2)]
    pjunk = P_ph.a("pjunk", 128, [1024], BF16)
    for nm_ in ("sc", "cand", "cid"):
        for i_ in range(2):
            o_ = P_ph.offs[nm_] + i_ * 1024
            gbuf.append(T.buf(arena[0:128, o_:o_ + 1024].bitcast(BF16), f"gal_{nm_}{i_}"))
    NG = len(gbuf)


    def merge_peer(ntok, xsrc, ydst):
        head_rms(o_a, o_an, ntok, 400, 1.0, pst, 0)
        act(sg[0:ntok, :], proj[0:ntok, COLS["z"][0] - PO:COLS["z"][1] - PO], AF.Silu, [proj], [sg])
        tt(o_an[0:ntok].rearrange("p h d -> p (h d)"), o_an[0:ntok].rearrange("p h d -> p (h d)"), sg[0:ntok, :], ALU.mult,
           [o_an, sg], [o_an])
        act(sg[0:ntok, :], proj[0:ntok, COLS["ga"][0] - PO:COLS["ga"][1] - PO], AF.Sigmoid, [proj, o_an], [sg])
        tt(o_an[0:ntok].rearrange("p h d -> p (h d)"), o_an[0:ntok].rearrange("p h d -> p (h d)"), sg[0:ntok, :], ALU.mult,
           [o_an, sg], [o_an])
        act(sg[0:ntok, :], proj[0:ntok, COLS["gb"][0] - PO:COLS["gb"][1] - PO], AF.Sigmoid, [proj, o_an], [sg])
        tt(sg[0:ntok, :], sg[0:ntok, :], o_bn[0:ntok].rearrange("p h d -> p (h d)"), ALU.mult, [sg, o_bn], [sg])
        tt(mg_b[0:ntok, :], sg[0:ntok, :], o_an[0:ntok].rearrange("p h d -> p (h d)"), ALU.add, [sg, o_an], [mg_b])
        pb = pbank[0]
        for kc in range(KC):
            tr(bfv(pb)[:, kc * 128:kc * 128 + ntok], mg_b[0:ntok, kc * 128:(kc + 1) * 128], ntok, [mg_b], [pb])
        act(mgT[:, :, 0:ntok], bfv(pb)[:, :].rearrange("p (h t) -> p h t", t=128)[:, :, 0:ntok], AF.Copy, [pb], [mgT])
        dma("sp", h_f[0:ntok, :], xsrc, [], [h_f], "hx")
        project(w_out, mgT, 0, D, ntok,
                lambda pb, b0, b1: tt(h_f[0:ntok, b0:b1], pb[0:ntok, 0:b1 - b0], h_f[0:ntok, b0:b1], ALU.add, [pb, h_f], [h_f]))
        act(pjunk[0:ntok, :], h_f[0:ntok, :], AF.Square, [h_f], [pjunk, stat], accum_out=stat[0:ntok, 0:1])
        rstd_of(0, 3, ntok, D)
        act(sg[0:ntok, :], h_f[0:ntok, :], AF.Copy, [h_f, stat], [sg], scale=stat[0:ntok, 3:4])
        tt(hn_b[0:ntok, :], sg[0:ntok, :], rows[0:ntok, 1024:2048], ALU.mult, [sg, rows], [hn_b])
        pb = pbank[0]
        for kc in range(KC):
            tr(bfv(pb)[:, kc * 128:kc * 128 + ntok], hn_b[0:ntok, kc * 128:(kc + 1) * 128], ntok, [hn_b], [pb])
        act(hnT[:, :, 0:ntok], bfv(pb)[:, :].rearrange("p (h t) -> p h t", t=128)[:, :, 0:ntok], AF.Copy, [pb], [hnT])
        project(w_pq, hnT, 0, 2048, ntok,
                lambda pb, b0, b1: act(pq_b[0:ntok, b0:b1], pb[0:ntok, 0:b1 - b0], AF.Copy, [pb], [pq_b]))
        for half in range(2):
            pb = pbank[3 + half]
            for j in range(8):
                hp = half * 8 + j
                tr(bfv(pb)[:, j * 128:j * 128 + ntok], pq_b[0:ntok, hp * 128:(hp + 1) * 128], ntok, [pq_b], [pb])
            act(pqT[:, half * 8:half * 8 + 8, 0:ntok], bfv(pb)[:, :].rearrange("p (h t) -> p h t", t=128)[:, :, 0:ntok],
                AF.Copy, [pb], [pqT])
        for q4 in range(4):
            pb = pbank[3 + q4]
            for j in range(4):
                hp = q4 * 4 + j
                mm(pb[0:ntok, j * 128:(j + 1) * 128], pqT[:, hp, 0:ntok], skT[:, hp, :], True, True, [pqT, skT], [pb])
            act(sc[0:ntok, q4 * 4:q4 * 4 + 4, :], hv(pb[0:ntok, :], 128), AF.Copy, [pb], [sc])

        def top16(vals_of, src_ap, srcb, n, dstv, dsti, g):
            T.op("dve", lambda e: e.max(out=dstv[0:ntok, g, 0:8], in_=src_ap), [srcb], [dstv])
            if dsti is not None:
                T.op("dve", lambda e: e.max_index(out=dsti[0:ntok, g, 0:8], in_max=dstv[0:ntok, g, 0:8], in_values=src_ap),
                     [srcb, dstv], [dsti])
            T.op("dve", lambda e: e.match_replace(out=sc2[0:ntok, 0:n], in_to_replace=dstv[0:ntok, g, 0:8], in_values=src_ap,
                                                  imm_value=-1.0e30), [srcb, dstv], [sc2])
            T.op("dve", lambda e: e.max(out=dstv[0:ntok, g, 8:16], in_=sc2[0:ntok, 0:n]), [sc2], [dstv])
            if dsti is not None:
                T.op("dve", lambda e: e.max_index(out=dsti[0:ntok, g, 8:16], in_max=dstv[0:ntok, g, 8:16],
                                                  in_values=sc2[0:ntok, 0:n]), [sc2, dstv], [dsti])

        for g in range(16):
            top16(None, sc[0:ntok, g, :], sc, 128, sv, si, g)
        T.op("dve", lambda e: e.tensor_copy(out=sif[0:ntok], in_=si[0:ntok]), [si], [sif])
        sv4 = sv[0:ntok].rearrange("p (h t) k -> p h t k", t=2)
        sif4 = sif[0:ntok].rearrange("p (h t) k -> p h t k", t=2)
        c4 = lambda b: b[0:ntok].rearrange("p h (a b) -> p h a b", b=16)
        tt(c4(cand), sv4[:, :, 0, :].unsqueeze(3).to_broadcast([ntok, 8, 16, 16]),
           sv4[:, :, 1, :].unsqueeze(2).to_broadcast([ntok, 8, 16, 16]), ALU.add, [sv], [cand])
        ts(sif4[:, :, 0, :], sif4[:, :, 0, :], float(cfg.NKEYS), None, ALU.mult, None, [sif], [sif])
        tt(c4(cid), sif4[:, :, 0, :].unsqueeze(3).to_broadcast([ntok, 8, 16, 16]),
           sif4[:, :, 1, :].unsqueeze(2).to_broadcast([ntok, 8, 16, 16]), ALU.add, [sif], [cid])
        for hh in range(8):
            top16(None, cand[0:ntok, hh, :], cand, 256, tv, None, hh)
        eidB = [T.buf(eid.t, f"eidB{k}") for k in range(4)]
        jk = [sc2[0:ntok, 0:256], sif[0:ntok].rearrange("p g k -> p (g k)")]
        jkB = [sc2, sif]
        for hh in range(8):
            for k in range(16):
                s_ = hh * 16 + k
                T.op("dve", lambda e, hh=hh, k=k, s_=s_: e.scalar_tensor_tensor(
                    out=jk[s_ % 2], in0=cand[0:ntok, hh, :], scalar=tv[0:ntok, hh, k:k + 1], in1=cid[0:ntok, hh, :],
                    op0=ALU.is_equal, op1=ALU.mult, accum_out=eid[0:ntok, s_:s_ + 1]),
                    [cand, tv, cid], [jkB[s_ % 2], eidB[s_ % 4]])
        ts(eid[0:ntok, :], eid[0:ntok, :], float(cfg.NKEYS * cfg.NKEYS - 1), 0.0, ALU.min, ALU.max, eidB, [eid])
        T.op("dve", lambda e: e.tensor_copy(out=eid_i[0:ntok, :], in_=eid[0:ntok, :]), [eid], [eid_i])
        ts(pst[0:ntok, 0:8], tv[0:ntok, :, 0], -1.0, None, ALU.mult, None, [tv], [pst])
        for hh in range(8):
            act(gate[0:ntok, hh, :], tv[0:ntok, hh, :], AF.Exp, [tv, pst], [gate, pst], bias=pst[0:ntok, hh:hh + 1],
                accum_out=pst[0:ntok, 8 + hh:9 + hh])
        T.op("dve", lambda e: e.reciprocal(out=pst[0:ntok, 16:24], in_=pst[0:ntok, 8:16]), [pst], [pst])
        tt(gate[0:ntok], gate[0:ntok], pst[0:ntok, 16:24].unsqueeze(2).to_broadcast([ntok, 8, 16]), ALU.mult, [gate, pst], [gate])
        T.barrier()
        Y = [pbank[1], pbank[2]]
        actvB = [T.buf(actv.t, f"actvB{k}") for k in range(4)]
        g1B = [T.buf(wgt.t, f"g1B{k}") for k in range(4)]
        gflat = gate[0:ntok].rearrange("p h k -> p (h k)")

        def st_gather(s_):
            gbf = gbuf[s_ % NG]
            T.op("pool", lambda e: e.indirect_dma_start(
                out=gbf[0:ntok, :], out_offset=None, in_=uv_b[:, :],
                in_offset=bass.IndirectOffsetOnAxis(ap=eid_i[0:ntok, s_:s_ + 1], axis=0)), [eid_i, tconv_b], [gbf], dma=f"g{s_ % NG}")
            T.op("dve", lambda e: e.scalar_tensor_tensor(
                out=pjunk[0:ntok, :], in0=gbf[0:ntok, 0:D], scalar=1.0, in1=hn_b[0:ntok, :], op0=ALU.mult,
                op1=ALU.mult, accum_out=actv[0:ntok, s_:s_ + 1]), [gbf, hn_b], [pjunk, actvB[s_ % 4]])

        def st_gelu(s_):
            act(wgt[0:ntok, s_:s_ + 1], actv[0:ntok, s_:s_ + 1], AF.Gelu, [actvB[s_ % 4]], [g1B[s_ % 4]])

        def st_acc(s_):
            gbf = gbuf[s_ % NG]
            dg = dgb[s_ % 2]
            T.op("dve", lambda e: e.tensor_scalar(out=dg[0:ntok, 0:ntok], in0=ident_b[0:ntok, 0:ntok],
                                                  scalar1=wgt[0:ntok, s_:s_ + 1], scalar2=gflat[:, s_:s_ + 1],
                                                  op0=ALU.mult, op1=ALU.mult), [ident_b, g1B[s_ % 4], gate], [dg])
            for hb in range(2):
                mm(Y[hb][0:ntok, :], dg[0:ntok, 0:ntok], gbf[0:ntok, D + hb * 512:D + (hb + 1) * 512], s_ == 0, s_ == 127,
                   [dg, gbf], [Y[hb]])

        for it in range(128 + 2):
            if it < 128:
                st_gather(it)
            if 0 <= it - 1 < 128:
                st_gelu(it - 1)
            if 0 <= it - 2 < 128:
                st_acc(it - 2)
        for hb in range(2):
            tt(h_f[0:ntok, hb * 512:(hb + 1) * 512], Y[hb][0:ntok, :], h_f[0:ntok, hb * 512:(hb + 1) * 512], ALU.add,
               [Y[hb], h_f], [h_f])
        dma("sp", ydst, h_f[0:ntok, :], [h_f], [], "yo")

    T.op("pool", lambda e: e.memset(tail[:], 0.0), [], [tail])
    T.op("pool", lambda e: e.memset(S_f[:], 0.0), [], [S_f])
    T.op("pool", lambda e: e.memset(S_b[:], 0.0), [], [S_b])

    def own_tail(ntq, nkt, kT_of, kT_b, v_of, v_b, bias_of, diag_kt, xsrc, ydst):
        lvl = getattr(cfg, "DBG", 9)
        if lvl == 0:
            return
        T.barrier()
        if lvl == 1:
            attention(ntq, nkt, kT_of, kT_b, v_of, v_b, bias_of, diag_kt)
            T.barrier()
            return
        attention(ntq, nkt, kT_of, kT_b, v_of, v_b, bias_of, diag_kt)
        T.barrier()
        merge_peer(ntq, xsrc, ydst)
        T.barrier()

    FV = slice(COLS["fv"][0] - PO, COLS["fv"][1] - PO)

    def front_pieces(p):
        own = (p % 4 == 3)
        qi = p % 2
        xsrc = xp[p * 128:(p + 1) * 128, :]
        pcs = [lambda: stage_a(xsrc, 128)]
        ranges = [COLS["qkv"], (4096, 4112), (COLS["fk"][0], COLS["fv"][1])]
        if own:
            ranges += [COLS["z"], COLS["fq"], (COLS["ga"][0], COLS["gb"][1])]
        pcs += inproj_blocks(ranges, 128, qi)
        pcs.append(lambda: ba_copy(128, qi))
        pcs.append(lambda: qknorm(kn, COLS["fk"][0], 64, 128))
        ts_ = slice(p * 128, (p + 1) * 128)
        pcs.append(lambda: stage_c(kn[:, :], kn, proj[:, FV], proj, 128, kT_scr[:, :, ts_], kT_scr_b, v_scr[:, :, p, :], v_scr_b))
        if own:
            j = p // 4

            def outs():
                dma("sp", k_out[j * 128:(j + 1) * 128, :], kn[:, :], [kn], [], "ko")
                dma("sp", v_out[j * 128:(j + 1) * 128, :], proj[:, FV], [proj], [], "vo")
            pcs.append(outs)
        return pcs

    for f_ in front_pieces(0):
        f_()
    for p in range(NT):
        own = (p % 4 == 3)
        qi = p % 2
        if p == 1:
            convert_tables()
        xsrc = xp[p * 128:(p + 1) * 128, :]
        gdn_tile(64, 128, own, qi, front_pieces(p + 1) if p + 1 < NT else [])
        dma("sp", tail[:, :], qkv_b[qi][125:128, 0:QKV_W], [qkv_b[qi]] + tailpb, [tail], "tl")
        if p == NT - 1:
            for hf in range(2):
                dma("pool", conv_out[:, hf * 1536:(hf + 1) * 1536], qkv_b[qi][125:128, hf * 1536:(hf + 1) * 1536], [qkv_b[qi]], [], "co")
            dma("sp", s_fin.rearrange("h d e -> d h e"), S_f[:], [S_f], [], "so")
        if own:
            j = p // 4
            own_tail(128, p + 1,
                     lambda h, t0, t1: kT_scr[h, :, t0 * 128:t1 * 128], kT_scr_b,
                     lambda h, t0, t1: v_scr[h, :, t0:t1, :],
                     v_scr_b,
                     lambda kt: (kmask[:, kt:kt + 1] if kt < 3 else None), p,
                     xsrc, y_out[j * 128:(j + 1) * 128, :])

    T.op("pool", lambda e: e.memset(kT_st[:], 0.0), [], [kT_st])
    T.op("pool", lambda e: e.memset(kv_b[:], 0.0), [], [kv_b])
    dma("sp", kTs_scr[:, :, PAST:PAST + 128].rearrange("h d t -> d h t"), kT_st[:, :, :], [kT_st], [kTs_scr_b], "ks")
    dma("sp", vs_scr[:, :, NPT, :].rearrange("h p d -> p h d"), kv_b[:, :].rearrange("p (h d) -> p h d", d=128), [kv_b], [vs_scr_b], "vs")
    for sidx in range(NS):
        T.barrier()
        for kt in range(NPT):
            dma("sp", kn[:, :], ck_in[sidx, kt * 128:(kt + 1) * 128, :], [], [kn], "xk")
            dma("sp", xt[:, :], cv_in[sidx, kt * 128:(kt + 1) * 128, :], [], [xt], "x")
            ts_ = slice(kt * 128, (kt + 1) * 128)
            stage_c(kn[:, :], kn, xt[:, :], xt, 128, kTs_scr[:, :, ts_], kTs_scr_b, vs_scr[:, :, kt, :], vs_scr_b)
        xsrc = xs_in[sidx * LS:(sidx + 1) * LS, :]
        stage_a(xsrc, LS)
        for f_ in inproj_blocks([(0, IN_W)], LS, 0):
            f_()
        ba_copy(LS, 0)
        qknorm(kn, COLS["fk"][0], 64, LS)
        ts_ = slice(PAST, PAST + LS)
        stage_c(kn[0:LS, :], kn, proj[0:LS, FV], proj, LS, kTs_scr[:, :, ts_], kTs_scr_b,
                vs_scr[:, 0:LS, NPT, :], vs_scr_b)
        dma("sp", ks_out[sidx * LS:(sidx + 1) * LS, :], kn[0:LS, :], [kn], [], "ko")
        dma("sp", vs_out[sidx * LS:(sidx + 1) * LS, :], proj[0:LS, FV], [proj], [], "vo")
        for hf in range(2):
            dma("pool", cs_out[sidx, :, hf * 1536:(hf + 1) * 1536], qkv_b[0][LS - 3:LS, hf * 1536:(hf + 1) * 1536], [qkv_b[0]], [], "co")
        for hf in range(2):
            dma("pool", tail[:, hf * 1536:(hf + 1) * 1536], cs_in[sidx, :, hf * 1536:(hf + 1) * 1536], tailpb, [tail], "tl")
        dma("sp", S_f[:], ss_in[sidx].rearrange("h d e -> d h e"), [], [S_f], "si")
        act(S_b[:], S_f[:], AF.Copy, [S_f], [S_b])
        gdn_tile(LS, LS, True, 0, [])
        dma("sp", ss_out[sidx].rearrange("h d e -> d h e"), S_f[:], [S_f], [], "so")
        own_tail(LS, NPT + 1,
                 lambda h, t0, t1: kTs_scr[h, :, t0 * 128:t1 * 128], kTs_scr_b,
                 lambda h, t0, t1: vs_scr[h, :, t0:t1, :],
                 vs_scr_b,
                 lambda kt: (smask[:, 0:1] if kt == NPT else None), None,
                 xsrc, ys_out[sidx * LS:(sidx + 1) * LS, :])

    T.emit()
    st.close()
    return nc


def _gconst(C, ntok):
    bf = ml_dtypes.bfloat16
    nch = ntok // C
    shA = np.zeros((ntok, nch * 4 * C), np.float32)
    sel = np.zeros((ntok, nch * C), np.float32)
    plc = np.zeros((C, nch * ntok), np.float32)
    for c in range(nch):
        for j in range(4):
            for i in range(C):
                t = C * c + i + j - 3
                if 0 <= t < ntok:
                    shA[t, (c * 4 + j) * C + i] = 1.0
        for i in range(C):
            sel[C * c + i, c * C + i] = 1.0
            plc[i, c * ntok + C * c + i] = 1.0
    shB = np.zeros((3, 3 * C), np.float32)
    for j in range(3):
        for i in range(C):
            t = i + j
            if t < 3:
                shB[t, j * C + i] = 1.0
    ar = np.arange(C)
    tri = (ar[:, None] <= ar[None, :]).astype(np.float32)
    rep = lambda m: np.ascontiguousarray(m.astype(np.float32))
    return {f"shA{C}": shA.astype(bf), f"shB{C}": shB.astype(bf), f"sel{C}": sel, f"tri{C}": tri,
            f"msl{C}": rep((ar[:, None] > ar[None, :]).astype(np.float32)),
            f"msu{C}": rep((ar[None, :] > ar[:, None]).astype(np.float32)),
            f"mui{C}": rep((ar[None, :] >= ar[:, None]).astype(np.float32)),
            f"id8{C}": rep(np.eye(C, dtype=np.float32)), f"plc{C}": plc.astype(bf)}


def _prep(cfg, inputs):
    NT = cfg.SEQ // 128
    f = lambda k: np.asarray(inputs[k], np.float32)
    rows = np.zeros((1, 2048), np.float32)
    rows[0, 0:64] = f("diff_q_norm_g")[0]
    rows[0, 64:128] = f("diff_k_norm_g")[0]
    rows[0, 128:136] = f("delta_a_log")[0]
    rows[0, 136:144] = f("delta_dt_bias")[0]
    rows[0, 144:208] = f("diff_lambda_q1")[0]
    rows[0, 208:272] = f("diff_lambda_k1")[0]
    rows[0, 272:336] = f("diff_lambda_q2")[0]
    rows[0, 336:400] = f("diff_lambda_k2")[0]
    rows[0, 400:528] = f("delta_norm_g")[0]
    rows[0, 528:656] = f("diff_norm_g")[0]
    rows[0, 1024:2048] = f("norm_ffn_g")[0]
    common = {
        "w_in": np.ascontiguousarray(f("w_in")[0]),
        "w_out": np.ascontiguousarray(f("w_out")[0]),
        "w_pq": np.ascontiguousarray(f("peer_w_q")[0]),
        "subk": np.ascontiguousarray(f("peer_sub_keys")[0].reshape(16 * 128, 128)),
        "peer_u": np.ascontiguousarray(f("peer_u")[0]),
        "peer_v": np.ascontiguousarray(f("peer_v")[0]),
        "g_mix": np.ascontiguousarray(f("norm_mix_g")[0].reshape(cfg.D // 128, 128).T),
        "rows": rows,
        "ident": np.eye(128, dtype=np.float32),
        "conv_w": np.ascontiguousarray(f("conv_w")[0]),
        **_gconst(64, 128), **_gconst(cfg.DEC_SEQ, cfg.DEC_SEQ),
    }
    smask = np.full((128, 1), NEG, np.float32)
    smask[:cfg.DEC_SEQ] = 0.0
    common["smask"] = smask
    xpr = f("x_prompt")
    in_maps = []
    for c in range(8):
        b, r = c // 4, c % 4
        lead = 3 - r
        xpad = np.zeros((NT * 128, cfg.D), np.float32)
        nreal = NT - lead
        xpad[lead * 128:] = xpr[b, :nreal * 128]
        kmask = np.zeros((128, NT), np.float32)
        kmask[:, :lead] = NEG
        m = dict(common)
        m.update({
            "xp": xpad, "kmask": kmask,
            "xs": np.ascontiguousarray(f("x_sample")[2 * c:2 * c + 2].reshape(-1, cfg.D)),
            "ss_in": np.ascontiguousarray(f("state_delta_s")[0][2 * c:2 * c + 2]),
            "cs_in": np.ascontiguousarray(f("state_delta_conv")[0][2 * c:2 * c + 2]),
            "ck_in": np.ascontiguousarray(f("cache_diff_k")[0][2 * c:2 * c + 2].reshape(2, cfg.PAST, 1024)),
            "cv_in": np.ascontiguousarray(f("cache_diff_v")[0][2 * c:2 * c + 2].reshape(2, cfg.PAST, 1024)),
        })
        in_maps.append(m)
    return NT, in_maps


def kernel(**inputs):
    cfg = Cfg
    NT, in_maps = _prep(cfg, inputs)
    nc = build(cfg, NT)
    res = run_bass_kernel_spmd(nc, in_maps, core_ids=list(range(8))).results
    B, S, Dm, H = cfg.BATCH, cfg.SEQ, cfg.D, cfg.H
    DB, DS = cfg.DEC_BATCH, cfg.DEC_SEQ
    f32 = np.float32
    y_p = np.zeros((B, NT, 128, Dm), f32)
    y_s = np.zeros((DB, DS, Dm), f32)
    k_p = np.zeros((1, B, NT, 128, 1024), f32)
    v_p = np.zeros((1, B, NT, 128, 1024), f32)
    s_p = np.zeros((1, B, H, 128, 128), f32)
    c_p = np.zeros((1, B, 3, QKV_W), f32)
    k_s = np.zeros((1, DB, DS, 1024), f32)
    v_s = np.zeros((1, DB, DS, 1024), f32)
    s_s = np.zeros((1, DB, H, 128, 128), f32)
    c_s = np.zeros((1, DB, 3, QKV_W), f32)
    for c in range(8):
        b, r = c // 4, c % 4
        o = res[c]
        y_p[b, r::4] = o["y_own"].reshape(NT // 4, 128, Dm)
        k_p[0, b, r::4] = o["k_own"].reshape(NT // 4, 128, 1024)
        v_p[0, b, r::4] = o["v_own"].reshape(NT // 4, 128, 1024)
        if r == 3:
            c_p[0, b] = o["conv_fin"]
            s_p[0, b] = o["s_fin"]
        y_s[2 * c:2 * c + 2] = o["ys"].reshape(2, DS, Dm)
        k_s[0, 2 * c:2 * c + 2] = o["ks"].reshape(2, DS, 1024)
        v_s[0, 2 * c:2 * c + 2] = o["vs"].reshape(2, DS, 1024)
        s_s[0, 2 * c:2 * c + 2] = o["ss"]
        c_s[0, 2 * c:2 * c + 2] = o["cs"]
    return (y_p.reshape(B, S, Dm), y_s, k_p.reshape(1, B, S, H, 2, 64), v_p.reshape(1, B, S, H, 128), s_p, c_p,
            k_s.reshape(1, DB, DS, H, 2, 64), v_s.reshape(1, DB, DS, H, 128), s_s, c_s)
```
